# Optimizing a Trainium2 kernel written in Bass

```python
import math
import jax, jax.numpy as jnp
from jax import lax
import numpy as np

D_MODEL = 1024
BATCH = 8
SEQ = 2048
DEPTH = 2
DEC_BATCH = 128
DEC_SEQ = 4
PAST_LEN = 16384
PAGE_SIZE = 128

HEAD_DIM = 64
D_FF = 2816
PLE_DIM = 256
NORM_EPS = 1e-6
CHUNK = 64

RWKV_W = D_MODEL // 4
RWKV_HEADS = RWKV_W // HEAD_DIM
RWKV_DECAY_LORA = 32
RWKV_A_LORA = 32
RWKV_GATE_LORA = 64
RWKV_GN_EPS = 64e-5
RWKV_PROJ = 3 * RWKV_W + RWKV_DECAY_LORA + RWKV_A_LORA + RWKV_GATE_LORA

GLA_W = D_MODEL // 4
GLA_HEADS = GLA_W // HEAD_DIM
GLA_DK = HEAD_DIM // 2
GLA_DV = HEAD_DIM
GLA_KEY_W = GLA_HEADS * GLA_DK
GLA_GATE_LORA = 16
GLA_TAU = 16.0
GLA_PROJ = 2 * GLA_KEY_W + GLA_W + GLA_GATE_LORA + GLA_W

M2_W = D_MODEL // 2
M2_HEADS = M2_W // HEAD_DIM
M2_STATE = 64
M2_GROUPS = 2
M2_CONV = 4
M2_CONV_DIM = M2_W + 2 * M2_GROUPS * M2_STATE
M2_PROJ = M2_W + M2_CONV_DIM + M2_HEADS

IN_PROJ = RWKV_PROJ + GLA_PROJ + M2_PROJ
MIX_W = RWKV_W + GLA_W + M2_W

kernel_name = 'hybrid_rwkv7_gla_ssd_decode_step'


def _f32(t):
    return t.astype(jnp.float32)


def _rmsnorm(x, w):
    xf = _f32(x)
    y = xf * lax.rsqrt(jnp.mean(xf * xf, axis=-1, keepdims=True) + NORM_EPS)
    return (y * _f32(w)).astype(x.dtype)


def _swiglu(h, w_gate, w_up, w_down):
    return (jax.nn.silu(h @ w_gate) * (h @ w_up)) @ w_down


def _to_chunks(t, c):
    pad = (-t.shape[1]) % c
    t = jnp.pad(t, [(0, 0), (0, pad)] + [(0, 0)] * (t.ndim - 2))
    n = t.shape[1] // c
    return jnp.moveaxis(t.reshape((t.shape[0], n, c) + t.shape[2:]), 1, 0)


def _from_chunks(t, L):
    t = jnp.moveaxis(t, 0, 1)
    return t.reshape((t.shape[0], -1) + t.shape[3:])[:, :L]


def _rwkv7_scan(r, w, k, v, a, b, S0):
    def step(S, inp):
        r_t, w_t, k_t, v_t, a_t, b_t = inp
        sa = jnp.einsum('bhvk,bhk->bhv', S, a_t)
        S = S * w_t[:, :, None, :] + sa[..., None] * b_t[:, :, None, :] + v_t[..., None] * k_t[:, :, None, :]
        return S, jnp.einsum('bhvk,bhk->bhv', S, r_t)
    xs = tuple(jnp.moveaxis(t, 1, 0) for t in (r, w, k, v, a, b))
    S, ys = lax.scan(step, S0, xs)
    return jnp.moveaxis(ys, 0, 1), S


def _rwkv7_mixer(z, shift_prev, S0, mu, w0, w2, a0, a2, g2, k_k, k_a, r_k, ln_w, ln_b):
    Bsz, L, _ = z.shape
    prev = jnp.concatenate([shift_prev[:, None, :].astype(z.dtype), z[:, :-1]], axis=1)
    zs = _f32(z + mu * (prev - z))
    i0 = 3 * RWKV_W
    r, k, v, wl, al, gl = jnp.split(zs, [RWKV_W, 2 * RWKV_W, i0, i0 + RWKV_DECAY_LORA,
                                        i0 + RWKV_DECAY_LORA + RWKV_A_LORA], axis=-1)
    w_raw = -jax.nn.softplus(-(_f32(w0) + jnp.tanh(wl) @ _f32(w2))) - 0.5
    decay = jnp.exp(-jnp.exp(w_raw))
    a = jax.nn.sigmoid(_f32(a0) + al @ _f32(a2))
    g = jax.nn.sigmoid(gl) @ _f32(g2)
    heads = lambda t: t.reshape(Bsz, L, RWKV_HEADS, HEAD_DIM)
    kk = heads(k * _f32(k_k))
    kk = kk / jnp.maximum(jnp.sqrt(jnp.sum(kk * kk, axis=-1, keepdims=True)), 1e-12)
    k = k * (1.0 + (a - 1.0) * _f32(k_a))
    rh, kh, vh, ah = heads(r), heads(k), heads(v), heads(a)
    y, S = _rwkv7_scan(rh, heads(decay), kh, vh, -kk, kk * ah, _f32(S0))
    mean = jnp.mean(y, axis=-1, keepdims=True)
    var = jnp.mean(jnp.square(y - mean), axis=-1, keepdims=True)
    y = ((y - mean) * lax.rsqrt(var + RWKV_GN_EPS)).reshape(Bsz, L, RWKV_W) * _f32(ln_w) + _f32(ln_b)
    bonus = jnp.sum(rh * kh * _f32(r_k), axis=-1, keepdims=True) * vh
    y = (y + bonus.reshape(Bsz, L, RWKV_W)) * g
    return y.astype(z.dtype), z[:, -1], S.astype(z.dtype)


def _gla_chunked(q, k, v, log_a, S0):
    L = q.shape[1]
    c = min(CHUNK, L)
    tri = jnp.tril(jnp.ones((c, c), dtype=bool))[None, :, :, None, None]

    def step(S, inp):
        qc, kc, vc, gc = inp
        b = jnp.cumsum(gc, axis=1)
        inter = jnp.einsum('bthk,bhkv->bthv', qc * jnp.exp(b), S)
        decay = jnp.exp(jnp.where(tri, b[:, :, None] - b[:, None, :], -jnp.inf))
        att = jnp.einsum('bthk,bshk,btshk->bhts', qc, kc, decay)
        intra = jnp.einsum('bhts,bshv->bthv', att, vc)
        b_last = b[:, -1]
        S = S * jnp.exp(b_last)[..., None] + jnp.einsum('bshk,bshv->bhkv', kc * jnp.exp(b_last[:, None] - b), vc)
        return S, inter + intra

    xs = tuple(_to_chunks(t, c) for t in (q, k, v, log_a))
    S, ys = lax.scan(step, S0, xs)
    return _from_chunks(ys, L), S


def _gla_mixer(z, S0, gate_w2, gate_b, norm_w):
    Bsz, L, _ = z.shape
    zf = _f32(z)
    q, k, v, gl, g = jnp.split(zf, [GLA_KEY_W, 2 * GLA_KEY_W, 2 * GLA_KEY_W + GLA_W,
                                    2 * GLA_KEY_W + GLA_W + GLA_GATE_LORA], axis=-1)
    log_a = jax.nn.log_sigmoid(gl @ _f32(gate_w2) + _f32(gate_b)) / GLA_TAU
    kh = lambda t: t.reshape(Bsz, L, GLA_HEADS, GLA_DK)
    o, S = _gla_chunked(kh(q) * (GLA_DK ** -0.5), kh(k), v.reshape(Bsz, L, GLA_HEADS, GLA_DV), kh(log_a), _f32(S0))
    o = _rmsnorm(o, norm_w).reshape(Bsz, L, GLA_W) * jax.nn.silu(g)
    return o.astype(z.dtype), S.astype(z.dtype)


def _ssd_chunked(x, dt, A, Bm, Cm, S0):
    L = x.shape[1]
    c = min(CHUNK, L)
    tri = jnp.tril(jnp.ones((c, c), dtype=bool))[None, :, :, None]

    def step(S, inp):
        xc, dtc, Bc, Cc = inp
        cum = jnp.cumsum(dtc * A, axis=1)
        seg = jnp.exp(jnp.where(tri, cum[:, :, None] - cum[:, None], -jnp.inf))
        scores = jnp.einsum('bthn,bshn->btsh', Cc, Bc) * seg
        y_intra = jnp.einsum('btsh,bshp->bthp', scores, xc * dtc[..., None])
        y_inter = jnp.einsum('bthn,bhpn->bthp', Cc, S) * jnp.exp(cum)[..., None]
        last = cum[:, -1]
        wgt = jnp.exp(last[:, None] - cum) * dtc
        S = S * jnp.exp(last)[..., None, None] + jnp.einsum('bshn,bshp->bhpn', Bc * wgt[..., None], xc)
        return S, y_intra + y_inter

    xs = tuple(_to_chunks(t, c) for t in (x, dt, Bm, Cm))
    S, ys = lax.scan(step, S0, xs)
    return _from_chunks(ys, L), S


def _mamba2_mixer(z, conv_prev, S0, conv_w, conv_b, dt_bias, A_log, D_skip, norm_w):
    Bsz, L, _ = z.shape
    zg, xBC, dtr = jnp.split(z, [M2_W, M2_W + M2_CONV_DIM], axis=-1)
    buf = jnp.concatenate([conv_prev.astype(z.dtype), xBC], axis=1)
    conv = conv_b + buf[:, 0:L] * conv_w[0]
    for j in range(1, M2_CONV):
        conv = conv + buf[:, j:j + L] * conv_w[j]
    xBC_c = _f32(jax.nn.silu(conv))
    xs, Bm, Cm = jnp.split(xBC_c, [M2_W, M2_W + M2_GROUPS * M2_STATE], axis=-1)
    dt = jax.nn.softplus(_f32(dtr) + _f32(dt_bias))
    A = -jnp.exp(_f32(A_log))
    rep = M2_HEADS // M2_GROUPS
    grp = lambda t: jnp.repeat(t.reshape(Bsz, L, M2_GROUPS, M2_STATE), rep, axis=2)
    xh = xs.reshape(Bsz, L, M2_HEADS, HEAD_DIM)
    y, S = _ssd_chunked(xh, dt, A, grp(Bm), grp(Cm), _f32(S0))
    y = (y + _f32(D_skip)[:, None] * xh).reshape(Bsz, L, M2_W)
    y = _rmsnorm(y * jax.nn.silu(_f32(zg)), norm_w)
    return y.astype(z.dtype), buf[:, -(M2_CONV - 1):], S.astype(z.dtype)


def _layer(x, p, state, W, i):
    shift0, wkv0, gla0, conv0, ssm0 = state
    x = x + 0.5 * _swiglu(_rmsnorm(x, W['ffn1_norm'][i]), W['ffn1_w_gate'][i], W['ffn1_w_up'][i], W['ffn1_w_down'][i])
    h = _rmsnorm(x, W['mix_norm'][i])
    zin = h @ W['w_in'][i]
    z_rwkv, z_gla, z_m2 = jnp.split(zin, [RWKV_PROJ, RWKV_PROJ + GLA_PROJ], axis=-1)
    y_rwkv, shift1, wkv1 = _rwkv7_mixer(z_rwkv, shift0, wkv0, W['rwkv_mu'][i], W['rwkv_w0'][i], W['rwkv_w2'][i],
                                        W['rwkv_a0'][i], W['rwkv_a2'][i], W['rwkv_g2'][i], W['rwkv_k_k'][i],
                                        W['rwkv_k_a'][i], W['rwkv_r_k'][i], W['rwkv_ln_w'][i], W['rwkv_ln_b'][i])
    y_gla, gla1 = _gla_mixer(z_gla, gla0, W['gla_gate_w2'][i], W['gla_gate_b'][i], W['gla_norm'][i])
    y_m2, conv1, ssm1 = _mamba2_mixer(z_m2, conv0, ssm0, W['mamba_conv_w'][i], W['mamba_conv_b'][i],
                                      W['mamba_dt_bias'][i], W['mamba_A_log'][i], W['mamba_D'][i], W['mamba_norm'][i])
    x = x + jnp.concatenate([y_rwkv, y_gla, y_m2], axis=-1) @ W['w_out'][i]
    x = x + 0.5 * _swiglu(_rmsnorm(x, W['ffn2_norm'][i]), W['ffn2_w_gate'][i], W['ffn2_w_up'][i], W['ffn2_w_down'][i])
    gate = jax.nn.sigmoid(_rmsnorm(x, W['ple_norm'][i]) @ W['ple_w_gate'][i])
    x = x + gate * (p @ W['ple_w_proj'][i])
    return x, (shift1, wkv1, gla1, conv1, ssm1)


def setup_inputs(seed: int = 0) -> dict:
    key = jax.random.key(seed)
    ks = iter(jax.random.split(key, 64))
    nrm = lambda shape, scale: scale * jax.random.normal(next(ks), shape, jnp.float32)
    gain = lambda shape: 1.0 + 0.05 * jax.random.normal(next(ks), shape, jnp.float32)
    unif = lambda shape, lo, hi: jax.random.uniform(next(ks), shape, jnp.float32, lo, hi)
    Dd = DEPTH
    dt0 = jnp.exp(unif((Dd, M2_HEADS), math.log(1e-3), math.log(1e-1)))
    return {
        'x_prompt': nrm((BATCH, SEQ, D_MODEL), 1.0),
        'x_sample': nrm((DEC_BATCH, DEC_SEQ, D_MODEL), 1.0),
        'p_prompt': nrm((DEPTH, BATCH, SEQ, PLE_DIM), 1.0),
        'p_sample': nrm((DEPTH, DEC_BATCH, DEC_SEQ, PLE_DIM), 1.0),
        'state_rwkv_shift': nrm((Dd, DEC_BATCH, RWKV_PROJ), 1.0),
        'state_rwkv_wkv': nrm((Dd, DEC_BATCH, RWKV_HEADS, HEAD_DIM, HEAD_DIM), 0.1),
        'state_gla': nrm((Dd, DEC_BATCH, GLA_HEADS, GLA_DK, GLA_DV), 0.1),
        'state_mamba_conv': nrm((Dd, DEC_BATCH, M2_CONV - 1, M2_CONV_DIM), 1.0),
        'state_mamba_ssm': nrm((Dd, DEC_BATCH, M2_HEADS, HEAD_DIM, M2_STATE), 0.1),
        'ffn1_norm': gain((Dd, D_MODEL)),
        'ffn1_w_gate': nrm((Dd, D_MODEL, D_FF), D_MODEL ** -0.5),
        'ffn1_w_up': nrm((Dd, D_MODEL, D_FF), D_MODEL ** -0.5),
        'ffn1_w_down': nrm((Dd, D_FF, D_MODEL), D_FF ** -0.5),
        'mix_norm': gain((Dd, D_MODEL)),
        'w_in': nrm((Dd, D_MODEL, IN_PROJ), D_MODEL ** -0.5),
        'rwkv_mu': unif((Dd, RWKV_PROJ), 0.0, 1.0),
        'rwkv_w0': unif((Dd, RWKV_W), -6.0, -1.0),
        'rwkv_w2': nrm((Dd, RWKV_DECAY_LORA, RWKV_W), 0.1 * RWKV_DECAY_LORA ** -0.5),
        'rwkv_a0': nrm((Dd, RWKV_W), 0.1),
        'rwkv_a2': nrm((Dd, RWKV_A_LORA, RWKV_W), 0.1 * RWKV_A_LORA ** -0.5),
        'rwkv_g2': nrm((Dd, RWKV_GATE_LORA, RWKV_W), RWKV_GATE_LORA ** -0.5),
        'rwkv_k_k': 0.85 + nrm((Dd, RWKV_W), 0.05),
        'rwkv_k_a': gain((Dd, RWKV_W)),
        'rwkv_r_k': nrm((Dd, RWKV_HEADS, HEAD_DIM), 0.1),
        'rwkv_ln_w': gain((Dd, RWKV_W)),
        'rwkv_ln_b': nrm((Dd, RWKV_W), 0.02),
        'gla_gate_w2': nrm((Dd, GLA_GATE_LORA, GLA_KEY_W), GLA_GATE_LORA ** -0.5),
        'gla_gate_b': nrm((Dd, GLA_KEY_W), 0.1),
        'gla_norm': gain((Dd, GLA_DV)),
        'mamba_conv_w': nrm((Dd, M2_CONV, M2_CONV_DIM), M2_CONV ** -0.5),
        'mamba_conv_b': nrm((Dd, M2_CONV_DIM), 0.02),
        'mamba_dt_bias': dt0 + jnp.log(-jnp.expm1(-dt0)),
        'mamba_A_log': jnp.log(unif((Dd, M2_HEADS), 1.0, 16.0)),
        'mamba_D': gain((Dd, M2_HEADS)),
        'mamba_norm': gain((Dd, M2_W)),
        'w_out': nrm((Dd, MIX_W, D_MODEL), MIX_W ** -0.5),
        'ffn2_norm': gain((Dd, D_MODEL)),
        'ffn2_w_gate': nrm((Dd, D_MODEL, D_FF), D_MODEL ** -0.5),
        'ffn2_w_up': nrm((Dd, D_MODEL, D_FF), D_MODEL ** -0.5),
        'ffn2_w_down': nrm((Dd, D_FF, D_MODEL), D_FF ** -0.5),
        'ple_norm': gain((Dd, D_MODEL)),
        'ple_w_gate': nrm((Dd, D_MODEL, D_MODEL), D_MODEL ** -0.5),
        'ple_w_proj': nrm((Dd, PLE_DIM, D_MODEL), PLE_DIM ** -0.5),
        'final_norm': gain((D_MODEL,)),
    }


def reference(x_prompt, x_sample, p_prompt, p_sample, state_rwkv_shift, state_rwkv_wkv, state_gla,
              state_mamba_conv, state_mamba_ssm, ffn1_norm, ffn1_w_gate, ffn1_w_up, ffn1_w_down, mix_norm, w_in,
              rwkv_mu, rwkv_w0, rwkv_w2, rwkv_a0, rwkv_a2, rwkv_g2, rwkv_k_k, rwkv_k_a, rwkv_r_k, rwkv_ln_w, rwkv_ln_b,
              gla_gate_w2, gla_gate_b, gla_norm, mamba_conv_w, mamba_conv_b, mamba_dt_bias, mamba_A_log, mamba_D,
              mamba_norm, w_out, ffn2_norm, ffn2_w_gate, ffn2_w_up, ffn2_w_down, ple_norm, ple_w_gate, ple_w_proj,
              final_norm):
    W = {
        'ffn1_norm': ffn1_norm, 'ffn1_w_gate': ffn1_w_gate, 'ffn1_w_up': ffn1_w_up, 'ffn1_w_down': ffn1_w_down,
        'mix_norm': mix_norm, 'w_in': w_in,
        'rwkv_mu': rwkv_mu, 'rwkv_w0': rwkv_w0, 'rwkv_w2': rwkv_w2, 'rwkv_a0': rwkv_a0, 'rwkv_a2': rwkv_a2,
        'rwkv_g2': rwkv_g2, 'rwkv_k_k': rwkv_k_k, 'rwkv_k_a': rwkv_k_a, 'rwkv_r_k': rwkv_r_k,
        'rwkv_ln_w': rwkv_ln_w, 'rwkv_ln_b': rwkv_ln_b,
        'gla_gate_w2': gla_gate_w2, 'gla_gate_b': gla_gate_b, 'gla_norm': gla_norm,
        'mamba_conv_w': mamba_conv_w, 'mamba_conv_b': mamba_conv_b, 'mamba_dt_bias': mamba_dt_bias,
        'mamba_A_log': mamba_A_log, 'mamba_D': mamba_D, 'mamba_norm': mamba_norm,
        'w_out': w_out,
        'ffn2_norm': ffn2_norm, 'ffn2_w_gate': ffn2_w_gate, 'ffn2_w_up': ffn2_w_up, 'ffn2_w_down': ffn2_w_down,
        'ple_norm': ple_norm, 'ple_w_gate': ple_w_gate, 'ple_w_proj': ple_w_proj,
    }
    nb = x_prompt.shape[0]
    dt_ = x_prompt.dtype
    xp, xs = x_prompt, x_sample
    p_states = ([], [], [], [], [])
    s_states = ([], [], [], [], [])
    for i in range(DEPTH):
        fresh = (jnp.zeros((nb, RWKV_PROJ), dt_),
                 jnp.zeros((nb, RWKV_HEADS, HEAD_DIM, HEAD_DIM), dt_),
                 jnp.zeros((nb, GLA_HEADS, GLA_DK, GLA_DV), dt_),
                 jnp.zeros((nb, M2_CONV - 1, M2_CONV_DIM), dt_),
                 jnp.zeros((nb, M2_HEADS, HEAD_DIM, M2_STATE), dt_))
        xp, st_p = _layer(xp, p_prompt[i], fresh, W, i)
        past = (state_rwkv_shift[i], state_rwkv_wkv[i], state_gla[i], state_mamba_conv[i], state_mamba_ssm[i])
        xs, st_s = _layer(xs, p_sample[i], past, W, i)
        for j in range(5):
            p_states[j].append(st_p[j])
            s_states[j].append(st_s[j])
    y_prompt = _rmsnorm(xp, final_norm)
    y_sample = _rmsnorm(xs, final_norm)
    prompt_rwkv_shift = jnp.stack(p_states[0])
    prompt_rwkv_wkv = jnp.stack(p_states[1])
    prompt_gla = jnp.stack(p_states[2])
    prompt_mamba_conv = jnp.stack(p_states[3])
    prompt_mamba_ssm = jnp.stack(p_states[4])
    sample_rwkv_shift = jnp.stack(s_states[0])
    sample_rwkv_wkv = jnp.stack(s_states[1])
    sample_gla = jnp.stack(s_states[2])
    sample_mamba_conv = jnp.stack(s_states[3])
    sample_mamba_ssm = jnp.stack(s_states[4])
    return (y_prompt, y_sample, prompt_rwkv_shift, prompt_rwkv_wkv, prompt_gla, prompt_mamba_conv, prompt_mamba_ssm,
            sample_rwkv_shift, sample_rwkv_wkv, sample_gla, sample_mamba_conv, sample_mamba_ssm)
```

```python
import numpy as np
from contextlib import ExitStack
import concourse.bass as bass
import concourse.mybir as mybir
from concourse.bass_utils import run_bass_kernel_spmd

F32 = mybir.dt.float32
BF16 = mybir.dt.bfloat16
AF = mybir.ActivationFunctionType
ALU = mybir.AluOpType
AX = mybir.AxisListType

NCORES = 8
D = 1024
DFF = 2816
NJ = DFF // 128
PLE = 256
TP = 2048
TS = 64
T = TP + TS
EPS = 1e-6
TILES = [(i * 512, 512) for i in range(4)] + [(2048, 64)]
FT = [(0, 448), (448, 448), (896, 448), (1344, 448), (1792, 320)]

WNAMES = ['ffn1_norm', 'ffn1_w_gate', 'ffn1_w_up', 'ffn1_w_down', 'mix_norm', 'w_in',
          'rwkv_mu', 'rwkv_w0', 'rwkv_w2', 'rwkv_a0', 'rwkv_a2', 'rwkv_g2', 'rwkv_k_k', 'rwkv_k_a', 'rwkv_r_k',
          'rwkv_ln_w', 'rwkv_ln_b', 'gla_gate_w2', 'gla_gate_b', 'gla_norm', 'mamba_conv_w', 'mamba_conv_b',
          'mamba_dt_bias', 'mamba_A_log', 'mamba_D', 'mamba_norm', 'w_out', 'ffn2_norm', 'ffn2_w_gate',
          'ffn2_w_up', 'ffn2_w_down', 'ple_norm', 'ple_w_gate', 'ple_w_proj', 'final_norm']


class Prog:
    ENG = ("pe", "act", "dve", "pool", "sp")
    NDMA = 12

    def __init__(self, nc, stack):
        self.nc = nc
        self.streams = {e: [] for e in self.ENG}
        self.sems = {}
        self.cnt = {}
        for e in self.ENG:
            self.sems[e] = stack.enter_context(nc.semaphore("s_" + e))
            self.cnt[e] = 0
        self.dq = {}
        for q in ("sp", "pool", "act"):
            lst = []
            for i in range(self.NDMA):
                nm = "d_%s_%d" % (q, i)
                self.sems[nm] = stack.enter_context(nc.semaphore(nm))
                self.cnt[nm] = 0
                lst.append(nm)
            self.dq[q] = [lst, 0]
        self.known = {e: {} for e in self.ENG}
        self.lastw = {}
        self.readers = {}
        self.pending = []
        self.sched = False

    def _need(self, eng, ev, waits):
        if ev is None:
            return
        s, v = ev
        if s == "pe" and eng == "pe":
            return
        if self.known[eng].get(s, 0) >= v:
            return
        if waits.get(s, 0) < v:
            waits[s] = v

    def _deps(self, eng, reads, writes):
        waits = {}
        for k in reads:
            self._need(eng, self.lastw.get(k), waits)
        for k in writes:
            self._need(eng, self.lastw.get(k), waits)
            for ev in self.readers.get(k, ()):
                self._need(eng, ev, waits)
        for s, v in waits.items():
            self.known[eng][s] = v
        return waits

    def _commit(self, ev, reads, writes):
        for k in reads:
            self.readers.setdefault(k, []).append(ev)
        for k in writes:
            self.lastw[k] = ev
            self.readers[k] = []

    def _op_now(self, eng, fn, reads=(), writes=()):
        waits = self._deps(eng, reads, writes)
        self.cnt[eng] += 1
        ev = (eng, self.cnt[eng])
        sem = self.sems[eng]
        wl = [(self.sems[s], v) for s, v in waits.items()]

        def emit(e, fn=fn, wl=wl, sem=sem):
            for s, v in wl:
                e.wait_ge(s, v)
            fn(e).then_inc(sem, 1)
        self.streams[eng].append(emit)
        self._commit(ev, reads, writes)
        return ev

    def _dma_now(self, q, out, in_, reads=(), writes=(), **kw):
        lst, i = self.dq[q]
        nm = lst[i % self.NDMA]
        self.dq[q][1] = i + 1
        waits = self._deps(q, reads, writes)
        prev = self.cnt[nm]
        if prev > 0 and self.known[q].get(nm, 0) < prev:
            waits[nm] = max(waits.get(nm, 0), prev)
            self.known[q][nm] = prev
        self.cnt[nm] += 16
        ev = (nm, self.cnt[nm])
        sem = self.sems[nm]
        wl = [(self.sems[s], v) for s, v in waits.items()]

        def emit(e, wl=wl, sem=sem, out=out, in_=in_, kw=kw):
            for s, v in wl:
                e.wait_ge(s, v)
            e.dma_start(out=out, in_=in_, **kw).then_inc(sem, 16)
        self.streams[q].append(emit)
        self._commit(ev, reads, writes)
        return ev

    import os as _os2
    _sc = 1.0
    DUR = {"pe": 0.45 * _sc, "act": 0.55 * _sc, "dve": 0.45 * _sc, "pool": 0.6, "sp": 2.5}
    LAT = 0.45
    import os as _os
    WINDOW = 700
    SLACK = 0.4

    def op(self, eng, fn, reads=(), writes=(), dur=None):
        if not getattr(self, "sched", False):
            return self._op_now(eng, fn, reads, writes)
        self.pending.append(("op", eng, fn, None, list(reads), list(writes), dur if dur is not None else self.DUR[eng]))
        if len(self.pending) >= self.WINDOW:
            self.flush()

    def dma(self, q, out, in_, reads=(), writes=(), **kw):
        if not getattr(self, "sched", False):
            return self._dma_now(q, out, in_, reads, writes, **kw)
        self.pending.append(("dma", q, (out, in_), kw, list(reads), list(writes), self.DUR["sp"]))
        if len(self.pending) >= self.WINDOW:
            self.flush()

    def flush(self):
        ops = getattr(self, "pending", [])
        self.pending = []
        n = len(ops)
        if n == 0:
            return
        lastw, readers = {}, {}
        preds = [set() for _ in range(n)]
        for i, o in enumerate(ops):
            rd, wr = o[4], o[5]
            for k in rd:
                if k in lastw:
                    preds[i].add(lastw[k])
            for k in wr:
                if k in lastw:
                    preds[i].add(lastw[k])
                for j in readers.get(k, ()):
                    preds[i].add(j)
            for k in rd:
                readers.setdefault(k, []).append(i)
            for k in wr:
                lastw[k] = i
                readers[k] = []
            preds[i].discard(i)
        succs = [[] for _ in range(n)]
        for i in range(n):
            for j in preds[i]:
                succs[j].append(i)
        prio = [0.0] * n
        for i in range(n - 1, -1, -1):
            m = 0.0
            for j in succs[i]:
                if prio[j] > m:
                    m = prio[j]
            prio[i] = ops[i][6] + m
        npred = [len(p) for p in preds]
        ready_t = [0.0] * n
        fin = [0.0] * n
        eng_free = {}
        avail = [i for i in range(n) if npred[i] == 0]
        done = 0
        while done < n:
            best, bkey = None, None
            for i in avail:
                e = ops[i][1]
                st = max(eng_free.get(e, 0.0), ready_t[i])
                key = (st, -prio[i], i)
                if bkey is None or key < bkey:
                    best, bkey = i, key
            lim = bkey[0] + self.SLACK
            for i in avail:
                e = ops[i][1]
                st = max(eng_free.get(e, 0.0), ready_t[i])
                if st <= lim and (prio[i], -i) > (prio[best], -best):
                    best, bkey = i, (st, -prio[i], i)
            i = best
            avail.remove(i)
            o = ops[i]
            e = o[1]
            st = bkey[0]
            fin[i] = st + o[6]
            eng_free[e] = fin[i] - (0.5 * o[6] if e != "sp" else o[6] - 0.1)
            if o[0] == "op":
                self._op_now(e, o[2], o[4], o[5])
            else:
                self._dma_now(e, o[2][0], o[2][1], o[4], o[5], **o[3])
            for j in succs[i]:
                npred[j] -= 1
                lat = self.LAT if ops[j][1] != e else 0.25
                if fin[i] + lat > ready_t[j]:
                    ready_t[j] = fin[i] + lat
                if npred[j] == 0:
                    avail.append(j)
            done += 1

    def wait_all(self, eng):
        wl = []
        for s, c in self.cnt.items():
            if s == "pool" or s.startswith("d_pool_"):
                continue
            if c > 0 and self.known[eng].get(s, 0) < c:
                wl.append((self.sems[s], c))
                self.known[eng][s] = c

        def emit(e, wl=wl):
            for s, v in wl:
                e.wait_ge(s, v)
        self.streams[eng].append(emit)

    def barrier(self, final=False):
        self.flush()
        for e in self.ENG:
            if e == "pool" and not final:
                continue
            self.wait_all(e)
        if final:
            wl = [(self.sems[s], c) for s, c in self.cnt.items() if c > 0]

            def emit(e, wl=wl):
                for s_, v in wl:
                    e.wait_ge(s_, v)
            self.streams["sp"].append(emit)

    def run(self, stack):
        self.flush()
        block = stack.enter_context(self.nc.Block())
        streams = self.streams

        @block.tensor
        def _(e):
            for f in streams["pe"]:
                f(e)

        @block.scalar
        def _(e):
            for f in streams["act"]:
                f(e)

        @block.vector
        def _(e):
            for f in streams["dve"]:
                f(e)

        @block.gpsimd
        def _(e):
            for f in streams["pool"]:
                f(e)

        @block.sync
        def _(e):
            for f in streams["sp"]:
                f(e)


def host_consts():
    c = {}
    c["ident"] = np.eye(128, dtype=np.float32)
    i = np.arange(128)
    c["triu"] = (i[:, None] <= i[None, :]).astype(np.float32)
    c["negmask"] = np.where(i[:, None] <= i[None, :], 0.0, -30000.0).astype(np.float32)
    c["mstrict"] = (i[:, None] < i[None, :]).astype(np.float32)
    c["blk64"] = ((i[:, None] // 64) == (i[None, :] // 64)).astype(np.float32)
    c["hm32"] = ((i[:, None] // 32) == np.arange(4)[None, :]).astype(np.float32)
    c["cm32"] = np.tile(((np.arange(128)[None, :] // 32) == np.arange(4)[:, None]).astype(np.float32).reshape(1, 512), (128, 1))
    return c


class K:
    def __init__(self, nc, st, shapes):
        self.nc = nc
        self.st = st
        self.P = Prog(nc, st)
        P = self.P
        din = lambda n, s: nc.dram_tensor(n, list(s), F32, kind="ExternalInput").ap()
        dout = lambda n, s: nc.dram_tensor(n, list(s), F32, kind="ExternalOutput").ap()
        self.xin = din("xin", [T, D])
        self.pin = din("pin", [2, T, PLE])
        self.w = {n: din(n, shapes[n]) for n in WNAMES}
        self.cst = {n: din("c_" + n, a.shape) for n, a in host_consts().items()}
        self.yout = dout("yout", [T, D])
        dscr = lambda n, s: nc.dram_tensor(n, list(s), BF16, kind="Internal").ap()
        self.s_gu = {}
        self.s_d = {}
        for f in (1, 2):
            gu = dscr("s_gu_%d" % f, [NJ, 128, 2, 8, 128])
            dd = dscr("s_d_%d" % f, [NJ, 128, D])
            for l in range(2):
                self.s_gu[l, f] = gu
                self.s_d[l, f] = dd
        pg = dscr("s_pg", [D, D])
        pp = dscr("s_pp", [PLE, D])
        self.s_pg = [pg, pg]
        self.s_pp = [pp, pp]
        sb = lambda name, shape, dt=F32: st.enter_context(nc.sbuf_tensor(name, list(shape), dt))
        ps = lambda name: st.enter_context(nc.psum_tensor(name, [128, 512], F32))
        self.sb = sb
        self.xT = sb("xT", [128, 8, T])
        self.hT = [sb("hT%d" % i, [128, 8, 512], BF16) for i in range(2)]
        self.ident = sb("ident", [128, 128])
        self.identb = sb("identb", [128, 128], BF16)
        self.onesb = sb("onesb", [128, 128], BF16)
        self.gam = sb("gam", [128, 11, 8])
        self.eps = sb("eps", [128, 1])
        self.sq = [sb("sq%d" % i, [128, 512], BF16) for i in range(2)]
        self.rstd = sb("rstd", [128, 512])
        self.cT = {n: sb("k_" + n, [128, 128]) for n in ("triu", "negmask", "mstrict", "blk64")}
        self.hm32 = sb("hm32", [128, 4])
        self.ones32 = sb("ones32", [128, 128])
        self.AW = 30192
        self.arena = sb("arena", [128, self.AW])
        self.aoff = 0
        A = self.carve
        self.act = A([128, NJ, 512], BF16)
        self.wd = A([128, NJ, D], BF16)
        self.wgu = [A([128, 2, 8, 128], BF16) for i in range(3)]
        self.sg = [A([128, 512], F32) for i in range(2)]
        self.pT = A([128, 2, 512], BF16)
        self.ptm = [A([128, PLE], F32) for i in range(2)]
        self.tin = [A([128, D], F32) for i in range(2)]
        self.ffn_end = self.aoff
        self.bank = [ps("bank%d" % i) for i in range(8)]
        self.sti = {"shift": din("st_shift", [2, 16, 896]), "wkv": din("st_wkv", [2, 16, 4, 64, 64]),
                   "gla": din("st_gla", [2, 16, 4, 32, 64]), "conv": din("st_conv", [2, 16, 3, 768]),
                   "ssm": din("st_ssm", [2, 16, 8, 64, 64])}
        self.o_p = {"shift": dout("o_p_shift", [2, 896]), "wkv": dout("o_p_wkv", [2, 4, 64, 64]),
                    "gla": dout("o_p_gla", [2, 4, 32, 64]), "conv": dout("o_p_conv", [2, 3, 768]),
                    "ssm": dout("o_p_ssm", [2, 8, 64, 64])}
        self.o_s = {"shift": dout("o_s_shift", [2, 16, 896]), "wkv": dout("o_s_wkv", [2, 16, 4, 64, 64]),
                    "gla": dout("o_s_gla", [2, 16, 4, 32, 64]), "conv": dout("o_s_conv", [2, 16, 3, 768]),
                    "ssm": dout("o_s_ssm", [2, 16, 8, 64, 64])}
        self.s_win = dscr("s_win", [D, 2968])
        self.s_wout = dscr("s_wout", [D, D])
        dsc32 = lambda n, s: nc.dram_tensor(n, list(s), F32, kind="Internal").ap()
        self.sc = {n: dsc32("sc_" + n, [128, 256]) for n in ("x", "B", "C", "y")}
        for nme in ("ra", "rw", "rb", "rk", "rr"):
            self.sc[nme] = dsc32("sc_" + nme, [128, 256])
        self.sc["rv"] = dsc32("sc_rv", [128, 128])
        self.sc["ry"] = dsc32("sc_ry", [128, 128])
        for nme, wdt_ in (("gq", 128), ("gk", 128), ("ge", 128), ("gv", 256), ("go", 256)):
            self.sc[nme] = dsc32("sc_" + nme, [64, wdt_])
        self.sc["dt"] = dsc32("sc_dt", [128, 4])
        self.sc["dA"] = dsc32("sc_dA", [128, 4])
        self.aoff = 0
        self.mwbuf = [A([128, 8, 128], BF16) for i in range(3)]
        self.yT = A([128, 8, 512], BF16)
        self.sm = A([128, 64], F32)
        self.stage = A([128, 512], F32)
        self.smp_tm = A([128, 768], F32)
        self.wrhs = A([128, 8, 512], BF16)
        self.wdt = A([128, 8, 8], BF16)
        self.ST = A([128, 4, 64], F32)
        self.STb = A([128, 4, 64], BF16)
        self.hist_ssd = A([128, 6, 3], F32)
        self.cw = A([128, 6, 4], F32)
        self.cb = A([128, 6], F32)
        self.ncb = A([128, 6], F32)
        self.dtb_bc = A([128, 8], F32)
        self.A_bc = A([128, 8], F32)
        self.D_bc = A([128, 8], F32)
        self.mnorm_bc = A([128, 512], F32)
        self.wrhs_g = A([128, 8, 512], BF16)
        self.gw2p = A([128, 128], F32)
        self.gb_bc = A([128, 128], F32)
        self.gnorm_bc = A([128, 256], F32)
        self.Sg = A([128, 64], F32)
        self.Sgb = A([128, 64], BF16)
        self.cm32 = A([128, 4, 128], F32)
        self.rw_mark = self.aoff
        self.rwkv_persist()
        self.scratch_mark = self.aoff
        self.zgs = A([128, 512], F32)
        self.ezg = A([128, 512], F32)
        self.ysb = A([128, 512], F32)
        self.ytmp = A([128, 512], F32)
        self.xtm = A([128, 512], BF16)
        self.xdt = A([128, 512], BF16)
        self.xD = A([128, 512], BF16)
        self.Btm = A([128, 128], F32)
        self.BCb = A([128, 2, 512], BF16)
        self.BCm = A([128, 4, 512], BF16)
        self.xbh = A([128, 4, 64], F32)
        self.Bbh = A([128, 4, 64], F32)
        self.Cbh = A([128, 4, 64], F32)
        self.ybh = A([128, 4, 64], F32)
        self.dtbh = A([128, 4], F32)
        self.dAbh = A([128, 4], F32)
        self.ustart = self.aoff
        self.raw = A([128, 7, 515], F32)
        self.proc = A([128, 7, 512], F32)
        self.E = A([128, 8, 128], F32)
        self.Bwp = A([128, 8, 128], BF16)
        self.scT = A([128, 8, 128], BF16)
        uend = self.aoff
        self.aoff = self.ustart
        self.raw_s = A([128, 7, 112], F32)
        self.proc_s = A([128, 7, 64], F32)
        self.Ssm = A([128, 64, 64], F32)
        self.tmpS = A([128, 64, 64], F32)
        ssd_end = max(self.aoff, uend)
        self.aoff = self.scratch_mark
        self.gla_carve()
        gla_end = self.aoff
        self.aoff = self.scratch_mark
        self.rwkv_carve()
        self.aoff = max(self.aoff, gla_end, ssd_end)
        self.uid = 0

    def u(self):
        self.uid += 1
        return self.uid

    def carve(self, shape, dt):
        n = 1
        for d in shape[1:]:
            n *= d
        words = (n * (4 if dt == F32 else 2) + 3) // 4
        off = self.aoff
        assert off + words <= self.AW, ("arena overflow", off, words, self.AW)
        self.aoff = off + words
        ap = self.arena[0:shape[0], off:off + words]
        if dt != F32:
            ap = ap.bitcast(dt)
        ap = ap[:, 0:n]
        if len(shape) == 3:
            ap = ap.rearrange("p (a b) -> p a b", a=shape[1])
        elif len(shape) == 4:
            ap = ap.rearrange("p (a b c) -> p a b c", a=shape[1], b=shape[2])
        return ap

    def setup(self):
        P = self.P
        P.dma("sp", self.ident[:], self.cst["ident"], writes=["ident"])
        P.op("act", lambda e: e.copy(out=self.identb[:], in_=self.ident[:]), reads=["ident"], writes=["identb"])
        P.op("pool", lambda e: e.memset(self.onesb[:], 1.0), writes=["onesb"])
        P.op("pool", lambda e: e.memset(self.ones32[:], 1.0), writes=["ones32"])
        for n in self.cT:
            P.dma("sp", self.cT[n][:], self.cst[n], writes=["k_" + n])
        P.dma("sp", self.hm32[:], self.cst["hm32"], writes=["hm32"])
        P.op("pool", lambda e: e.memset(self.eps[:], EPS), writes=["eps"])
        names = []
        for l in range(2):
            names += [("ffn1_norm", l), ("mix_norm", l), ("ffn2_norm", l), ("ple_norm", l)]
        names.append(("final_norm", None))
        self.gidx = {}
        for i, (n, l) in enumerate(names):
            src = self.w[n][l] if l is not None else self.w[n]
            self.gidx[n, l] = i
            P.dma("sp", self.gam[:, i, :], src.rearrange("(c p) -> p c", p=128), writes=[("gam", i)],
                  allow_slow_non_contiguous=True)

    def convert_ffn(self, l, f):
        P = self.P
        pre = "ffn%d_" % f
        wg = self.w[pre + "w_gate"][l]
        wu = self.w[pre + "w_up"][l]
        wdn = self.w[pre + "w_down"][l]
        for j in range(NJ):
            for gi, wsrc in enumerate((wg, wu)):
                P.dma("pool", self.s_gu[l, f][j, :, gi, :, :],
                      wsrc[:, j * 128:(j + 1) * 128].rearrange("(c p) m -> p c m", p=128),
                      writes=[("s_gu", f, j, gi)])
        for j0 in range(0, NJ, 2):
            P.dma("pool", self.s_d[l, f][j0:j0 + 2], wdn[j0 * 128:(j0 + 2) * 128, :].rearrange("(j p) n -> j p n", p=128),
                  writes=[("s_d", f, j0)])

    def convert_ple(self, l):
        P = self.P
        for c0 in range(0, 8, 2):
            P.dma("pool", self.s_pg[l][c0 * 128:(c0 + 2) * 128, :].rearrange("(j p) n -> j p n", p=128),
                  self.w["ple_w_gate"][l][c0 * 128:(c0 + 2) * 128, :].rearrange("(j p) n -> j p n", p=128),
                  writes=[("s_pg", c0)])
        P.dma("pool", self.s_pp[l].rearrange("(j p) n -> j p n", p=128),
              self.w["ple_w_proj"][l].rearrange("(j p) n -> j p n", p=128), writes=[("s_pp",)])

    def load_x(self):
        P = self.P
        nt = (T + 127) // 128
        for i in range(nt):
            t0 = i * 128
            n = min(128, T - t0)
            tin = self.tin[i % 2]
            tk = ("tin", i % 2)
            P.dma("sp", tin[0:n, :], self.xin[t0:t0 + n, :], writes=[tk])
            for half in range(2):
                bk = 6 + half
                bank = self.bank[bk]

                def tr(e, half=half, bank=bank, tin=tin, n=n):
                    r = None
                    for c4 in range(4):
                        c = half * 4 + c4
                        r = e.transpose(out=bank[:, c4 * 128:c4 * 128 + n], in_=tin[0:n, c * 128:(c + 1) * 128],
                                        identity=self.ident[0:n, 0:n])
                    return r
                P.op("pe", tr, reads=[tk, "ident"], writes=[("bank", bk)])
                eng = "dve" if half == 0 else "act"
                if eng == "dve":
                    fn = lambda e, half=half, bank=bank, t0=t0, n=n: e.tensor_copy(
                        out=self.xT[:, half * 4:half * 4 + 4, t0:t0 + n],
                        in_=bank[:].rearrange("p (c t) -> p c t", c=4)[:, :, 0:n])
                else:
                    fn = lambda e, half=half, bank=bank, t0=t0, n=n: e.copy(
                        out=self.xT[:, half * 4:half * 4 + 4, t0:t0 + n],
                        in_=bank[:].rearrange("p (c t) -> p c t", c=4)[:, :, 0:n])
                P.op(eng, fn, reads=[("bank", bk)], writes=[("xT", c, i) for c in range(half * 4, half * 4 + 4)])

    def xkeys(self, c, t0, n):
        return [("xT", c, i) for i in range(t0 // 128, (t0 + n + 127) // 128)]

    def norm(self, ti, gi, hbuf, out_f32=None):
        P = self.P
        t0, n = self.tiles[ti]
        hT = self.hT[hbuf]
        SS = 5
        ss = self.bank[SS]
        for c in range(8):
            sq = self.sq[c % 2]
            sk = ("sq", c % 2)
            if c % 4 != 3:
                P.op("act", lambda e, sq=sq, c=c: e.activation(out=sq[:, 0:n], in_=self.xT[:, c, t0:t0 + n], func=AF.Square),
                     reads=self.xkeys(c, t0, n), writes=[sk])
            else:
                P.op("dve", lambda e, sq=sq, c=c: e.tensor_tensor(out=sq[:, 0:n], in0=self.xT[:, c, t0:t0 + n],
                                                                    in1=self.xT[:, c, t0:t0 + n], op=ALU.mult),
                     reads=self.xkeys(c, t0, n), writes=[sk])
            P.op("pe", lambda e, sq=sq, c=c: e.matmul(ss[:, 0:n], lhsT=self.onesb[:], rhs=sq[:, 0:n], start=(c == 0), stop=(c == 7)),
                 reads=[sk, "onesb"], writes=[("bank", SS)])
        P.op("act", lambda e: e.activation(out=self.rstd[:, 0:n], in_=ss[:, 0:n], func=AF.Ln, bias=self.eps[:], scale=1.0 / D),
             reads=[("bank", SS), "eps"], writes=["rstd"])
        P.op("act", lambda e: e.activation(out=self.rstd[:, 0:n], in_=self.rstd[:, 0:n], func=AF.Exp, scale=-0.5),
             reads=["rstd"], writes=["rstd"])
        for c in range(8):
            eng = "dve"
            if out_f32 is None:
                dst = hT[:, c, 0:n]
                wk = [("hT", hbuf, c)]
            else:
                dst = out_f32[:, c, 0:n]
                wk = [("of32", c)]
            P.op(eng, lambda e, c=c, dst=dst: e.scalar_tensor_tensor(
                out=dst, in0=self.xT[:, c, t0:t0 + n], scalar=self.gam[:, gi, c:c + 1], in1=self.rstd[:, 0:n],
                op0=ALU.mult, op1=ALU.mult),
                reads=self.xkeys(c, t0, n) + ["rstd", ("gam", gi)], writes=wk)

    def ffn_phase(self, l, f):
        P = self.P
        self.tiles = FT
        gi = self.gidx["ffn%d_norm" % f, l]
        for j0 in range(0, NJ, 2):
            P.dma("sp", self.wd[:, j0:j0 + 2, :], self.s_d[l, f][j0:j0 + 2].rearrange("j p n -> p j n"),
                  reads=[("s_d", f, j0)], writes=[("wd", j0), ("wd", j0 + 1)])
        self.norm(0, gi, 0)
        for ti in range(len(FT)):
            if ti + 1 < len(FT):
                self.norm(ti + 1, gi, (ti + 1) % 2)
            self.ffn_tile(l, f, ti, ti % 2)

    def ffn_tile(self, l, f, ti, hbuf):
        P = self.P
        t0, n = self.tiles[ti]
        hT = self.hT[hbuf]
        hk = [("hT", hbuf, c) for c in range(8)]
        for j in range(NJ):
            wb = self.wgu[self.wcount % 3]
            wk = ("wgu", self.wcount % 3)
            self.wcount += 1
            P.dma("sp", wb[:], self.s_gu[l, f][j], reads=[("s_gu", f, j, 0), ("s_gu", f, j, 1)], writes=[wk])
            gb = j % 2
            G = self.bank[gb]
            U = self.bank[2 + gb]

            def mm(e, wb=wb, gi_=0, dst=G):
                r = None
                for c in range(8):
                    r = e.matmul(dst[:, 0:n], lhsT=wb[:, gi_, c, :], rhs=hT[:, c, 0:n], start=(c == 0), stop=(c == 7))
                return r
            dmm = 0.3 + 8 * n / 2400.0
            P.op("pe", lambda e, wb=wb, G=G: mm(e, wb, 0, G), reads=[wk] + hk, writes=[("bank", gb)], dur=dmm)
            P.op("pe", lambda e, wb=wb, U=U: mm(e, wb, 1, U), reads=[wk] + hk, writes=[("bank", 2 + gb)], dur=dmm)
            sg = self.sg[gb]
            P.op("act", lambda e, sg=sg, G=G: e.activation(out=sg[:, 0:n], in_=G[:, 0:n], func=AF.Silu),
                 reads=[("bank", gb)], writes=[("sg", gb)])
            P.op("dve", lambda e, sg=sg, U=U, j=j: e.tensor_tensor(out=self.act[:, j, 0:n], in0=sg[:, 0:n], in1=U[:, 0:n], op=ALU.mult),
                 reads=[("sg", gb), ("bank", 2 + gb)], writes=[("act", j)])
        for c in range(8):
            yb = 6 + c % 2
            Y = self.bank[yb]

            def dn(e, c=c, Y=Y):
                r = None
                for j in range(NJ):
                    r = e.matmul(Y[:, 0:n], lhsT=self.wd[:, j, c * 128:(c + 1) * 128], rhs=self.act[:, j, 0:n],
                                 start=(j == 0), stop=(j == NJ - 1))
                return r
            P.op("pe", dn, reads=[("wd", j) for j in range(NJ)] + [("act", j) for j in range(NJ)], writes=[("bank", yb)], dur=0.3 + NJ * n / 2400.0)
            xk = self.xkeys(c, t0, n)
            P.op("dve", lambda e, c=c, Y=Y: e.scalar_tensor_tensor(
                out=self.xT[:, c, t0:t0 + n], in0=Y[:, 0:n], scalar=0.5, in1=self.xT[:, c, t0:t0 + n],
                op0=ALU.mult, op1=ALU.add), reads=[("bank", yb)] + xk, writes=xk)

    def ple_phase(self, l):
        P = self.P
        self.tiles = FT
        gi = self.gidx["ple_norm", l]
        wpg = self.wd[:, 0:8, :]
        wpp = self.wd[:, 8:10, :]
        for c0 in range(0, 8, 2):
            P.dma("sp", wpg[:, c0:c0 + 2, :], self.s_pg[l][c0 * 128:(c0 + 2) * 128, :].rearrange("(c p) n -> p c n", p=128),
                  reads=[("s_pg", c0)], writes=[("wd", c0), ("wd", c0 + 1)])
        P.dma("sp", wpp, self.s_pp[l].rearrange("(c p) n -> p c n", p=128), reads=[("s_pp",)], writes=[("wd", 8), ("wd", 9)])
        self.norm(0, gi, 0)
        for ti in range(len(FT)):
            if ti + 1 < len(FT):
                self.norm(ti + 1, gi, (ti + 1) % 2)
            self.ple_tile(l, ti, wpg, wpp)

    def ple_tile(self, l, ti, wpg, wpp):
        P = self.P
        if True:
            t0, n = self.tiles[ti]
            hbuf = ti % 2
            hT = self.hT[hbuf]
            hk = [("hT", hbuf, c) for c in range(8)]
            for s in range((n + 127) // 128):
                m = min(128, n - s * 128)
                ptm = self.ptm[s % 2]
                P.dma("sp", ptm[0:m, :], self.pin[l, t0 + s * 128:t0 + s * 128 + m, :], writes=[("ptm", s % 2)])
                TB = 4
                tb = self.bank[TB]

                def tr(e, ptm=ptm, m=m, tb=tb):
                    r = None
                    for k in range(2):
                        r = e.transpose(out=tb[:, k * 128:k * 128 + m], in_=ptm[0:m, k * 128:(k + 1) * 128], identity=self.ident[0:m, 0:m])
                    return r
                P.op("pe", tr, reads=[("ptm", s % 2), "ident"], writes=[("bank", TB)])
                P.op("act", lambda e, s=s, m=m, tb=tb: e.copy(out=self.pT[:, :, s * 128:s * 128 + m],
                                                             in_=tb[:, 0:256].rearrange("p (k t) -> p k t", k=2)[:, :, 0:m]),
                     reads=[("bank", TB)], writes=[("pT", s)])
            pk = [("pT", s) for s in range((n + 127) // 128)]
            for dc in range(8):
                gb = dc % 2
                G = self.bank[gb]
                Pp = self.bank[2 + gb]

                def gm(e, dc=dc, G=G):
                    r = None
                    for c in range(8):
                        r = e.matmul(G[:, 0:n], lhsT=wpg[:, c, dc * 128:(dc + 1) * 128], rhs=hT[:, c, 0:n], start=(c == 0), stop=(c == 7))
                    return r
                P.op("pe", gm, reads=hk + [("wd", c) for c in range(8)], writes=[("bank", gb)])

                def pm(e, dc=dc, Pp=Pp):
                    r = None
                    for k in range(2):
                        r = e.matmul(Pp[:, 0:n], lhsT=wpp[:, k, dc * 128:(dc + 1) * 128], rhs=self.pT[:, k, 0:n], start=(k == 0), stop=(k == 1))
                    return r
                P.op("pe", pm, reads=pk + [("wd", 8), ("wd", 9)], writes=[("bank", 2 + gb)])
                sg = self.sg[gb]
                P.op("act", lambda e, sg=sg, G=G: e.activation(out=sg[:, 0:n], in_=G[:, 0:n], func=AF.Exp, scale=-1.0),
                     reads=[("bank", gb)], writes=[("sg", gb)])
                P.op("act", lambda e, sg=sg: e.activation(out=sg[:, 0:n], in_=sg[:, 0:n], func=AF.Ln, bias=1.0),
                     reads=[("sg", gb)], writes=[("sg", gb)])
                P.op("act", lambda e, sg=sg: e.activation(out=sg[:, 0:n], in_=sg[:, 0:n], func=AF.Exp, scale=-1.0),
                     reads=[("sg", gb)], writes=[("sg", gb)])
                P.op("dve", lambda e, sg=sg, Pp=Pp: e.tensor_tensor(out=sg[:, 0:n], in0=sg[:, 0:n], in1=Pp[:, 0:n], op=ALU.mult),
                     reads=[("sg", gb), ("bank", 2 + gb)], writes=[("sg", gb)])
                xk = self.xkeys(dc, t0, n)
                import os
                kple = ""
                if kple == "proj":
                    P.op("dve", lambda e, Pp=Pp, dc=dc: e.tensor_copy(out=self.xT[:, dc, t0:t0 + n], in_=Pp[:, 0:n]),
                         reads=[("bank", 2 + gb), ("sg", gb)] + xk, writes=xk)
                    continue
                P.op("dve", lambda e, sg=sg, dc=dc: e.tensor_tensor(out=self.xT[:, dc, t0:t0 + n], in0=self.xT[:, dc, t0:t0 + n],
                                                                      in1=sg[:, 0:n], op=ALU.add),
                     reads=[("sg", gb)] + xk, writes=xk)

    def final(self):
        P = self.P
        self.tiles = FT
        gi = self.gidx["final_norm", None]
        of = self.act[:].rearrange("p j n -> p (j n)")[:, 0:8192].bitcast(F32).rearrange("p (c n) -> p c n", c=8)
        for ti in range(len(FT)):
            t0, n = FT[ti]
            self.norm(ti, gi, 0, out_f32=of)
            for s in range((n + 127) // 128):
                m = min(128, n - s * 128)
                ob = self.tin[s % 2]
                ok = ("tin", s % 2)
                for half in range(2):
                    bk = 6 + half
                    bank = self.bank[bk]

                    def tr(e, half=half, bank=bank, s=s, m=m):
                        r = None
                        for c4 in range(4):
                            c = half * 4 + c4
                            r = e.transpose(out=bank[0:m, c4 * 128:(c4 + 1) * 128], in_=of[:, c, s * 128:s * 128 + m],
                                            identity=self.ident[:])
                        return r
                    P.op("pe", tr, reads=[("of32", c) for c in range(half * 4, half * 4 + 4)] + ["ident"], writes=[("bank", bk)])
                    if half == 0:
                        P.op("dve", lambda e, ob=ob, bank=bank, m=m: e.tensor_copy(out=ob[0:m, 0:512], in_=bank[0:m, :]),
                             reads=[("bank", bk)], writes=[ok])
                    else:
                        P.op("act", lambda e, ob=ob, bank=bank, m=m: e.copy(out=ob[0:m, 512:1024], in_=bank[0:m, :]),
                             reads=[("bank", bk)], writes=[ok])
                P.dma("sp", self.yout[t0 + s * 128:t0 + s * 128 + m, :], ob[0:m, :], reads=[ok], writes=[("yout", ti, s)])

    def rwkv_persist(self):
        A = self.carve
        r = {}
        r["mu7"] = A([128, 7], F32)
        for nme in ("w0n", "a0n", "kk_", "ka", "oma", "rk", "hm64"):
            r[nme] = A([128, 2], F32)
        r["lsc"] = A([128, 1], F32)
        r["w2p"] = A([128, 256], F32)
        r["a2p"] = A([128, 256], F32)
        r["g2p"] = A([128, 256], F32)
        r["lnw_bc"] = A([128, 256], F32)
        r["lnb_bc"] = A([128, 256], F32)
        r["Sf"] = A([128, 2, 128], F32)
        r["Sb"] = A([128, 2, 128], BF16)
        r["hist"] = A([128, 7, 1], F32)
        r["tiny"] = A([128, 1], F32)
        r["gneps"] = A([128, 1], F32)
        r["lstrict"] = A([128, 128], F32)
        self.r = r

    def rwkv_carve(self):
        A = self.carve
        r = self.r
        u0 = self.aoff
        r["raw"] = A([128, 7, 513], F32)
        r["zs"] = A([128, 7, 512], F32)
        u1 = self.aoff
        self.aoff = u0
        r["raw_s"] = A([128, 7, 80], F32)
        r["zs_s"] = A([128, 7, 64], F32)
        r["Ss"] = A([128, 32, 64], F32)
        r["tS"] = A([128, 32, 64], F32)
        for nme in ("va", "vw", "vb", "vk", "vr"):
            r[nme] = A([128, 4, 64], F32)
        r["vv"] = A([128, 4, 32], F32)
        r["vy"] = A([128, 4, 32], F32)
        r["sa"] = A([128, 32], F32)
        assert self.aoff <= u1, (self.aoff, u1)
        self.aoff = u1
        r["lact"] = A([128, 512], F32)
        for nme in ("sgw", "al", "kk", "kp", "bq", "cs", "Ep", "Em", "Epv", "At", "Rt", "Bh", "Kh", "rkr", "T1"):
            r[nme] = A([128, 2, 128], F32)
        for nme in ("Atb", "Rtb", "Bhb"):
            r[nme] = A([128, 2, 128], BF16)
        r["Atb_1"] = A([128, 2, 128], BF16)
        r["Rtb_1"] = A([128, 2, 128], BF16)
        r["Ep_1"] = A([128, 2, 128], F32)
        for nme in ("Atm", "Bhm", "Khm"):
            r[nme] = A([128, 4, 128], BF16)
        for nme in ("Vtb", "Bcb", "Kcb", "RHSb", "Ub"):
            r[nme] = A([128, 256], BF16)
        for nme in ("vtf", "Ysb", "ysq", "ytmp", "gsb"):
            r[nme] = A([128, 256], F32)
        for nme in ("A", "M", "MakT", "NrbT", "NrkT", "Z"):
            r[nme] = A([128, 4, 128], BF16)
        r["stg"] = A([128, 4, 128], F32)
        r["bon"] = A([128, 4], F32)

    def gla_carve(self):
        A = self.carve
        g = {}
        g["qT"] = A([128, 512], F32)
        g["kT"] = A([128, 512], F32)
        g["glT"] = A([128, 512], F32)
        g["la"] = A([128, 128], F32)
        g["eb"] = A([128, 128], F32)
        g["enb"] = A([128, 128], F32)
        g["ktf"] = A([128, 128], F32)
        g["qtb"] = A([128, 128], BF16)
        g["km"] = A([128, 4, 128], BF16)
        g["qm"] = A([128, 4, 128], BF16)
        g["ktm"] = A([128, 128], BF16)
        g["ktmm"] = A([128, 4, 128], BF16)
        g["attf"] = A([128, 4, 128], F32)
        g["attm"] = A([128, 4, 128], BF16)
        g["vtm"] = A([128, 256], BF16)
        g["vtf"] = A([128, 256], F32)
        g["gs"] = A([128, 256], F32)
        g["gtmp"] = A([128, 256], F32)
        g["osb"] = A([128, 256], F32)
        g["osq"] = A([128, 256], F32)
        g["t1"] = A([128, 64], F32)
        g["qtm"] = A([128, 128], F32)
        g["ktm32"] = A([128, 128], F32)
        g["ela"] = A([128, 128], F32)
        g["Ss"] = A([128, 32, 64], F32)
        g["tS"] = A([128, 32, 64], F32)
        g["qbh"] = A([128, 4, 32], F32)
        g["kbh"] = A([128, 4, 32], F32)
        g["ebh"] = A([128, 4, 32], F32)
        g["vbh"] = A([128, 4, 64], F32)
        g["obh"] = A([128, 4, 64], F32)
        self.g = g

    DBK = ("Atb", "Rtb", "Ep")

    def _mk(self, k):
        par = getattr(self, "kpar", None)
        if par is None:
            return k
        if isinstance(k, str) and k in self.DBK:
            return k + "#%d" % par
        if isinstance(k, tuple) and isinstance(k[0], str) and k[0] in self.DBK:
            return (k[0] + "#%d" % par,) + tuple(k[1:])
        return k

    def OP(self, eng, fn, r=(), w=()):
        return self.P.op(eng, fn, reads=[self._mk(k) for k in r], writes=[self._mk(k) for k in w])

    def convert_mix(self, l):
        P = self.P
        for c in range(8):
            P.dma("pool", self.s_win[c * 128:(c + 1) * 128, :], self.w["w_in"][l][c * 128:(c + 1) * 128, :], writes=[("s_win", c)])
        for c0 in range(0, 8, 2):
            P.dma("pool", self.s_wout[c0 * 128:(c0 + 2) * 128, :].rearrange("(j p) n -> j p n", p=128),
                  self.w["w_out"][l][c0 * 128:(c0 + 2) * 128, :].rearrange("(j p) n -> j p n", p=128), writes=[("s_wout", c0)])

    def bcast_load(self, dst, src1d, key):
        self.P.dma("sp", dst, src1d.partition_broadcast(128), writes=[key])

    def load_wchunk(self, col0, width):
        slot = self.wcount % 3
        self.wcount += 1
        wb = self.mwbuf[slot]
        wk = ("mwbuf", slot)
        self.P.dma("sp", wb[:, :, 0:width], self.s_win.rearrange("(c p) n -> p c n", p=128)[:, :, col0:col0 + width],
                   reads=[("s_win", c) for c in range(8)], writes=[wk])
        return wb, wk

    def inproj_fm(self, ti, col0, width, dst, dkeys, eng="act"):
        t0, n = TILES[ti]
        hb = ti % 2
        hT = self.hT[hb]
        hk = [("hT", hb, c) for c in range(8)]
        wb, wk = self.load_wchunk(col0, width)
        bi = self.bcount % 2
        self.bcount += 1
        bank = self.bank[bi]

        def mm(e):
            r = None
            for c in range(8):
                r = e.matmul(bank[0:width, 0:n], lhsT=wb[:, c, 0:width], rhs=hT[:, c, 0:n], start=(c == 0), stop=(c == 7))
            return r
        self.OP("pe", mm, [wk] + hk, [("bank", bi)])
        if eng == "act":
            self.OP("act", lambda e: e.copy(out=dst, in_=bank[0:width, 0:n]), [("bank", bi)], dkeys)
        else:
            self.OP("dve", lambda e: e.tensor_copy(out=dst, in_=bank[0:width, 0:n]), [("bank", bi)], dkeys)

    def load_wrhs(self, segs):
        for (col0, w, off) in segs:
            self.P.dma("sp", self.wrhs[:, :, off:off + w], self.s_win.rearrange("(c p) n -> p c n", p=128)[:, :, col0:col0 + w],
                       reads=[("s_win", c) for c in range(8)], writes=[("wrhs", off)])

    def mixer_phase(self, l):
        P = self.P
        mix = self.mix
        self.tiles = TILES
        P.barrier()
        import os
        P.sched = True
        self.OP("dve", lambda e: e.memset(self.yT[:], 0.0), [], [("yT", c) for c in range(8)])
        if "ssd" in mix:
            self.ssd_setup(l)
        if "gla" in mix:
            self.gla_setup(l)
        if "rwkv" in mix:
            self.rwkv_setup(l)
        gi = self.gidx["mix_norm", l]
        self.norm(0, gi, 0)
        for ti in range(len(TILES)):
            if ti + 1 < len(TILES):
                self.norm(ti + 1, gi, (ti + 1) % 2)
            if ti == len(TILES) - 1:
                P.barrier()
            import os
            kssd = ""
            if "ssd" in mix and kssd != "setup":
                if ti < 4:
                    self.ssd_tile(l, ti)
                elif kssd == "":
                    self.ssd_sample(l)
                P.barrier()
            if "gla" in mix:
                if ti < 4:
                    self.gla_tile(l, ti)
                else:
                    self.gla_sample(l)
                P.barrier()
            if "rwkv" in mix:
                if ti < 4:
                    self.rwkv_tile(l, ti)
                else:
                    self.rwkv_sample(l)
                P.barrier()
            self.outproj(ti)
        P.barrier()
        P.sched = self.gsched

    def outproj(self, ti):
        t0, n = TILES[ti]
        yk = [("yT", c) for c in range(8)]
        for dc in range(8):
            slot = self.wcount % 3
            self.wcount += 1
            wb = self.mwbuf[slot]
            wk = ("mwbuf", slot)
            self.P.dma("sp", wb[:], self.s_wout.rearrange("(c p) n -> p c n", p=128)[:, :, dc * 128:(dc + 1) * 128],
                       reads=[("s_wout", c0) for c0 in range(0, 8, 2)], writes=[wk])
            bi = 6 + dc % 2
            bank = self.bank[bi]

            def mm(e, wb=wb, bank=bank):
                r = None
                for c in range(8):
                    r = e.matmul(bank[:, 0:n], lhsT=wb[:, c, :], rhs=self.yT[:, c, 0:n], start=(c == 0), stop=(c == 7))
                return r
            self.OP("pe", mm, yk + [wk], [("bank", bi)])
            xk = self.xkeys(dc, t0, n)
            self.OP("dve", lambda e, dc=dc, bank=bank: e.tensor_tensor(out=self.xT[:, dc, t0:t0 + n], in0=self.xT[:, dc, t0:t0 + n],
                                                                     in1=bank[:, 0:n], op=ALU.add), [("bank", bi)] + xk, xk)

    M2 = 1680
    def ssd_setup(self, l):
        P = self.P
        w = self.w
        for j in range(4):
            P.dma("sp", self.cw[:, :, j], w["mamba_conv_w"][l, j].rearrange("(q p) -> p q", p=128), writes=["cw"], allow_slow_non_contiguous=True)
        P.dma("sp", self.cb[:], w["mamba_conv_b"][l].rearrange("(q p) -> p q", p=128), writes=["cb"], allow_slow_non_contiguous=True)
        self.OP("dve", lambda e: e.tensor_scalar_mul(out=self.ncb[:], in0=self.cb[:], scalar1=-1.0), ["cb"], ["ncb"])
        self.bcast_load(self.dtb_bc[:], w["mamba_dt_bias"][l], "dtb_bc")
        self.bcast_load(self.A_bc[:], w["mamba_A_log"][l], "A_bc")
        self.OP("act", lambda e: e.activation(out=self.A_bc[:], in_=self.A_bc[:], func=AF.Exp), ["A_bc"], ["A_bc"])
        self.OP("dve", lambda e: e.tensor_scalar_mul(out=self.A_bc[:], in0=self.A_bc[:], scalar1=-1.0), ["A_bc"], ["A_bc"])
        self.bcast_load(self.D_bc[:], w["mamba_D"][l], "D_bc")
        self.bcast_load(self.mnorm_bc[:], w["mamba_norm"][l], "mnorm_bc")
        self.OP("dve", lambda e: e.memset(self.hist_ssd[:], 0.0), [], ["hist_ssd"])
        self.OP("dve", lambda e: e.memset(self.ST[:], 0.0), [], ["ST"])
        self.OP("dve", lambda e: e.memset(self.STb[:], 0.0), [], ["STb"])
        self.OP("dve", lambda e: e.memset(self.Bwp[:], 0.0), [], ["Bwp"])
        self.load_wrhs([(self.M2, 512, 0)])
        P.dma("sp", self.wdt[:], self.s_win.rearrange("(c p) n -> p c n", p=128)[:, :, self.M2 + 1280:self.M2 + 1288],
              reads=[("s_win", c) for c in range(8)], writes=["wdt"])

    def silu_into(self, dst, src, tmp, rk, wk, tk, np_=128):
        self.OP("act", lambda e: e.activation(out=tmp, in_=src, func=AF.Exp, scale=-1.0), rk, [tk])
        self.OP("act", lambda e: e.activation(out=tmp, in_=tmp, func=AF.Ln, bias=1.0), [tk], [tk])
        self.OP("act", lambda e: e.activation(out=tmp, in_=tmp, func=AF.Exp, scale=-1.0), [tk], [tk])
        self.OP("dve", lambda e: e.tensor_tensor(out=dst, in0=src, in1=tmp, op=ALU.mult), list(rk) + [tk], [wk])

    def ssd_conv(self, raw, xc, n, step):
        for q in range(6):
            rk = [("raw", q)]
            ok = ("xc", q)
            dst = xc[:, q, 0:n]
            self.OP("dve", lambda e, q=q, dst=dst: e.tensor_scalar_mul(out=dst, in0=raw[:, q, 0:n], scalar1=self.cw[:, q, 0:1]), rk + ["cw"], [ok])
            for j in range(1, 4):
                self.OP("dve", lambda e, q=q, j=j, dst=dst: e.scalar_tensor_tensor(
                    out=dst, in0=raw[:, q, j * step:j * step + n], scalar=self.cw[:, q, j:j + 1], in1=dst, op0=ALU.mult, op1=ALU.add),
                    rk + ["cw", ok], [ok])
            tmp = self.ezg[:, 0:n]
            self.OP("act", lambda e, q=q, dst=dst, tmp=tmp: e.activation(out=tmp, in_=dst, func=AF.Exp, scale=-1.0, bias=self.ncb[:, q:q + 1]),
                    [ok, "ncb"], ["ezg"])
            self.OP("act", lambda e, tmp=tmp: e.activation(out=tmp, in_=tmp, func=AF.Ln, bias=1.0), ["ezg"], ["ezg"])
            self.OP("act", lambda e, tmp=tmp: e.activation(out=tmp, in_=tmp, func=AF.Exp, scale=-1.0), ["ezg"], ["ezg"])
            self.OP("dve", lambda e, q=q, dst=dst, tmp=tmp: e.scalar_tensor_tensor(
                out=dst, in0=dst, scalar=self.cb[:, q:q + 1], in1=tmp, op0=ALU.add, op1=ALU.mult), [ok, "ezg", "cb"], [ok])

    def ssd_tile(self, l, ti):
        P = self.P
        t0, n = TILES[ti]
        raw, xc = self.raw, self.proc
        M2 = self.M2
        self.OP("dve", lambda e: e.memset(self.Bwp[:], 0.0), [], ["Bwp"])
        self.OP("dve", lambda e: e.tensor_copy(out=raw[:, 0:6, 0:3], in_=self.hist_ssd[:]), ["hist_ssd"], [("raw", q) for q in range(6)])
        for q in range(6):
            self.inproj_fm(ti, M2 + 512 + q * 128, 128, raw[:, q, 3:3 + n], [("raw", q)], eng="act")
        self.OP("dve", lambda e: e.tensor_copy(out=self.hist_ssd[:], in_=raw[:, 0:6, n:n + 3]), [("raw", q) for q in range(6)], ["hist_ssd"])
        if ti == 3:
            for j in range(3):
                P.dma("sp", self.o_p["conv"][l, j].rearrange("(q p) -> p q", p=128), raw[:, 0:6, n + j], reads=[("raw", q) for q in range(6)],
                      writes=[("o_p_conv", l, j)], allow_slow_non_contiguous=True)
        import os
        kssd = ""
        if kssd == "inproj":
            return
        self.ssd_conv(raw, xc, n, 1)
        self.OP("act", lambda e: e.copy(out=self.BCb[:, :, 0:n], in_=xc[:, 4:6, 0:n]), [("xc", 4), ("xc", 5)], ["BCb"])
        for bc in range(2):
            for g in range(2):
                self.OP("act", lambda e, bc=bc, g=g: e.activation(out=self.BCm[:, bc * 2 + g, 0:n], in_=xc[:, 4 + bc, 0:n], func=AF.Copy,
                                                                  scale=self.cT["blk64"][:, g * 64:g * 64 + 1]),
                        [("xc", 4 + bc), "k_blk64"], ["BCm"])
        if kssd == "conv":
            return
        for k in range(n // 128):
            self.ssd_chunk(l, ti, k)
        if ti == 3 and kssd != "nostate":
            self.ssd_state_out(l)

    def ssd_tokmajor(self, hT, hk, c0, m):
        Z = self.bank[2]
        DT = self.bank[3]
        sm = self.sm

        def mmz(e):
            r = None
            for c in range(8):
                r = e.matmul(Z[0:m, :], lhsT=hT[:, c, c0:c0 + m], rhs=self.wrhs[:, c, :], start=(c == 0), stop=(c == 7))
            return r
        self.OP("pe", mmz, hk + [("wrhs", 0)], [("bank", 2)])

        def mmd(e):
            r = None
            for c in range(8):
                r = e.matmul(DT[0:m, 0:8], lhsT=hT[:, c, c0:c0 + m], rhs=self.wdt[:, c, :], start=(c == 0), stop=(c == 7))
            return r
        self.OP("pe", mmd, hk + ["wdt"], [("bank", 3)])
        self.silu_into(self.zgs[0:m, :], Z[0:m, :], self.ezg[0:m, :], [("bank", 2)], "zgs", "ezg")
        self.OP("dve", lambda e: e.tensor_tensor(out=sm[0:m, 0:8], in0=DT[0:m, 0:8], in1=self.dtb_bc[0:m, :], op=ALU.add), [("bank", 3), "dtb_bc"], ["sm"])
        self.OP("act", lambda e: e.activation(out=sm[0:m, 0:8], in_=sm[0:m, 0:8], func=AF.Exp), ["sm"], ["sm"])
        self.OP("act", lambda e: e.activation(out=sm[0:m, 0:8], in_=sm[0:m, 0:8], func=AF.Ln, bias=1.0), ["sm"], ["sm"])
        self.OP("dve", lambda e: e.tensor_tensor(out=sm[0:m, 8:16], in0=sm[0:m, 0:8], in1=self.A_bc[0:m, :], op=ALU.mult), ["sm", "A_bc"], ["sm"])

    def ssd_post(self, m, tokcols):
        sm = self.sm
        self.OP("dve", lambda e: e.tensor_tensor(out=self.ysb[0:m, :], in0=self.ysb[0:m, :], in1=self.zgs[0:m, :], op=ALU.mult), ["ysb", "zgs"], ["ysb"])
        self.OP("dve", lambda e: e.memset(sm[0:m, 56:57], 0.0), [], ["sm"])
        self.OP("act", lambda e: e.activation(out=self.ytmp[0:m, :], in_=self.ysb[0:m, :], func=AF.Square, accum_out=sm[0:m, 56:57]), ["ysb", "sm"], ["ytmp", "sm"])
        self.OP("act", lambda e: e.activation(out=sm[0:m, 57:58], in_=sm[0:m, 56:57], func=AF.Ln, bias=self.eps[0:m, :], scale=1.0 / 512), ["sm", "eps"], ["sm"])
        self.OP("act", lambda e: e.activation(out=sm[0:m, 57:58], in_=sm[0:m, 57:58], func=AF.Exp, scale=-0.5), ["sm"], ["sm"])
        self.OP("dve", lambda e: e.scalar_tensor_tensor(out=self.ytmp[0:m, :], in0=self.ysb[0:m, :], scalar=sm[0:m, 57:58], in1=self.mnorm_bc[0:m, :],
                                                        op0=ALU.mult, op1=ALU.mult), ["ysb", "sm", "mnorm_bc"], ["ytmp"])
        TB = self.bank[5]

        def tr(e):
            r = None
            for q in range(4):
                r = e.transpose(out=TB[:, q * 128:q * 128 + m], in_=self.ytmp[0:m, q * 128:(q + 1) * 128], identity=self.ident[0:m, 0:m])
            return r
        self.OP("pe", tr, ["ytmp", "ident"], [("bank", 5)])
        self.OP("act", lambda e: e.copy(out=self.yT[:, 4:8, tokcols], in_=TB[:].rearrange("p (q t) -> p q t", q=4)[:, :, 0:m]),
                [("bank", 5)], [("yT", c) for c in range(4, 8)])

    def ssd_chunk(self, l, ti, k):
        hb = ti % 2
        hT = self.hT[hb]
        hk = [("hT", hb, c) for c in range(8)]
        c0 = k * 128
        tok = slice(c0, c0 + 128)
        sm = self.sm
        xc = self.proc
        self.ssd_tokmajor(hT, hk, c0, 128)
        if self.kch == 1:
            return
        CB = self.bank[3]
        self.OP("pe", lambda e: e.matmul(CB[:, 16:24], lhsT=self.cT["triu"][:], rhs=sm[:, 8:16], start=True, stop=True), ["sm", "k_triu"], [("bank", 3)])
        self.OP("pe", lambda e: e.matmul(CB[:, 24:32], lhsT=self.ones32[:], rhs=sm[:, 8:16], start=True, stop=True), ["sm", "ones32"], [("bank", 3)])
        self.OP("dve", lambda e: e.tensor_copy(out=sm[:, 16:24], in_=CB[:, 16:24]), [("bank", 3)], ["sm"])
        self.OP("dve", lambda e: e.tensor_scalar_mul(out=sm[:, 24:32], in0=CB[:, 16:24], scalar1=-1.0), [("bank", 3)], ["sm"])
        self.OP("act", lambda e: e.activation(out=sm[:, 32:40], in_=CB[:, 16:24], func=AF.Exp), [("bank", 3)], ["sm"])
        self.OP("dve", lambda e: e.tensor_tensor(out=sm[:, 40:48], in0=CB[:, 24:32], in1=sm[:, 16:24], op=ALU.subtract), [("bank", 3), "sm"], ["sm"])
        self.OP("act", lambda e: e.activation(out=sm[:, 40:48], in_=sm[:, 40:48], func=AF.Exp), ["sm"], ["sm"])
        self.OP("dve", lambda e: e.tensor_tensor(out=sm[:, 40:48], in0=sm[:, 40:48], in1=sm[:, 0:8], op=ALU.mult), ["sm"], ["sm"])
        self.OP("act", lambda e: e.activation(out=sm[:, 48:56], in_=CB[:, 24:32], func=AF.Exp), [("bank", 3)], ["sm"])
        if self.kch == 2:
            return
        XB = self.bank[4]

        def trx(e):
            r = None
            for q in range(4):
                r = e.transpose(out=XB[:, q * 128:(q + 1) * 128], in_=xc[:, q, tok], identity=self.ident[:])
            return r
        self.OP("pe", trx, [("xc", q) for q in range(4)] + ["ident"], [("bank", 4)])
        self.OP("act", lambda e: e.copy(out=self.xtm[:], in_=XB[:]), [("bank", 4)], ["xtm"])
        if self.kch == 21:
            return
        xb3 = self.xtm[:].rearrange("p (h d) -> p h d", h=8)
        self.OP("dve", lambda e: e.tensor_tensor(out=self.xdt[:].rearrange("p (h d) -> p h d", h=8), in0=xb3,
                                                 in1=sm[:, 0:8].unsqueeze(2).to_broadcast([128, 8, 64]), op=ALU.mult), ["xtm", "sm"], ["xdt"])
        self.OP("dve", lambda e: e.tensor_tensor(out=self.xD[:].rearrange("p (h d) -> p h d", h=8), in0=xb3,
                                                 in1=self.D_bc[:].unsqueeze(2).to_broadcast([128, 8, 64]), op=ALU.mult), ["xtm", "D_bc"], ["xD"])
        if self.kch == 22:
            return
        BT = self.bank[5]
        self.OP("pe", lambda e: e.transpose(out=BT[:, 0:128], in_=xc[:, 4, tok], identity=self.ident[:]), [("xc", 4), "ident"], [("bank", 5)])
        self.OP("act", lambda e: e.copy(out=self.Btm[:], in_=BT[:, 0:128]), [("bank", 5)], ["Btm"])
        if self.kch == 3:
            return
        CBT = self.bank[5]
        for g in range(2):
            self.OP("pe", lambda e, g=g: e.matmul(CBT[:, 128 + g * 128:256 + g * 128], lhsT=self.BCm[:, g, tok],
                                                   rhs=self.BCb[:, 1, tok], start=True, stop=True), ["BCb", "BCm"], [("bank", 5)])
        if self.kch == 4:
            return
        for h in range(8):
            bi = 6 + h // 4
            bank = self.bank[bi]
            col = (h % 4) * 128

            def mmc(e, h=h, bank=bank, col=col):
                e.matmul(bank[:, col:col + 128], lhsT=sm[:, 8 + h:9 + h].to_broadcast([128, 128]), rhs=self.cT["triu"][:], start=True, stop=False)
                return e.matmul(bank[:, col:col + 128], lhsT=self.ident[:], rhs=self.cT["negmask"][:], start=False, stop=True)
            self.OP("pe", mmc, ["sm", "k_triu", "k_negmask", "ident"], [("bank", bi)])
            self.OP("act", lambda e, h=h, bank=bank, col=col: e.activation(out=self.E[:, h, :], in_=bank[:, col:col + 128], func=AF.Exp,
                                                                        bias=sm[:, 24 + h:25 + h]), [("bank", bi), "sm"], [("E", h)])
        self.OP("act", lambda e: e.copy(out=self.stage[:, 0:256], in_=CBT[:, 128:384]), [("bank", 5)], ["stage"])
        for g in range(2):
            self.OP("dve", lambda e, g=g: e.tensor_tensor(out=self.scT[:, g * 4:(g + 1) * 4, :], in0=self.E[:, g * 4:(g + 1) * 4, :],
                                                          in1=self.stage[:, g * 128:(g + 1) * 128].unsqueeze(1).to_broadcast([128, 4, 128]), op=ALU.mult),
                    [("E", h) for h in range(g * 4, g * 4 + 4)] + ["stage"], [("scT", g)])
        if self.kch == 5:
            return
        Y = self.bank[2]
        YI = self.bank[4]

        def mmy(e):
            e.matmul(Y[:, :], lhsT=self.identb[:], rhs=self.xD[:], start=True, stop=False)
            r = None
            for h in range(8):
                r = e.matmul(Y[:, h * 64:(h + 1) * 64], lhsT=self.scT[:, h, :], rhs=self.xdt[:, h * 64:(h + 1) * 64], start=False, stop=(h == 7))
            return r
        self.OP("pe", mmy, ["identb", "xD", "xdt", ("scT", 0), ("scT", 1)], [("bank", 2)])

        def mmi(e):
            r = None
            for h in range(8):
                g, i = h // 4, h % 4
                r = e.matmul(YI[:, h * 64:(h + 1) * 64], lhsT=self.BCm[:, 2 + g, tok], rhs=self.STb[:, i, :], start=True, stop=True)
            return r
        self.OP("pe", mmi, ["BCm", "STb"], [("bank", 4)])
        self.OP("act", lambda e: e.copy(out=self.ytmp[:], in_=YI[:]), [("bank", 4)], ["ytmp"])
        self.OP("dve", lambda e: e.tensor_tensor(out=self.ytmp[:].rearrange("p (h d) -> p h d", h=8), in0=self.ytmp[:].rearrange("p (h d) -> p h d", h=8),
                                                 in1=sm[:, 32:40].unsqueeze(2).to_broadcast([128, 8, 64]), op=ALU.mult), ["ytmp", "sm"], ["ytmp"])
        self.OP("dve", lambda e: e.tensor_tensor(out=self.ysb[:], in0=self.ytmp[:], in1=Y[:], op=ALU.add), ["ytmp", ("bank", 2)], ["ysb"])
        if self.kch == 6:
            return
        self.ssd_post(128, tok)
        if self.kch == 7:
            return
        for g in range(2):
            self.OP("dve", lambda e, g=g: e.tensor_tensor(out=self.Bwp[:, g * 4:(g + 1) * 4, g * 64:(g + 1) * 64],
                                                          in0=self.Btm[:, g * 64:(g + 1) * 64].unsqueeze(1).to_broadcast([128, 4, 64]),
                                                          in1=sm[:, 40 + g * 4:44 + g * 4].unsqueeze(2).to_broadcast([128, 4, 64]), op=ALU.mult),
                    ["Btm", "sm"], ["Bwp"])
        SN = self.bank[3]

        def mms(e):
            r = None
            for i in range(4):
                e.matmul(SN[:, 256 + i * 64:256 + (i + 1) * 64], lhsT=self.Bwp[:, i, :], rhs=self.xtm[:, i * 64:(i + 1) * 64], start=True, stop=False)
                r = e.matmul(SN[:, 256 + i * 64:256 + (i + 1) * 64], lhsT=self.Bwp[:, 4 + i, :], rhs=self.xtm[:, (4 + i) * 64:(5 + i) * 64], start=False, stop=True)
            return r
        self.OP("pe", mms, ["Bwp", "xtm"], [("bank", 3)])
        self.OP("dve", lambda e: e.tensor_copy(out=sm[0:64, 58:62], in_=sm[0:64, 48:52]), ["sm"], ["sm"])
        self.OP("dve", lambda e: e.tensor_copy(out=sm[64:128, 58:62], in_=sm[64:128, 52:56]), ["sm"], ["sm"])
        self.OP("dve", lambda e: e.tensor_tensor(out=self.ST[:], in0=self.ST[:], in1=sm[:, 58:62].unsqueeze(2).to_broadcast([128, 4, 64]), op=ALU.mult),
                ["ST", "sm"], ["ST"])
        self.OP("dve", lambda e: e.tensor_tensor(out=self.ST[:], in0=self.ST[:], in1=SN[:, 256:512].rearrange("p (i d) -> p i d", i=4), op=ALU.add),
                ["ST", ("bank", 3)], ["ST"])
        self.OP("act", lambda e: e.copy(out=self.STb[:], in_=self.ST[:]), ["ST"], ["STb"])

    def ssd_state_out(self, l):
        TB = self.bank[5]

        def tr(e):
            r = None
            for i in range(4):
                r = e.transpose(out=TB[0:64, i * 128:(i + 1) * 128], in_=self.ST[:, i, :], identity=self.ident[:])
            return r
        self.OP("pe", tr, ["ST", "ident"], [("bank", 5)])
        self.OP("act", lambda e: e.copy(out=self.stage[0:64, :], in_=TB[0:64, :]), [("bank", 5)], ["stage"])
        for i in range(4):
            self.P.dma("sp", self.o_p["ssm"][l].rearrange("(g i) p n -> i p g n", g=2)[i],
                       self.stage[0:64, i * 128:(i + 1) * 128].rearrange("p (g n) -> p g n", g=2), reads=["stage"], writes=[("o_p_ssm", l, i)])

    def ssd_sample(self, l):
        P = self.P
        ti = 4
        t0, n = TILES[ti]
        raw, xc = self.raw_s, self.proc_s
        M2 = self.M2
        hb = ti % 2
        hT = self.hT[hb]
        hk = [("hT", hb, c) for c in range(8)]
        sm = self.sm
        for j in range(3):
            P.dma("sp", self.smp_tm[j * 16:(j + 1) * 16, :], self.sti["conv"][l, :, j, :], writes=["smp_tm"])
        for q in range(6):
            TB = self.bank[5]
            self.OP("pe", lambda e, q=q: e.transpose(out=TB[:, 0:48], in_=self.smp_tm[0:48, q * 128:(q + 1) * 128], identity=self.ident[0:48, 0:48]),
                    ["smp_tm", "ident"], [("bank", 5)])
            self.OP("act", lambda e, q=q: e.copy(out=raw[:, q, 0:48], in_=TB[:, 0:48]), [("bank", 5)], [("raw", q)])
        for q in range(6):
            self.inproj_fm(ti, M2 + 512 + q * 128, 128, raw[:, q, 48:48 + n], [("raw", q)], eng="act")
        for q in range(6):
            TB = self.bank[5]
            self.OP("pe", lambda e, q=q: e.transpose(out=TB[0:48, 0:128], in_=raw[:, q, 64:112], identity=self.ident[:]), [("raw", q), "ident"], [("bank", 5)])
            self.OP("act", lambda e, q=q: e.copy(out=self.ytmp[0:48, 0:128], in_=TB[0:48, 0:128]), [("bank", 5)], ["ytmp"])
            for j in range(3):
                P.dma("sp", self.o_s["conv"][l, :, j, q * 128:(q + 1) * 128], self.ytmp[j * 16:(j + 1) * 16, 0:128], reads=["ytmp"],
                      writes=[("o_s_conv", l, j, q)])
        self.ssd_conv(raw, xc, n, 16)
        for q in range(6):
            TB = self.bank[5]
            self.OP("pe", lambda e, q=q: e.transpose(out=TB[0:64, 0:128], in_=xc[:, q, 0:64], identity=self.ident[:]), [("xc", q), "ident"], [("bank", 5)])
            self.OP("act", lambda e, q=q: e.copy(out=self.smp_tm[0:64, q * 128:(q + 1) * 128], in_=TB[0:64, 0:128]), [("bank", 5)], ["smp_tm"])
        self.ssd_tokmajor(hT, hk, 0, 64)
        self.OP("act", lambda e: e.activation(out=sm[0:64, 16:24], in_=sm[0:64, 8:16], func=AF.Exp), ["sm"], ["sm"])
        sc = self.sc
        for t in range(4):
            rows = slice(t * 16, (t + 1) * 16)
            P.dma("sp", sc["x"].rearrange("(b h) (t p) -> b h t p", h=8, t=4)[:, :, t, :],
                  self.smp_tm[rows, 0:512].rearrange("b (h p) -> b h p", h=8), reads=["smp_tm"], writes=["sc_x"])
            for i in range(4):
                P.dma("sp", sc["B"].rearrange("(b g i) (t p) -> b g i t p", g=2, i=4, t=4)[:, :, i, t, :],
                      self.smp_tm[rows, 512:640].rearrange("b (g p) -> b g p", g=2), reads=["smp_tm"], writes=["sc_B"])
                P.dma("sp", sc["C"].rearrange("(b g i) (t p) -> b g i t p", g=2, i=4, t=4)[:, :, i, t, :],
                      self.smp_tm[rows, 640:768].rearrange("b (g p) -> b g p", g=2), reads=["smp_tm"], writes=["sc_C"])
            P.dma("sp", sc["dt"].rearrange("(b h) t -> b h t", h=8)[:, :, t], sm[rows, 0:8], reads=["sm"], writes=["sc_dt"],
                  allow_slow_non_contiguous=True)
            P.dma("sp", sc["dA"].rearrange("(b h) t -> b h t", h=8)[:, :, t], sm[rows, 16:24], reads=["sm"], writes=["sc_dA"],
                  allow_slow_non_contiguous=True)
        P.dma("sp", self.xbh[:].rearrange("p t d -> p (t d)"), sc["x"], reads=["sc_x"], writes=["xbh"])
        P.dma("sp", self.Bbh[:].rearrange("p t d -> p (t d)"), sc["B"], reads=["sc_B"], writes=["Bbh"])
        P.dma("sp", self.Cbh[:].rearrange("p t d -> p (t d)"), sc["C"], reads=["sc_C"], writes=["Cbh"])
        P.dma("sp", self.dtbh[:], sc["dt"], reads=["sc_dt"], writes=["dtbh"])
        P.dma("sp", self.dAbh[:], sc["dA"], reads=["sc_dA"], writes=["dAbh"])
        P.dma("sp", self.Ssm[:].rearrange("p a b -> p (a b)"), self.sti["ssm"][l].rearrange("b h p n -> (b h) (p n)"), writes=["Ssm"])
        self.OP("dve", lambda e: e.tensor_tensor(out=self.xbh[:], in0=self.xbh[:], in1=self.dtbh[:].unsqueeze(2).to_broadcast([128, 4, 64]), op=ALU.mult),
                ["xbh", "dtbh"], ["xbh"])
        for t in range(4):
            self.OP("dve", lambda e, t=t: e.tensor_tensor(out=self.tmpS[:], in0=self.xbh[:, t, :].unsqueeze(2).to_broadcast([128, 64, 64]),
                                                          in1=self.Bbh[:, t, :].unsqueeze(1).to_broadcast([128, 64, 64]), op=ALU.mult),
                    ["xbh", "Bbh"], ["tmpS"])
            self.OP("dve", lambda e, t=t: e.scalar_tensor_tensor(out=self.Ssm[:], in0=self.Ssm[:], scalar=self.dAbh[:, t:t + 1], in1=self.tmpS[:],
                                                                 op0=ALU.mult, op1=ALU.add), ["Ssm", "tmpS", "dAbh"], ["Ssm"])
            self.OP("dve", lambda e, t=t: e.tensor_tensor(out=self.tmpS[:], in0=self.Ssm[:], in1=self.Cbh[:, t, :].unsqueeze(1).to_broadcast([128, 64, 64]),
                                                          op=ALU.mult), ["Ssm", "Cbh"], ["tmpS"])
            self.OP("dve", lambda e, t=t: e.tensor_reduce(out=self.ybh[:, t, :], in_=self.tmpS[:], axis=AX.X, op=ALU.add), ["tmpS"], ["ybh"])
        P.dma("sp", self.o_s["ssm"][l].rearrange("b h p n -> (b h) (p n)"), self.Ssm[:].rearrange("p a b -> p (a b)"), reads=["Ssm"], writes=[("o_s_ssm", l)])
        P.dma("sp", sc["y"], self.ybh[:].rearrange("p t d -> p (t d)"), reads=["ybh"], writes=["sc_y"])
        for t in range(4):
            rows = slice(t * 16, (t + 1) * 16)
            P.dma("sp", self.ysb[rows, :].rearrange("b (h p) -> b h p", h=8), sc["y"].rearrange("(b h) (t p) -> b h t p", h=8, t=4)[:, :, t, :],
                  reads=["sc_y"], writes=["ysb"])
        self.OP("dve", lambda e: e.tensor_tensor(out=self.ytmp[0:64, :].rearrange("p (h d) -> p h d", h=8),
                                                 in0=self.smp_tm[0:64, 0:512].rearrange("p (h d) -> p h d", h=8),
                                                 in1=self.D_bc[0:64, :].unsqueeze(2).to_broadcast([64, 8, 64]), op=ALU.mult), ["smp_tm", "D_bc"], ["ytmp"])
        self.OP("dve", lambda e: e.tensor_tensor(out=self.ysb[0:64, :], in0=self.ysb[0:64, :], in1=self.ytmp[0:64, :], op=ALU.add), ["ysb", "ytmp"], ["ysb"])
        self.ssd_post(64, slice(0, 64))

    GB = 896
    def gla_setup(self, l):
        P = self.P
        w = self.w
        win3 = self.s_win.rearrange("(c p) n -> p c n", p=128)
        rk = [("s_win", c) for c in range(8)]
        P.dma("sp", self.wrhs_g[:, :, 0:256], win3[:, :, self.GB + 256:self.GB + 512], reads=rk, writes=["wrhs_g"])
        P.dma("sp", self.wrhs_g[:, :, 256:512], win3[:, :, self.GB + 528:self.GB + 784], reads=rk, writes=["wrhs_g"])
        self.OP("dve", lambda e: e.memset(self.gw2p[:], 0.0), [], ["gw2p"])
        P.dma("sp", self.gw2p[0:16, :], w["gla_gate_w2"][l], reads=["gw2p"], writes=["gw2p"])
        self.bcast_load(self.gb_bc[:], w["gla_gate_b"][l], "gb_bc")
        for i in range(4):
            self.bcast_load(self.gnorm_bc[:, i * 64:(i + 1) * 64], w["gla_norm"][l], "gnorm_bc")
        P.dma("sp", self.cm32[:].rearrange("p h j -> p (h j)"), self.cst["cm32"], writes=["cm32"])
        self.OP("dve", lambda e: e.memset(self.Sg[:], 0.0), [], ["Sg"])
        self.OP("dve", lambda e: e.memset(self.Sgb[:], 0.0), [], ["Sgb"])

    def gla_inproj(self, ti):
        t0, n = TILES[ti]
        g = self.g
        self.OP("dve", lambda e: e.memset(g["glT"][:, 0:n], 0.0), [], ["glT"])
        self.inproj_fm(ti, self.GB, 128, g["qT"][:, 0:n], ["qT"], eng="act")
        self.inproj_fm(ti, self.GB + 128, 128, g["kT"][:, 0:n], ["kT"], eng="dve")
        self.inproj_fm(ti, self.GB + 512, 16, g["glT"][0:16, 0:n], ["glT"], eng="act")

    def gla_tok(self, ti, c0, m):
        g = self.g
        hb = ti % 2
        hT = self.hT[hb]
        hk = [("hT", hb, c) for c in range(8)]
        VG = self.bank[2]

        def mmv(e):
            r = None
            for c in range(8):
                r = e.matmul(VG[0:m, :], lhsT=hT[:, c, c0:c0 + m], rhs=self.wrhs_g[:, c, :], start=(c == 0), stop=(c == 7))
            return r
        self.OP("pe", mmv, hk + ["wrhs_g"], [("bank", 2)])
        self.OP("act", lambda e: e.copy(out=g["vtm"][0:m, :], in_=VG[0:m, 0:256]), [("bank", 2)], ["vtm"])
        self.OP("act", lambda e: e.copy(out=g["vtf"][0:m, :], in_=VG[0:m, 0:256]), [("bank", 2)], ["vtf"])
        self.silu_into(g["gs"][0:m, :], VG[0:m, 256:512], g["gtmp"][0:m, :], [("bank", 2)], "gs", "gtmp")
        GP = self.bank[3]
        self.OP("pe", lambda e: e.matmul(GP[0:m, 0:128], lhsT=g["glT"][:, c0:c0 + m], rhs=self.gw2p[:], start=True, stop=True), ["glT", "gw2p"], [("bank", 3)])
        la = g["la"]
        self.OP("dve", lambda e: e.tensor_tensor(out=la[0:m, :], in0=GP[0:m, 0:128], in1=self.gb_bc[0:m, :], op=ALU.add), [("bank", 3), "gb_bc"], ["la"])
        self.OP("act", lambda e: e.activation(out=la[0:m, :], in_=la[0:m, :], func=AF.Exp, scale=-1.0), ["la"], ["la"])
        self.OP("act", lambda e: e.activation(out=la[0:m, :], in_=la[0:m, :], func=AF.Ln, bias=1.0), ["la"], ["la"])
        self.OP("dve", lambda e: e.tensor_scalar_mul(out=la[0:m, :], in0=la[0:m, :], scalar1=-1.0 / 16.0), ["la"], ["la"])

    def gla_post(self, m, tokcols, src_psum=None):
        g = self.g
        sm = self.sm
        osb, osq = g["osb"], g["osq"]
        self.OP("act", lambda e: e.activation(out=osq[0:m, :], in_=osb[0:m, :], func=AF.Square), ["osb"], ["osq"])
        self.OP("dve", lambda e: e.tensor_reduce(out=sm[0:m, 0:4], in_=osq[0:m, :].rearrange("p (h d) -> p h d", h=4), axis=AX.X, op=ALU.add), ["osq"], ["sm"])
        self.OP("act", lambda e: e.activation(out=sm[0:m, 0:4], in_=sm[0:m, 0:4], func=AF.Ln, bias=self.eps[0:m, :], scale=1.0 / 64), ["sm", "eps"], ["sm"])
        self.OP("act", lambda e: e.activation(out=sm[0:m, 0:4], in_=sm[0:m, 0:4], func=AF.Exp, scale=-0.5), ["sm"], ["sm"])
        self.OP("dve", lambda e: e.tensor_tensor(out=osb[0:m, :].rearrange("p (h d) -> p h d", h=4), in0=osb[0:m, :].rearrange("p (h d) -> p h d", h=4),
                                                 in1=sm[0:m, 0:4].unsqueeze(2).to_broadcast([m, 4, 64]), op=ALU.mult), ["osb", "sm"], ["osb"])
        self.OP("dve", lambda e: e.tensor_tensor(out=osb[0:m, :], in0=osb[0:m, :], in1=self.gnorm_bc[0:m, :], op=ALU.mult), ["osb", "gnorm_bc"], ["osb"])
        self.OP("dve", lambda e: e.tensor_tensor(out=osb[0:m, :], in0=osb[0:m, :], in1=g["gs"][0:m, :], op=ALU.mult), ["osb", "gs"], ["osb"])
        TB = self.bank[5]

        def tr(e):
            r = None
            for q in range(2):
                r = e.transpose(out=TB[:, q * 128:q * 128 + m], in_=osb[0:m, q * 128:(q + 1) * 128], identity=self.ident[0:m, 0:m])
            return r
        self.OP("pe", tr, ["osb", "ident"], [("bank", 5)])
        self.OP("act", lambda e: e.copy(out=self.yT[:, 2:4, tokcols], in_=TB[:, 0:256].rearrange("p (q t) -> p q t", q=2)[:, :, 0:m]),
                [("bank", 5)], [("yT", 2), ("yT", 3)])

    def gla_tile(self, l, ti):
        t0, n = TILES[ti]
        self.gla_inproj(ti)
        for k in range(n // 128):
            self.gla_chunk(l, ti, k)
        if ti == 3:
            self.P.dma("sp", self.o_p["gla"][l].rearrange("h k v -> (h k) v"), self.Sg[:], reads=["Sg"], writes=[("o_p_gla", l)])

    def gla_chunk(self, l, ti, k):
        g = self.g
        c0 = k * 128
        tok = slice(c0, c0 + 128)
        self.gla_tok(ti, c0, 128)
        la = g["la"]
        BP = self.bank[3]
        self.OP("pe", lambda e: e.matmul(BP[:, 128:256], lhsT=la[:], rhs=self.cT["triu"][:], start=True, stop=True), ["la", "k_triu"], [("bank", 3)])
        self.OP("act", lambda e: e.activation(out=g["eb"][:], in_=BP[:, 128:256], func=AF.Exp), [("bank", 3)], ["eb"])
        self.OP("act", lambda e: e.activation(out=g["enb"][:], in_=BP[:, 128:256], func=AF.Exp, scale=-1.0), [("bank", 3)], ["enb"])
        self.OP("dve", lambda e: e.scalar_tensor_tensor(out=g["qtb"][:], in0=g["qT"][:, tok], scalar=32.0 ** -0.5, in1=g["eb"][:], op0=ALU.mult, op1=ALU.mult),
                ["qT", "eb"], ["qtb"])
        self.OP("dve", lambda e: e.tensor_tensor(out=g["ktf"][:], in0=g["kT"][:, tok], in1=g["enb"][:], op=ALU.mult), ["kT", "enb"], ["ktf"])
        self.OP("dve", lambda e: e.tensor_tensor(out=g["km"][:], in0=g["ktf"][:].unsqueeze(1).to_broadcast([128, 4, 128]),
                                                 in1=self.hm32[:].unsqueeze(2).to_broadcast([128, 4, 128]), op=ALU.mult), ["ktf", "hm32"], ["km"])
        self.OP("dve", lambda e: e.tensor_tensor(out=g["qm"][:], in0=g["qtb"][:].unsqueeze(1).to_broadcast([128, 4, 128]),
                                                 in1=self.hm32[:].unsqueeze(2).to_broadcast([128, 4, 128]), op=ALU.mult), ["qtb", "hm32"], ["qm"])
        KT = self.bank[5]
        self.OP("pe", lambda e: e.transpose(out=KT[:, 0:128], in_=g["ktf"][:], identity=self.ident[:]), ["ktf", "ident"], [("bank", 5)])
        self.OP("act", lambda e: e.copy(out=g["ktm32"][:], in_=KT[:, 0:128]), [("bank", 5)], ["ktm32"])
        self.OP("dve", lambda e: e.tensor_tensor(out=g["ktmm"][:], in0=g["ktm32"][:].unsqueeze(1).to_broadcast([128, 4, 128]),
                                                 in1=self.cm32[:], op=ALU.mult), ["ktm32", "cm32"], ["ktmm"])
        ATT = self.bank[4]

        def mma(e):
            r = None
            for h in range(4):
                r = e.matmul(ATT[:, h * 128:(h + 1) * 128], lhsT=g["km"][:, h, :], rhs=g["qtb"][:], start=True, stop=True)
            return r
        self.OP("pe", mma, ["km", "qtb"], [("bank", 4)])
        self.OP("act", lambda e: e.copy(out=g["attf"][:].rearrange("p h t -> p (h t)"), in_=ATT[:]), [("bank", 4)], ["attf"])
        self.OP("dve", lambda e: e.tensor_tensor(out=g["attm"][:], in0=g["attf"][:], in1=self.cT["triu"][:].unsqueeze(1).to_broadcast([128, 4, 128]), op=ALU.mult),
                ["attf", "k_triu"], ["attm"])
        O = self.bank[2]

        def mmo(e):
            r = None
            for h in range(4):
                e.matmul(O[:, h * 64:(h + 1) * 64], lhsT=g["attm"][:, h, :], rhs=g["vtm"][:, h * 64:(h + 1) * 64], start=True, stop=False)
                r = e.matmul(O[:, h * 64:(h + 1) * 64], lhsT=g["qm"][:, h, :], rhs=self.Sgb[:], start=False, stop=True)
            return r
        self.OP("pe", mmo, ["attm", "vtm", "qm", "Sgb"], [("bank", 2)])
        self.OP("act", lambda e: e.copy(out=g["osb"][:], in_=O[:, 0:256]), [("bank", 2)], ["osb"])
        self.gla_post(128, tok)
        SN = self.bank[3]

        def mms(e):
            r = None
            for h in range(4):
                r = e.matmul(SN[:, 256:320], lhsT=g["ktmm"][:, h, :], rhs=g["vtm"][:, h * 64:(h + 1) * 64], start=(h == 0), stop=(h == 3))
            return r
        self.OP("pe", mms, ["ktmm", "vtm"], [("bank", 3)])
        self.OP("act", lambda e: e.activation(out=g["t1"][:], in_=SN[:, 256:320], func=AF.Copy, scale=g["eb"][:, 127:128]), [("bank", 3), "eb"], ["t1"])
        self.OP("dve", lambda e: e.scalar_tensor_tensor(out=self.Sg[:], in0=self.Sg[:], scalar=g["eb"][:, 127:128], in1=g["t1"][:], op0=ALU.mult, op1=ALU.add),
                ["Sg", "eb", "t1"], ["Sg"])
        self.OP("act", lambda e: e.copy(out=self.Sgb[:], in_=self.Sg[:]), ["Sg"], ["Sgb"])

    def gla_sample(self, l):
        P = self.P
        g = self.g
        ti = 4
        t0, n = TILES[ti]
        self.gla_inproj(ti)
        self.gla_tok(ti, 0, 64)
        la = g["la"]
        self.OP("act", lambda e: e.activation(out=g["ela"][0:64, :], in_=la[0:64, :], func=AF.Exp), ["la"], ["ela"])
        TB = self.bank[5]
        self.OP("pe", lambda e: e.transpose(out=TB[0:64, 0:128], in_=g["qT"][:, 0:64], identity=self.ident[:]), ["qT", "ident"], [("bank", 5)])
        self.OP("act", lambda e: e.activation(out=g["qtm"][0:64, :], in_=TB[0:64, 0:128], func=AF.Copy, scale=32.0 ** -0.5), [("bank", 5)], ["qtm"])
        self.OP("pe", lambda e: e.transpose(out=TB[0:64, 128:256], in_=g["kT"][:, 0:64], identity=self.ident[:]), ["kT", "ident"], [("bank", 5)])
        self.OP("act", lambda e: e.copy(out=g["ktm32"][0:64, :], in_=TB[0:64, 128:256]), [("bank", 5)], ["ktm32"])
        sc = self.sc
        for t in range(4):
            rows = slice(t * 16, (t + 1) * 16)
            for nme, src, wd_, key in (("gq", g["qtm"], 32, "qtm"), ("gk", g["ktm32"], 32, "ktm32"), ("ge", g["ela"], 32, "ela"), ("gv", g["vtf"], 64, "vtf")):
                P.dma("sp", sc[nme].rearrange("(b h) (t w) -> b h t w", h=4, t=4)[:, :, t, :],
                      src[rows, 0:4 * wd_].rearrange("b (h w) -> b h w", h=4), reads=[key], writes=["sc_" + nme])
        for nme, dst, key in (("gq", g["qbh"], "qbh"), ("gk", g["kbh"], "kbh"), ("ge", g["ebh"], "ebh"), ("gv", g["vbh"], "vbh")):
            P.dma("sp", dst[0:64].rearrange("p t w -> p (t w)"), sc[nme], reads=["sc_" + nme], writes=[key])
        Ss, tS = g["Ss"], g["tS"]
        P.dma("sp", Ss[0:64].rearrange("p k v -> p (k v)"), self.sti["gla"][l].rearrange("b h k v -> (b h) (k v)"), writes=["Ss"])
        for t in range(4):
            self.OP("dve", lambda e, t=t: e.tensor_tensor(out=tS[0:64], in0=g["kbh"][0:64, t, :].unsqueeze(2).to_broadcast([64, 32, 64]),
                                                          in1=g["vbh"][0:64, t, :].unsqueeze(1).to_broadcast([64, 32, 64]), op=ALU.mult), ["kbh", "vbh"], ["tS"])
            self.OP("dve", lambda e, t=t: e.tensor_tensor(out=Ss[0:64], in0=Ss[0:64], in1=g["ebh"][0:64, t, :].unsqueeze(2).to_broadcast([64, 32, 64]), op=ALU.mult),
                    ["Ss", "ebh"], ["Ss"])
            self.OP("dve", lambda e, t=t: e.tensor_tensor(out=Ss[0:64], in0=Ss[0:64], in1=tS[0:64], op=ALU.add), ["Ss", "tS"], ["Ss"])
            self.OP("dve", lambda e, t=t: e.tensor_tensor(out=tS[0:64], in0=Ss[0:64], in1=g["qbh"][0:64, t, :].unsqueeze(2).to_broadcast([64, 32, 64]), op=ALU.mult),
                    ["Ss", "qbh"], ["tS"])
            self.OP("dve", lambda e, t=t: e.tensor_reduce(out=g["obh"][0:64, t, :], in_=tS[0:64].rearrange("p k v -> p v k"), axis=AX.X, op=ALU.add), ["tS"], ["obh"])
        P.dma("sp", self.o_s["gla"][l].rearrange("b h k v -> (b h) (k v)"), Ss[0:64].rearrange("p k v -> p (k v)"), reads=["Ss"], writes=[("o_s_gla", l)])
        P.dma("sp", sc["go"], g["obh"][0:64].rearrange("p t w -> p (t w)"), reads=["obh"], writes=["sc_go"])
        for t in range(4):
            rows = slice(t * 16, (t + 1) * 16)
            P.dma("sp", g["osb"][rows, :].rearrange("b (h w) -> b h w", h=4), sc["go"].rearrange("(b h) (t w) -> b h t w", h=4, t=4)[:, :, t, :],
                  reads=["sc_go"], writes=["osb"])
        self.gla_post(64, slice(0, 64))

    C0 = 0.6065306597126334
    def rwkv_setup(self, l):
        P = self.P
        r = self.r
        w = self.w

        def pp(dst, src1d, key):
            P.dma("sp", dst, src1d.rearrange("(j p) -> p j", p=128), writes=[key], allow_slow_non_contiguous=True)
        P.dma("sp", r["mu7"][:], w["rwkv_mu"][l].rearrange("(q p) -> p q", p=128), writes=["mu7"], allow_slow_non_contiguous=True)
        pp(r["w0n"][:], w["rwkv_w0"][l], "w0n")
        self.OP("dve", lambda e: e.tensor_scalar_mul(out=r["w0n"][:], in0=r["w0n"][:], scalar1=-1.0), ["w0n"], ["w0n"])
        pp(r["a0n"][:], w["rwkv_a0"][l], "a0n")
        self.OP("dve", lambda e: e.tensor_scalar_mul(out=r["a0n"][:], in0=r["a0n"][:], scalar1=-1.0), ["a0n"], ["a0n"])
        pp(r["kk_"][:], w["rwkv_k_k"][l], "kk_")
        pp(r["ka"][:], w["rwkv_k_a"][l], "ka")
        self.OP("dve", lambda e: e.tensor_scalar(out=r["oma"][:], in0=r["ka"][:], scalar1=-1.0, scalar2=1.0, op0=ALU.mult, op1=ALU.add), ["ka"], ["oma"])
        pp(r["rk"][:], w["rwkv_r_k"][l].rearrange("h k -> (h k)"), "rk")
        self.OP("dve", lambda e: e.tensor_copy(out=r["hm64"][:, 0:1], in_=self.cT["blk64"][:, 0:1]), ["k_blk64"], ["hm64"])
        self.OP("dve", lambda e: e.tensor_copy(out=r["hm64"][:, 1:2], in_=self.cT["blk64"][:, 64:65]), ["k_blk64", "hm64"], ["hm64"])
        self.OP("dve", lambda e: e.memset(r["lsc"][0:32, :], -2.0), [], ["lsc"])
        self.OP("dve", lambda e: e.memset(r["lsc"][32:64, :], 0.0), ["lsc"], ["lsc"])
        self.OP("dve", lambda e: e.memset(r["lsc"][64:128, :], -1.0), ["lsc"], ["lsc"])
        for nme, src, r0, r1 in (("w2p", "rwkv_w2", 0, 32), ("a2p", "rwkv_a2", 32, 64), ("g2p", "rwkv_g2", 64, 128)):
            self.OP("dve", lambda e, nme=nme: e.memset(r[nme][:], 0.0), [], [nme])
            P.dma("sp", r[nme][r0:r1, :], w[src][l], reads=[nme], writes=[nme])
        self.bcast_load(r["lnw_bc"][:], w["rwkv_ln_w"][l], "lnw_bc")
        self.bcast_load(r["lnb_bc"][:], w["rwkv_ln_b"][l], "lnb_bc")
        self.OP("dve", lambda e: e.memset(r["Sf"][:], 0.0), [], ["Sf"])
        self.OP("dve", lambda e: e.memset(r["Sb"][:], 0.0), [], ["Sb"])
        self.OP("dve", lambda e: e.memset(r["hist"][:], 0.0), [], ["rhist"])
        self.OP("dve", lambda e: e.memset(r["tiny"][:], 1e-24), [], ["tiny"])
        self.OP("dve", lambda e: e.memset(r["gneps"][:], 64e-5), [], ["gneps"])
        self.OP("dve", lambda e: e.tensor_scalar(out=r["lstrict"][:], in0=self.cT["triu"][:], scalar1=-1.0, scalar2=1.0, op0=ALU.mult, op1=ALU.add),
                ["k_triu"], ["lstrict"])

    def rwkv_front(self, ti, raw, zs, n, step):
        r = self.r
        rk = [("rraw", q) for q in range(7)]
        for q in range(7):
            self.inproj_fm(ti, q * 128, 128, raw[:, q, step:step + n], [("rraw", q)], eng="act")
        self.OP("dve", lambda e: e.tensor_tensor(out=zs[:, :, 0:n], in0=raw[:, :, 0:n], in1=raw[:, :, step:step + n], op=ALU.subtract), rk, ["zs"])
        self.OP("dve", lambda e: e.tensor_tensor(out=zs[:, :, 0:n], in0=zs[:, :, 0:n], in1=r["mu7"][:].unsqueeze(2).to_broadcast([128, 7, n]), op=ALU.mult),
                ["zs", "mu7"], ["zs"])
        self.OP("dve", lambda e: e.tensor_tensor(out=zs[:, :, 0:n], in0=zs[:, :, 0:n], in1=raw[:, :, step:step + n], op=ALU.add), ["zs"] + rk, ["zs"])
        la = r["lact"]
        self.OP("act", lambda e: e.activation(out=la[:, 0:n], in_=zs[:, 6, 0:n], func=AF.Exp, scale=r["lsc"][:]), ["zs", "lsc"], ["lact"])
        self.OP("act", lambda e: e.activation(out=la[:, 0:n], in_=la[:, 0:n], func=AF.Ln, bias=1.0), ["lact"], ["lact"])
        self.OP("act", lambda e: e.activation(out=la[:, 0:n], in_=la[:, 0:n], func=AF.Exp, scale=-1.0), ["lact"], ["lact"])
        self.OP("dve", lambda e: e.tensor_scalar(out=la[0:32, 0:n], in0=la[0:32, 0:n], scalar1=2.0, scalar2=-1.0, op0=ALU.mult, op1=ALU.add), ["lact"], ["lact"])
        self.OP("dve", lambda e: e.tensor_copy(out=la[32:64, 0:n], in_=zs[32:64, 6, 0:n]), ["lact", "zs"], ["lact"])

    def sigm_from(self, dst, src_psum, nbias, rk, key):
        self.OP("act", lambda e: e.activation(out=dst, in_=src_psum, func=AF.Exp, scale=-1.0, bias=nbias), rk, [key])
        self.OP("act", lambda e: e.activation(out=dst, in_=dst, func=AF.Ln, bias=1.0), [key], [key])
        self.OP("act", lambda e: e.activation(out=dst, in_=dst, func=AF.Exp, scale=-1.0), [key], [key])

    def rwkv_prelim(self, j, zs, c0, n, chunked):
        r = self.r
        tok = slice(c0, c0 + n)
        rr, kr = zs[:, j, tok], zs[:, 2 + j, tok]
        la = r["lact"]
        PS = self.bank[2]
        jc = slice(j * 128, (j + 1) * 128)
        V = lambda nme: r[nme][:, j, 0:n]
        K = lambda nme: (nme, j)
        self.OP("pe", lambda e: e.matmul(PS[:, 0:n], lhsT=r["w2p"][:, jc], rhs=la[:, tok], start=True, stop=True), ["w2p", "lact"], [("bank", 2)])
        self.OP("pe", lambda e: e.matmul(PS[:, 128:128 + n], lhsT=r["a2p"][:, jc], rhs=la[:, tok], start=True, stop=True), ["a2p", "lact"], [("bank", 2)])
        self.sigm_from(V("sgw"), PS[:, 0:n], r["w0n"][:, j:j + 1], [("bank", 2), "w0n"], K("sgw"))
        self.sigm_from(V("al"), PS[:, 128:128 + n], r["a0n"][:, j:j + 1], [("bank", 2), "a0n"], K("al"))
        self.OP("act", lambda e: e.activation(out=V("kk"), in_=kr, func=AF.Copy, scale=r["kk_"][:, j:j + 1]), ["zs", "kk_"], [K("kk")])
        self.OP("act", lambda e: e.activation(out=V("T1"), in_=V("kk"), func=AF.Square), [K("kk")], [K("T1")])
        self.OP("pe", lambda e: e.matmul(PS[:, 256:256 + n], lhsT=self.cT["blk64"][:], rhs=V("T1"), start=True, stop=True), [K("T1"), "k_blk64"], [("bank", 2)])
        self.OP("act", lambda e: e.activation(out=V("T1"), in_=PS[:, 256:256 + n], func=AF.Ln, bias=r["tiny"][:]), [("bank", 2), "tiny"], [K("T1")])
        self.OP("act", lambda e: e.activation(out=V("T1"), in_=V("T1"), func=AF.Exp, scale=-0.5), [K("T1")], [K("T1")])
        self.OP("dve", lambda e: e.tensor_tensor(out=V("kk"), in0=V("kk"), in1=V("T1"), op=ALU.mult), [K("kk"), K("T1")], [K("kk")])
        self.OP("dve", lambda e: e.tensor_scalar(out=V("T1"), in0=V("al"), scalar1=r["ka"][:, j:j + 1], scalar2=r["oma"][:, j:j + 1], op0=ALU.mult, op1=ALU.add),
                [K("al"), "ka", "oma", K("T1")], [K("T1")])
        self.OP("dve", lambda e: e.tensor_tensor(out=V("kp"), in0=kr, in1=V("T1"), op=ALU.mult), ["zs", K("T1")], [K("kp")])
        self.OP("dve", lambda e: e.tensor_tensor(out=V("bq"), in0=V("kk"), in1=V("al"), op=ALU.mult), [K("kk"), K("al")], [K("bq")])
        self.OP("dve", lambda e: e.scalar_tensor_tensor(out=V("rkr"), in0=rr, scalar=r["rk"][:, j:j + 1], in1=V("kp"), op0=ALU.mult, op1=ALU.mult),
                ["zs", "rk", K("kp")], [K("rkr")])
        if not chunked:
            self.OP("act", lambda e: e.activation(out=V("Ep"), in_=V("sgw"), func=AF.Exp, scale=-self.C0), [K("sgw")], [K("Ep")])
            return
        self.OP("dve", lambda e: e.tensor_tensor_scan(out=V("cs"), data0=self.ones32[:, 0:n], data1=V("sgw"), initial=0.0, op0=ALU.mult, op1=ALU.add),
                [K("sgw"), "ones32"], [K("cs")])
        self.OP("act", lambda e: e.activation(out=V("Ep"), in_=V("cs"), func=AF.Exp, scale=-self.C0), [K("cs")], [K("Ep")])
        self.OP("act", lambda e: e.activation(out=V("Em"), in_=V("cs"), func=AF.Exp, scale=self.C0), [K("cs")], [K("Em")])
        self.OP("dve", lambda e: e.tensor_tensor(out=V("sgw"), in0=V("cs"), in1=V("sgw"), op=ALU.subtract), [K("cs"), K("sgw")], [K("sgw")])
        self.OP("act", lambda e: e.activation(out=V("Epv"), in_=V("sgw"), func=AF.Exp, scale=-self.C0), [K("sgw")], [K("Epv")])
        self.OP("dve", lambda e: e.scalar_tensor_tensor(out=V("At"), in0=V("kk"), scalar=-1.0, in1=V("Epv"), op0=ALU.mult, op1=ALU.mult), [K("kk"), K("Epv")], [K("At")])
        self.OP("dve", lambda e: e.tensor_tensor(out=V("Rt"), in0=rr, in1=V("Ep"), op=ALU.mult), ["zs", K("Ep")], [K("Rt")])
        self.OP("dve", lambda e: e.tensor_tensor(out=V("Bh"), in0=V("bq"), in1=V("Em"), op=ALU.mult), [K("bq"), K("Em")], [K("Bh")])
        self.OP("dve", lambda e: e.tensor_tensor(out=V("Kh"), in0=V("kp"), in1=V("Em"), op=ALU.mult), [K("kp"), K("Em")], [K("Kh")])
        for src, dstb in (("At", "Atb"), ("Rt", "Rtb"), ("Bh", "Bhb")):
            self.OP("act", lambda e, src=src, dstb=dstb: e.copy(out=r[dstb][:, j, :], in_=r[src][:, j, :]), [K(src)], [K(dstb)])
        hm = r["hm64"][:].unsqueeze(2).to_broadcast([128, 2, 128])
        for src, dstm in (("At", "Atm"), ("Bh", "Bhm"), ("Kh", "Khm")):
            self.OP("dve", lambda e, src=src, dstm=dstm: e.tensor_tensor(out=r[dstm][:, 2 * j:2 * j + 2, :],
                                                                         in0=r[src][:, j, :].unsqueeze(1).to_broadcast([128, 2, 128]), in1=hm, op=ALU.mult),
                    [K(src), "hm64"], [K(dstm)])
        for nme in ("Bh", "Kh"):
            self.OP("act", lambda e, nme=nme: e.activation(out=r[nme][:, j, :], in_=r[nme][:, j, :], func=AF.Copy, scale=r["Ep"][:, j, 127:128]),
                    [K(nme), K("Ep")], [K(nme)])

    def rwkv_tokmajor(self, zs, c0, m, chunked):
        r = self.r
        tok = slice(c0, c0 + m)
        TV = self.bank[4]

        def trv(e):
            rr_ = None
            for j in range(2):
                rr_ = e.transpose(out=TV[0:m, j * 128:(j + 1) * 128], in_=zs[:, 4 + j, tok], identity=self.ident[:])
            return rr_
        self.OP("pe", trv, ["zs", "ident"], [("bank", 4)])
        self.OP("act", lambda e: e.copy(out=r["vtf"][0:m, :], in_=TV[0:m, 0:256]), [("bank", 4)], ["vtf"])
        self.OP("act", lambda e: e.copy(out=r["Vtb"][0:m, :], in_=TV[0:m, 0:256]), [("bank", 4)], ["Vtb"])
        if chunked:
            TK = self.bank[5]

            def trb(e):
                rr_ = None
                for j in range(2):
                    e.transpose(out=TV[:, 256 + j * 128:256 + (j + 1) * 128], in_=r["Bh"][:, j, :], identity=self.ident[:])
                    rr_ = e.transpose(out=TK[:, j * 128:(j + 1) * 128], in_=r["Kh"][:, j, :], identity=self.ident[:])
                return rr_
            self.OP("pe", trb, [("Bh", 0), ("Bh", 1), ("Kh", 0), ("Kh", 1), "ident"], [("bank", 4), ("bank", 5)])
            self.OP("act", lambda e: e.copy(out=r["Bcb"][:], in_=TV[:, 256:512]), [("bank", 4)], ["Bcb"])
            self.OP("act", lambda e: e.copy(out=r["Kcb"][:], in_=TK[:, 0:256]), [("bank", 5)], ["Kcb"])
        G = self.bank[3]
        self.OP("pe", lambda e: e.matmul(G[0:m, 0:256], lhsT=r["lact"][:, tok], rhs=r["g2p"][:], start=True, stop=True), ["lact", "g2p"], [("bank", 3)])
        self.OP("act", lambda e: e.copy(out=r["gsb"][0:m, :], in_=G[0:m, 0:256]), [("bank", 3)], ["gsb"])

        def mmb(e):
            rr_ = None
            for j in range(2):
                rr_ = e.matmul(G[0:m, 256 + 2 * j:258 + 2 * j], lhsT=r["rkr"][:, j, 0:m], rhs=r["hm64"][:], start=True, stop=True)
            return rr_
        self.OP("pe", mmb, [("rkr", 0), ("rkr", 1), "hm64"], [("bank", 3)])
        self.OP("act", lambda e: e.copy(out=r["bon"][0:m, :], in_=G[0:m, 256:260]), [("bank", 3)], ["bon"])

    def rwkv_post(self, m, tokcols):
        r = self.r
        sm = self.sm
        Y, ysq, yt = r["Ysb"], r["ysq"], r["ytmp"]
        v3 = lambda ap: ap[0:m, :].rearrange("p (h d) -> p h d", h=4)
        self.OP("act", lambda e: e.activation(out=ysq[0:m, :], in_=Y[0:m, :], func=AF.Square), ["Ysb"], ["ysq"])
        self.OP("dve", lambda e: e.tensor_reduce(out=sm[0:m, 0:4], in_=v3(Y), axis=AX.X, op=ALU.add), ["Ysb"], ["sm"])
        self.OP("dve", lambda e: e.tensor_reduce(out=sm[0:m, 4:8], in_=v3(ysq), axis=AX.X, op=ALU.add), ["ysq", "sm"], ["sm"])
        self.OP("dve", lambda e: e.tensor_scalar_mul(out=sm[0:m, 0:8], in0=sm[0:m, 0:8], scalar1=1.0 / 64), ["sm"], ["sm"])
        self.OP("dve", lambda e: e.tensor_tensor(out=sm[0:m, 8:12], in0=sm[0:m, 0:4], in1=sm[0:m, 0:4], op=ALU.mult), ["sm"], ["sm"])
        self.OP("dve", lambda e: e.tensor_tensor(out=sm[0:m, 8:12], in0=sm[0:m, 4:8], in1=sm[0:m, 8:12], op=ALU.subtract), ["sm"], ["sm"])
        self.OP("act", lambda e: e.activation(out=sm[0:m, 8:12], in_=sm[0:m, 8:12], func=AF.Ln, bias=r["gneps"][0:m, :]), ["sm", "gneps"], ["sm"])
        self.OP("act", lambda e: e.activation(out=sm[0:m, 8:12], in_=sm[0:m, 8:12], func=AF.Exp, scale=-0.5), ["sm"], ["sm"])
        self.OP("dve", lambda e: e.tensor_tensor(out=v3(Y), in0=v3(Y), in1=sm[0:m, 0:4].unsqueeze(2).to_broadcast([m, 4, 64]), op=ALU.subtract), ["Ysb", "sm"], ["Ysb"])
        self.OP("dve", lambda e: e.tensor_tensor(out=v3(Y), in0=v3(Y), in1=sm[0:m, 8:12].unsqueeze(2).to_broadcast([m, 4, 64]), op=ALU.mult), ["Ysb", "sm"], ["Ysb"])
        self.OP("dve", lambda e: e.tensor_tensor(out=Y[0:m, :], in0=Y[0:m, :], in1=r["lnw_bc"][0:m, :], op=ALU.mult), ["Ysb", "lnw_bc"], ["Ysb"])
        self.OP("dve", lambda e: e.tensor_tensor(out=Y[0:m, :], in0=Y[0:m, :], in1=r["lnb_bc"][0:m, :], op=ALU.add), ["Ysb", "lnb_bc"], ["Ysb"])
        self.OP("dve", lambda e: e.tensor_tensor(out=v3(yt), in0=v3(r["vtf"]), in1=r["bon"][0:m, :].unsqueeze(2).to_broadcast([m, 4, 64]), op=ALU.mult),
                ["vtf", "bon"], ["ytmp_r"])
        self.OP("dve", lambda e: e.tensor_tensor(out=Y[0:m, :], in0=Y[0:m, :], in1=yt[0:m, :], op=ALU.add), ["Ysb", "ytmp_r"], ["Ysb"])
        self.OP("dve", lambda e: e.tensor_tensor(out=Y[0:m, :], in0=Y[0:m, :], in1=r["gsb"][0:m, :], op=ALU.mult), ["Ysb", "gsb"], ["Ysb"])
        TB = self.bank[5]

        def tr(e):
            rr_ = None
            for q in range(2):
                rr_ = e.transpose(out=TB[:, q * 128:q * 128 + m], in_=Y[0:m, q * 128:(q + 1) * 128], identity=self.ident[0:m, 0:m])
            return rr_
        self.OP("pe", tr, ["Ysb", "ident"], [("bank", 5)])
        self.OP("act", lambda e: e.copy(out=self.yT[:, 0:2, tokcols], in_=TB[:, 0:256].rearrange("p (q t) -> p q t", q=2)[:, :, 0:m]),
                [("bank", 5)], [("yT", 0), ("yT", 1)])

    def rwkv_tile(self, l, ti):
        P = self.P
        r = self.r
        t0, n = TILES[ti]
        raw, zs = r["raw"], r["zs"]
        self.OP("dve", lambda e: e.tensor_copy(out=raw[:, :, 0:1], in_=r["hist"][:]), ["rhist"], [("rraw", q) for q in range(7)])
        self.rwkv_front(ti, raw, zs, n, 1)
        self.OP("dve", lambda e: e.tensor_copy(out=r["hist"][:], in_=raw[:, :, n:n + 1]), [("rraw", q) for q in range(7)], ["rhist"])
        if ti == 3:
            P.dma("sp", self.o_p["shift"][l].rearrange("(q p) -> p q", p=128), raw[:, :, n], reads=[("rraw", q) for q in range(7)],
                  writes=[("o_p_shift", l)], allow_slow_non_contiguous=True)
        for k in range(n // 128):
            self.rwkv_chunk(l, ti, k)
        if ti == 3:
            self.rwkv_state_out(l)

    def rwkv_scores(self, name, lhs, rhs, mask, bi):
        r = self.r
        bank = self.bank[bi]

        def mm(e):
            rr_ = None
            for h in range(4):
                rr_ = e.matmul(bank[:, h * 128:(h + 1) * 128], lhsT=r[lhs][:, h, :], rhs=r[rhs][:, h // 2, :], start=True, stop=True)
            return rr_
        self.OP("pe", mm, [(lhs, 0), (lhs, 1), (rhs, 0), (rhs, 1)], [("bank", bi)])
        self.OP("act", lambda e: e.copy(out=r["stg"][:].rearrange("p h t -> p (h t)"), in_=bank[:]), [("bank", bi)], ["stg"])
        self.OP("dve", lambda e: e.tensor_tensor(out=r[name][:], in0=r["stg"][:], in1=mask.unsqueeze(1).to_broadcast([128, 4, 128]), op=ALU.mult),
                ["stg", "k_triu", "k_mstrict", "lstrict"], [name])

    def rwkv_chunk(self, l, ti, k):
        base = self.r
        par = (ti * 4 + k) % 2
        if par == 1:
            rr = dict(base)
            for nme in self.DBK:
                rr[nme] = base[nme + "_1"]
            self.r = rr
        self.kpar = par
        try:
            self._rwkv_chunk(l, ti, k)
        finally:
            self.r = base
            self.kpar = None

    def _rwkv_chunk(self, l, ti, k):
        r = self.r
        c0 = k * 128
        tok = slice(c0, c0 + 128)
        zs = r["zs"]
        for j in range(2):
            self.rwkv_prelim(j, zs, c0, 128, True)
        self.rwkv_tokmajor(zs, c0, 128, True)
        triu, mstrict, lstrict = self.cT["triu"][:], self.cT["mstrict"][:], r["lstrict"][:]
        self.rwkv_scores("A", "Bhm", "Atb", mstrict, 6)
        self.rwkv_scores("M", "Atm", "Bhb", lstrict, 7)
        self.rwkv_scores("MakT", "Khm", "Atb", mstrict, 6)
        self.rwkv_scores("NrbT", "Bhm", "Rtb", triu, 7)
        self.rwkv_scores("NrkT", "Khm", "Rtb", triu, 6)
        self.OP("dve", lambda e: e.tensor_tensor(out=r["Z"][:], in0=r["A"][:], in1=self.ident[:].unsqueeze(1).to_broadcast([128, 4, 128]), op=ALU.add),
                ["A", "ident"], ["Z"])
        BA, BM, BZ = self.bank[0], self.bank[1], self.bank[7]
        for lvl in range(6):
            last = (lvl == 5)
            if not last:
                def mmA(e):
                    rr_ = None
                    for h in range(4):
                        rr_ = e.matmul(BA[:, h * 128:(h + 1) * 128], lhsT=r["M"][:, h, :], rhs=r["A"][:, h, :], start=True, stop=True)
                    return rr_
                self.OP("pe", mmA, ["M", "A"], [("bank", 0)])

            def mmM(e):
                rr_ = None
                for h in range(4):
                    rr_ = e.matmul(BM[:, h * 128:(h + 1) * 128], lhsT=r["A"][:, h, :], rhs=r["M"][:, h, :], start=True, stop=True)
                return rr_
            self.OP("pe", mmM, ["M", "A"], [("bank", 1)])
            if not last:
                self.OP("act", lambda e: e.copy(out=r["A"][:].rearrange("p h t -> p (h t)"), in_=BA[:]), [("bank", 0)], ["A"])
            self.OP("act", lambda e: e.copy(out=r["M"][:].rearrange("p h t -> p (h t)"), in_=BM[:]), [("bank", 1)], ["M"])

            def mmZ(e):
                rr_ = None
                for h in range(4):
                    rr_ = e.matmul(BZ[:, h * 128:(h + 1) * 128], lhsT=r["M"][:, h, :], rhs=r["Z"][:, h, :], start=True, stop=True)
                return rr_
            self.OP("pe", mmZ, ["M", "Z"], [("bank", 7)])
            self.OP("dve", lambda e: e.tensor_tensor(out=r["Z"][:].rearrange("p h t -> p (h t)"), in0=r["Z"][:].rearrange("p h t -> p (h t)"), in1=BZ[:], op=ALU.add),
                    [("bank", 7), "Z"], ["Z"])
        R1, R2 = self.bank[0], self.bank[1]

        def mmR(e):
            rr_ = None
            for j in range(2):
                e.matmul(R1[:, j * 128:(j + 1) * 128], lhsT=r["Atb"][:, j, :], rhs=r["Sb"][:, j, :], start=True, stop=False)
                for h2 in range(2):
                    h = 2 * j + h2
                    rr_ = e.matmul(R1[:, h * 64:(h + 1) * 64], lhsT=r["MakT"][:, h, :], rhs=r["Vtb"][:, h * 64:(h + 1) * 64], start=False, stop=(h2 == 1))
            return rr_
        self.OP("pe", mmR, [("Atb", 0), ("Atb", 1), "Sb", "MakT", "Vtb"], [("bank", 0)])
        self.OP("act", lambda e: e.copy(out=r["RHSb"][:], in_=R1[:, 0:256]), [("bank", 0)], ["RHSb"])

        def mmU(e):
            rr_ = None
            for h in range(4):
                rr_ = e.matmul(R2[:, h * 64:(h + 1) * 64], lhsT=r["Z"][:, h, :], rhs=r["RHSb"][:, h * 64:(h + 1) * 64], start=True, stop=True)
            return rr_
        self.OP("pe", mmU, ["Z", "RHSb"], [("bank", 1)])
        self.OP("act", lambda e: e.copy(out=r["Ub"][:], in_=R2[:, 0:256]), [("bank", 1)], ["Ub"])

        def mmY(e):
            rr_ = None
            for j in range(2):
                e.matmul(R1[:, 256 + j * 128:256 + (j + 1) * 128], lhsT=r["Rtb"][:, j, :], rhs=r["Sb"][:, j, :], start=True, stop=False)
                for h2 in range(2):
                    h = 2 * j + h2
                    cs_ = slice(256 + h * 64, 256 + (h + 1) * 64)
                    e.matmul(R1[:, cs_], lhsT=r["NrbT"][:, h, :], rhs=r["Ub"][:, h * 64:(h + 1) * 64], start=False, stop=False)
                    rr_ = e.matmul(R1[:, cs_], lhsT=r["NrkT"][:, h, :], rhs=r["Vtb"][:, h * 64:(h + 1) * 64], start=False, stop=(h2 == 1))
            return rr_
        self.OP("pe", mmY, [("Rtb", 0), ("Rtb", 1), "Sb", "NrbT", "NrkT", "Ub", "Vtb"], [("bank", 0)])
        self.OP("act", lambda e: e.copy(out=r["Ysb"][:], in_=R1[:, 256:512]), [("bank", 0)], ["Ysb"])

        def mmS(e):
            rr_ = None
            for j in range(2):
                jc = slice(j * 128, (j + 1) * 128)
                e.matmul(R2[:, 256 + j * 128:256 + (j + 1) * 128], lhsT=r["Bcb"][:, jc], rhs=r["Ub"][:, jc], start=True, stop=False)
                rr_ = e.matmul(R2[:, 256 + j * 128:256 + (j + 1) * 128], lhsT=r["Kcb"][:, jc], rhs=r["Vtb"][:, jc], start=False, stop=True)
            return rr_
        self.OP("pe", mmS, ["Bcb", "Kcb", "Ub", "Vtb"], [("bank", 1)])
        for j in range(2):
            self.OP("dve", lambda e, j=j: e.tensor_tensor(out=r["T1"][:, j, :], in0=R2[:, 256 + j * 128:256 + (j + 1) * 128], in1=self.cT["blk64"][:], op=ALU.mult),
                    [("bank", 1), "k_blk64"], [("T1", j)])
            self.OP("dve", lambda e, j=j: e.scalar_tensor_tensor(out=r["Sf"][:, j, :], in0=r["Sf"][:, j, :], scalar=r["Ep"][:, j, 127:128], in1=r["T1"][:, j, :],
                                                                 op0=ALU.mult, op1=ALU.add), ["Sf", ("Ep", j), ("T1", j)], ["Sf"])
        self.OP("act", lambda e: e.copy(out=r["Sb"][:], in_=r["Sf"][:]), ["Sf"], ["Sb"])
        self.rwkv_post(128, tok)

    def rwkv_state_out(self, l):
        r = self.r
        TB = self.bank[5]

        def tr(e):
            rr_ = None
            for j in range(2):
                rr_ = e.transpose(out=TB[:, j * 128:(j + 1) * 128], in_=r["Sf"][:, j, :], identity=self.ident[:])
            return rr_
        self.OP("pe", tr, ["Sf", "ident"], [("bank", 5)])
        self.OP("act", lambda e: e.copy(out=r["ytmp"][:], in_=TB[:, 0:256]), [("bank", 5)], ["ytmp_r"])
        for j in range(2):
            for h2 in range(2):
                self.P.dma("sp", self.o_p["wkv"][l, 2 * j + h2], r["ytmp"][h2 * 64:(h2 + 1) * 64, j * 128 + h2 * 64:j * 128 + (h2 + 1) * 64],
                           reads=["ytmp_r"], writes=[("o_p_wkv", l, j, h2)])

    def rwkv_sample(self, l):
        P = self.P
        r = self.r
        ti = 4
        t0, n = TILES[ti]
        raw, zs = r["raw_s"], r["zs_s"]
        P.dma("sp", self.smp_tm[0:16, 0:768], self.sti["shift"][l][:, 0:768], writes=["smp_tm"])
        P.dma("sp", self.stage[0:16, 0:128], self.sti["shift"][l][:, 768:896], writes=["stage"])
        for q in range(7):
            TB = self.bank[5]
            src = self.smp_tm[0:16, q * 128:(q + 1) * 128] if q < 6 else self.stage[0:16, 0:128]
            self.OP("pe", lambda e, src=src: e.transpose(out=TB[:, 0:16], in_=src, identity=self.ident[0:16, 0:16]), ["smp_tm", "stage", "ident"], [("bank", 5)])
            self.OP("act", lambda e, q=q: e.copy(out=raw[:, q, 0:16], in_=TB[:, 0:16]), [("bank", 5)], [("rraw", q)])
        self.rwkv_front(ti, raw, zs, n, 16)
        for q in range(7):
            TB = self.bank[5]
            self.OP("pe", lambda e, q=q: e.transpose(out=TB[0:16, 0:128], in_=raw[:, q, 64:80], identity=self.ident[:]), [("rraw", q), "ident"], [("bank", 5)])
            self.OP("act", lambda e: e.copy(out=self.stage[0:16, 128:256], in_=TB[0:16, 0:128]), [("bank", 5)], ["stage"])
            P.dma("sp", self.o_s["shift"][l][:, q * 128:(q + 1) * 128], self.stage[0:16, 128:256], reads=["stage"], writes=[("o_s_shift", l, q)])
        for j in range(2):
            self.rwkv_prelim(j, zs, 0, 64, False)
        self.rwkv_tokmajor(zs, 0, 64, False)
        sc = self.sc
        for nme, src, scn, neg in (("vr", None, "rr", False), ("vw", "Ep", "rw", False), ("vk", "kp", "rk", False), ("va", "kk", "ra", True), ("vb", "bq", "rb", False)):
            TB = self.bank[5]

            def trq(e, src=src):
                rr_ = None
                for j in range(2):
                    in_ = zs[:, j, 0:64] if src is None else r[src][:, j, 0:64]
                    rr_ = e.transpose(out=TB[0:64, j * 128:(j + 1) * 128], in_=in_, identity=self.ident[:])
                return rr_
            rkeys = ["zs"] if src is None else [(src, 0), (src, 1)]
            self.OP("pe", trq, rkeys + ["ident"], [("bank", 5)])
            self.OP("act", lambda e, neg=neg: e.activation(out=r["ysq"][0:64, :], in_=TB[0:64, 0:256], func=AF.Copy, scale=(-1.0 if neg else 1.0)),
                    [("bank", 5)], ["ysq"])
            for t in range(4):
                rows = slice(t * 16, (t + 1) * 16)
                for vh in range(2):
                    P.dma("sp", sc[scn].rearrange("(vh b h) (t w) -> vh b h t w", vh=2, h=4, t=4)[vh, :, :, t, :],
                          r["ysq"][rows, :].rearrange("b (h w) -> b h w", h=4), reads=["ysq"], writes=["sc_" + scn])
            P.dma("sp", r[nme][:].rearrange("p t w -> p (t w)"), sc[scn], reads=["sc_" + scn], writes=[nme])
        for t in range(4):
            rows = slice(t * 16, (t + 1) * 16)
            for vh in range(2):
                P.dma("sp", sc["rv"].rearrange("(vh b h) (t w) -> vh b h t w", vh=2, h=4, t=4)[vh, :, :, t, :],
                      r["vtf"][rows, :].rearrange("b (h w) -> b h w", h=4)[:, :, vh * 32:(vh + 1) * 32], reads=["vtf"], writes=["sc_rv"])
        P.dma("sp", r["vv"][:].rearrange("p t w -> p (t w)"), sc["rv"], reads=["sc_rv"], writes=["vv"])
        Ss, tS = r["Ss"], r["tS"]
        for vh in range(2):
            P.dma("sp", Ss[vh * 64:(vh + 1) * 64].rearrange("p v k -> p (v k)"),
                  self.sti["wkv"][l].rearrange("b h v k -> (b h) (v k)")[:, vh * 2048:(vh + 1) * 2048], writes=["Ss_r"])
        bk = lambda ap, t: ap[:, t, :].unsqueeze(1).to_broadcast([128, 32, 64])
        for t in range(4):
            self.OP("dve", lambda e, t=t: e.tensor_tensor(out=tS[:], in0=Ss[:], in1=bk(r["va"], t), op=ALU.mult), ["Ss_r", "va"], ["tS_r"])
            self.OP("dve", lambda e, t=t: e.tensor_reduce(out=r["sa"][:], in_=tS[:], axis=AX.X, op=ALU.add), ["tS_r"], ["sa"])
            self.OP("dve", lambda e, t=t: e.tensor_tensor(out=Ss[:], in0=Ss[:], in1=bk(r["vw"], t), op=ALU.mult), ["Ss_r", "vw"], ["Ss_r"])
            self.OP("dve", lambda e, t=t: e.tensor_tensor(out=tS[:], in0=r["sa"][:].unsqueeze(2).to_broadcast([128, 32, 64]), in1=bk(r["vb"], t), op=ALU.mult),
                    ["sa", "vb", "tS_r"], ["tS_r"])
            self.OP("dve", lambda e, t=t: e.tensor_tensor(out=Ss[:], in0=Ss[:], in1=tS[:], op=ALU.add), ["Ss_r", "tS_r"], ["Ss_r"])
            self.OP("dve", lambda e, t=t: e.tensor_tensor(out=tS[:], in0=r["vv"][:, t, :].unsqueeze(2).to_broadcast([128, 32, 64]), in1=bk(r["vk"], t), op=ALU.mult),
                    ["vv", "vk", "tS_r"], ["tS_r"])
            self.OP("dve", lambda e, t=t: e.tensor_tensor(out=Ss[:], in0=Ss[:], in1=tS[:], op=ALU.add), ["Ss_r", "tS_r"], ["Ss_r"])
            self.OP("dve", lambda e, t=t: e.tensor_tensor(out=tS[:], in0=Ss[:], in1=bk(r["vr"], t), op=ALU.mult), ["Ss_r", "vr", "tS_r"], ["tS_r"])
            self.OP("dve", lambda e, t=t: e.tensor_reduce(out=r["vy"][:, t, :], in_=tS[:], axis=AX.X, op=ALU.add), ["tS_r"], ["vy"])
        for vh in range(2):
            P.dma("sp", self.o_s["wkv"][l].rearrange("b h v k -> (b h) (v k)")[:, vh * 2048:(vh + 1) * 2048],
                  Ss[vh * 64:(vh + 1) * 64].rearrange("p v k -> p (v k)"), reads=["Ss_r"], writes=[("o_s_wkv", l, vh)])
        P.dma("sp", sc["ry"], r["vy"][:].rearrange("p t w -> p (t w)"), reads=["vy"], writes=["sc_ry"])
        for t in range(4):
            rows = slice(t * 16, (t + 1) * 16)
            for vh in range(2):
                P.dma("sp", r["Ysb"][rows, :].rearrange("b (h w) -> b h w", h=4)[:, :, vh * 32:(vh + 1) * 32],
                      sc["ry"].rearrange("(vh b h) (t w) -> vh b h t w", vh=2, h=4, t=4)[vh, :, :, t, :], reads=["sc_ry"], writes=["Ysb"])
        self.rwkv_post(64, slice(0, 64))

    def build(self):
        P = self.P
        self.wcount = 0
        self.bcount = 0
        import os
        self.kch = 0
        self.mix = ["ssd", "gla", "rwkv"]
        self.setup()
        self.gsched = True
        P.sched = self.gsched
        self.convert_ffn(0, 1)
        self.load_x()
        self.convert_mix(0)
        self.convert_ffn(0, 2)
        self.convert_ple(0)
        for l in range(2):
            self.ffn_phase(l, 1)
            if l == 0:
                self.convert_ffn(1, 1)
            if self.mix:
                self.mixer_phase(l)
            if l == 0:
                self.convert_mix(1)
            self.ffn_phase(l, 2)
            if l == 0:
                self.convert_ffn(1, 2)
            self.ple_phase(l)
            if l == 0:
                self.convert_ple(1)
        P.barrier()
        self.final()
        P.barrier(final=True)
        P.run(self.st)


_CACHE = {}


def build_program(shapes):
    nc = bass.Bass("TRN2", target_bir_lowering=False)
    with ExitStack() as st:
        k = K(nc, st, shapes)
        k.build()
    return nc


def kernel(**inputs):
    inp = {k: np.ascontiguousarray(np.asarray(v, dtype=np.float32)) for k, v in inputs.items()}
    shapes = {n: inp[n].shape for n in WNAMES}
    key = tuple(sorted((n, tuple(s)) for n, s in shapes.items()))
    if key not in _CACHE:
        _CACHE[key] = build_program(shapes)
    nc = _CACHE[key]
    consts = host_consts()
    in_maps = []
    for c in range(NCORES):
        m = {}
        m["xin"] = np.ascontiguousarray(np.concatenate(
            [inp["x_prompt"][c], inp["x_sample"][16 * c:16 * c + 16].transpose(1, 0, 2).reshape(TS, D)], axis=0))
        m["pin"] = np.ascontiguousarray(np.concatenate(
            [inp["p_prompt"][:, c], inp["p_sample"][:, 16 * c:16 * c + 16].transpose(0, 2, 1, 3).reshape(2, TS, PLE)], axis=1))
        m["st_shift"] = np.ascontiguousarray(inp["state_rwkv_shift"][:, 16 * c:16 * c + 16])
        m["st_wkv"] = np.ascontiguousarray(inp["state_rwkv_wkv"][:, 16 * c:16 * c + 16])
        m["st_gla"] = np.ascontiguousarray(inp["state_gla"][:, 16 * c:16 * c + 16])
        m["st_conv"] = np.ascontiguousarray(inp["state_mamba_conv"][:, 16 * c:16 * c + 16])
        m["st_ssm"] = np.ascontiguousarray(inp["state_mamba_ssm"][:, 16 * c:16 * c + 16])
        for n in WNAMES:
            m[n] = inp[n]
        for n, a in consts.items():
            m["c_" + n] = a
        in_maps.append(m)
    res = run_bass_kernel_spmd(nc, in_maps, core_ids=list(range(NCORES)))
    R = list(res.results)
    y = np.stack([r["yout"] for r in R], axis=0)
    y_prompt = np.ascontiguousarray(y[:, :TP, :])
    y_sample = np.ascontiguousarray(y[:, TP:, :].reshape(NCORES, 4, 16, D).transpose(0, 2, 1, 3).reshape(NCORES * 16, 4, D))
    outs = [y_prompt, y_sample]
    for nm in ("shift", "wkv", "gla", "conv", "ssm"):
        outs.append(np.ascontiguousarray(np.stack([r["o_p_" + nm] for r in R], axis=1)))
    for nm in ("shift", "wkv", "gla", "conv", "ssm"):
        outs.append(np.ascontiguousarray(np.concatenate([r["o_s_" + nm] for r in R], axis=1)))
    return tuple(outs)
```

```python
import numpy as np
from contextlib import ExitStack
import concourse.bass as bass
import concourse.mybir as mybir
from concourse.bass_utils import run_bass_kernel_spmd

F32 = mybir.dt.float32
BF16 = mybir.dt.bfloat16
AF = mybir.ActivationFunctionType
ALU = mybir.AluOpType
AX = mybir.AxisListType

NCORES = 8
D = 1024
DFF = 2816
NJ = DFF // 128
PLE = 256
TP = 2048
TS = 64
T = TP + TS
EPS = 1e-6
TILES = [(i * 512, 512) for i in range(4)] + [(2048, 64)]
FT = [(0, 448), (448, 448), (896, 448), (1344, 448), (1792, 320)]

WNAMES = ['ffn1_norm', 'ffn1_w_gate', 'ffn1_w_up', 'ffn1_w_down', 'mix_norm', 'w_in',
          'rwkv_mu', 'rwkv_w0', 'rwkv_w2', 'rwkv_a0', 'rwkv_a2', 'rwkv_g2', 'rwkv_k_k', 'rwkv_k_a', 'rwkv_r_k',
          'rwkv_ln_w', 'rwkv_ln_b', 'gla_gate_w2', 'gla_gate_b', 'gla_norm', 'mamba_conv_w', 'mamba_conv_b',
          'mamba_dt_bias', 'mamba_A_log', 'mamba_D', 'mamba_norm', 'w_out', 'ffn2_norm', 'ffn2_w_gate',
          'ffn2_w_up', 'ffn2_w_down', 'ple_norm', 'ple_w_gate', 'ple_w_proj', 'final_norm']


class Prog:
    ENG = ("pe", "act", "dve", "pool", "sp")
    NDMA = 12

    def __init__(self, nc, stack):
        self.nc = nc
        self.streams = {e: [] for e in self.ENG}
        self.sems = {}
        self.cnt = {}
        for e in self.ENG:
            self.sems[e] = stack.enter_context(nc.semaphore("s_" + e))
            self.cnt[e] = 0
        self.dq = {}
        for q in ("sp", "pool", "act"):
            lst = []
            for i in range(self.NDMA):
                nm = "d_%s_%d" % (q, i)
                self.sems[nm] = stack.enter_context(nc.semaphore(nm))
                self.cnt[nm] = 0
                lst.append(nm)
            self.dq[q] = [lst, 0]
        self.known = {e: {} for e in self.ENG}
        self.lastw = {}
        self.readers = {}
        self.pending = []
        self.sched = False

    def _need(self, eng, ev, waits):
        if ev is None:
            return
        s, v = ev
        if s == "pe" and eng == "pe":
            return
        if self.known[eng].get(s, 0) >= v:
            return
        if waits.get(s, 0) < v:
            waits[s] = v

    def _deps(self, eng, reads, writes):
        waits = {}
        for k in reads:
            self._need(eng, self.lastw.get(k), waits)
        for k in writes:
            self._need(eng, self.lastw.get(k), waits)
            for ev in self.readers.get(k, ()):
                self._need(eng, ev, waits)
        for s, v in waits.items():
            self.known[eng][s] = v
        return waits

    def _commit(self, ev, reads, writes):
        for k in reads:
            self.readers.setdefault(k, []).append(ev)
        for k in writes:
            self.lastw[k] = ev
            self.readers[k] = []

    def _op_now(self, eng, fn, reads=(), writes=()):
        waits = self._deps(eng, reads, writes)
        self.cnt[eng] += 1
        ev = (eng, self.cnt[eng])
        sem = self.sems[eng]
        wl = [(self.sems[s], v) for s, v in waits.items()]

        def emit(e, fn=fn, wl=wl, sem=sem):
            for s, v in wl:
                e.wait_ge(s, v)
            fn(e).then_inc(sem, 1)
        self.streams[eng].append(emit)
        self._commit(ev, reads, writes)
        return ev

    def _dma_now(self, q, out, in_, reads=(), writes=(), **kw):
        lst, i = self.dq[q]
        nm = lst[i % self.NDMA]
        self.dq[q][1] = i + 1
        waits = self._deps(q, reads, writes)
        prev = self.cnt[nm]
        if prev > 0 and self.known[q].get(nm, 0) < prev:
            waits[nm] = max(waits.get(nm, 0), prev)
            self.known[q][nm] = prev
        self.cnt[nm] += 16
        ev = (nm, self.cnt[nm])
        sem = self.sems[nm]
        wl = [(self.sems[s], v) for s, v in waits.items()]

        def emit(e, wl=wl, sem=sem, out=out, in_=in_, kw=kw):
            for s, v in wl:
                e.wait_ge(s, v)
            e.dma_start(out=out, in_=in_, **kw).then_inc(sem, 16)
        self.streams[q].append(emit)
        self._commit(ev, reads, writes)
        return ev

    import os as _os2
    _sc = 1.0
    DUR = {"pe": 0.45 * _sc, "act": 0.55 * _sc, "dve": 0.45 * _sc, "pool": 0.6, "sp": 2.5}
    LAT = 0.45
    import os as _os
    WINDOW = 700
    SLACK = 0.4

    def op(self, eng, fn, reads=(), writes=(), dur=None):
        if not getattr(self, "sched", False):
            return self._op_now(eng, fn, reads, writes)
        self.pending.append(("op", eng, fn, None, list(reads), list(writes), dur if dur is not None else self.DUR[eng]))
        if len(self.pending) >= self.WINDOW:
            self.flush()

    def dma(self, q, out, in_, reads=(), writes=(), **kw):
        if not getattr(self, "sched", False):
            return self._dma_now(q, out, in_, reads, writes, **kw)
        self.pending.append(("dma", q, (out, in_), kw, list(reads), list(writes), self.DUR["sp"]))
        if len(self.pending) >= self.WINDOW:
            self.flush()

    def flush(self):
        ops = getattr(self, "pending", [])
        self.pending = []
        n = len(ops)
        if n == 0:
            return
        lastw, readers = {}, {}
        preds = [set() for _ in range(n)]
        for i, o in enumerate(ops):
            rd, wr = o[4], o[5]
            for k in rd:
                if k in lastw:
                    preds[i].add(lastw[k])
            for k in wr:
                if k in lastw:
                    preds[i].add(lastw[k])
                for j in readers.get(k, ()):
                    preds[i].add(j)
            for k in rd:
                readers.setdefault(k, []).append(i)
            for k in wr:
                lastw[k] = i
                readers[k] = []
            preds[i].discard(i)
        succs = [[] for _ in range(n)]
        for i in range(n):
            for j in preds[i]:
                succs[j].append(i)
        prio = [0.0] * n
        for i in range(n - 1, -1, -1):
            m = 0.0
            for j in succs[i]:
                if prio[j] > m:
                    m = prio[j]
            prio[i] = ops[i][6] + m
        npred = [len(p) for p in preds]
        ready_t = [0.0] * n
        fin = [0.0] * n
        eng_free = {}
        avail = [i for i in range(n) if npred[i] == 0]
        done = 0
        while done < n:
            best, bkey = None, None
            for i in avail:
                e = ops[i][1]
                st = max(eng_free.get(e, 0.0), ready_t[i])
                key = (st, -prio[i], i)
                if bkey is None or key < bkey:
                    best, bkey = i, key
            lim = bkey[0] + self.SLACK
            for i in avail:
                e = ops[i][1]
                st = max(eng_free.get(e, 0.0), ready_t[i])
                if st <= lim and (prio[i], -i) > (prio[best], -best):
                    best, bkey = i, (st, -prio[i], i)
            i = best
            avail.remove(i)
            o = ops[i]
            e = o[1]
            st = bkey[0]
            fin[i] = st + o[6]
            eng_free[e] = fin[i] - (0.5 * o[6] if e != "sp" else o[6] - 0.1)
            if o[0] == "op":
                self._op_now(e, o[2], o[4], o[5])
            else:
                self._dma_now(e, o[2][0], o[2][1], o[4], o[5], **o[3])
            for j in succs[i]:
                npred[j] -= 1
                lat = self.LAT if ops[j][1] != e else 0.25
                if fin[i] + lat > ready_t[j]:
                    ready_t[j] = fin[i] + lat
                if npred[j] == 0:
                    avail.append(j)
            done += 1

    def wait_all(self, eng):
        wl = []
        for s, c in self.cnt.items():
            if s == "pool" or s.startswith("d_pool_"):
                continue
            if c > 0 and self.known[eng].get(s, 0) < c:
                wl.append((self.sems[s], c))
                self.known[eng][s] = c

        def emit(e, wl=wl):
            for s, v in wl:
                e.wait_ge(s, v)
        self.streams[eng].append(emit)

    def barrier(self, final=False):
        self.flush()
        for e in self.ENG:
            if e == "pool" and not final:
                continue
            self.wait_all(e)
        if final:
            wl = [(self.sems[s], c) for s, c in self.cnt.items() if c > 0]

            def emit(e, wl=wl):
                for s_, v in wl:
                    e.wait_ge(s_, v)
            self.streams["sp"].append(emit)

    def run(self, stack):
        self.flush()
        block = stack.enter_context(self.nc.Block())
        streams = self.streams

        @block.tensor
        def _(e):
            for f in streams["pe"]:
                f(e)

        @block.scalar
        def _(e):
            for f in streams["act"]:
                f(e)

        @block.vector
        def _(e):
            for f in streams["dve"]:
                f(e)

        @block.gpsimd
        def _(e):
            for f in streams["pool"]:
                f(e)

        @block.sync
        def _(e):
            for f in streams["sp"]:
                f(e)


def host_consts():
    c = {}
    c["ident"] = np.eye(128, dtype=np.float32)
    i = np.arange(128)
    c["triu"] = (i[:, None] <= i[None, :]).astype(np.float32)
    c["negmask"] = np.where(i[:, None] <= i[None, :], 0.0, -30000.0).astype(np.float32)
    c["mstrict"] = (i[:, None] < i[None, :]).astype(np.float32)
    c["blk64"] = ((i[:, None] // 64) == (i[None, :] // 64)).astype(np.float32)
    c["hm32"] = ((i[:, None] // 32) == np.arange(4)[None, :]).astype(np.float32)
    c["cm32"] = np.tile(((np.arange(128)[None, :] // 32) == np.arange(4)[:, None]).astype(np.float32).reshape(1, 512), (128, 1))
    return c


class K:
    def __init__(self, nc, st, shapes):
        self.nc = nc
        self.st = st
        self.P = Prog(nc, st)
        P = self.P
        din = lambda n, s: nc.dram_tensor(n, list(s), F32, kind="ExternalInput").ap()
        dout = lambda n, s: nc.dram_tensor(n, list(s), F32, kind="ExternalOutput").ap()
        self.xin = din("xin", [T, D])
        self.pin = din("pin", [2, T, PLE])
        self.w = {n: din(n, shapes[n]) for n in WNAMES}
        self.cst = {n: din("c_" + n, a.shape) for n, a in host_consts().items()}
        self.yout = dout("yout", [T, D])
        dscr = lambda n, s: nc.dram_tensor(n, list(s), BF16, kind="Internal").ap()
        self.s_gu = {}
        self.s_d = {}
        for f in (1, 2):
            gu = dscr("s_gu_%d" % f, [NJ, 128, 2, 8, 128])
            dd = dscr("s_d_%d" % f, [NJ, 128, D])
            for l in range(2):
                self.s_gu[l, f] = gu
                self.s_d[l, f] = dd
        pg = dscr("s_pg", [D, D])
        pp = dscr("s_pp", [PLE, D])
        self.s_pg = [pg, pg]
        self.s_pp = [pp, pp]
        sb = lambda name, shape, dt=F32: st.enter_context(nc.sbuf_tensor(name, list(shape), dt))
        ps = lambda name: st.enter_context(nc.psum_tensor(name, [128, 512], F32))
        self.sb = sb
        self.xT = sb("xT", [128, 8, T])
        self.hT = [sb("hT%d" % i, [128, 8, 512], BF16) for i in range(2)]
        self.ident = sb("ident", [128, 128])
        self.identb = sb("identb", [128, 128], BF16)
        self.onesb = sb("onesb", [128, 128], BF16)
        self.gam = sb("gam", [128, 11, 8])
        self.eps = sb("eps", [128, 1])
        self.sq = [sb("sq%d" % i, [128, 512], BF16) for i in range(2)]
        self.rstd = sb("rstd", [128, 512])
        self.cT = {n: sb("k_" + n, [128, 128]) for n in ("triu", "negmask", "mstrict", "blk64")}
        self.hm32 = sb("hm32", [128, 4])
        self.ones32 = sb("ones32", [128, 128])
        self.AW = 30192
        self.arena = sb("arena", [128, self.AW])
        self.aoff = 0
        A = self.carve
        self.act = A([128, NJ, 512], BF16)
        self.wd = A([128, NJ, D], BF16)
        self.wgu = [A([128, 2, 8, 128], BF16) for i in range(3)]
        self.sg = [A([128, 512], F32) for i in range(2)]
        self.pT = A([128, 2, 512], BF16)
        self.ptm = [A([128, PLE], F32) for i in range(2)]
        self.tin = [A([128, D], F32) for i in range(2)]
        self.ffn_end = self.aoff
        self.bank = [ps("bank%d" % i) for i in range(8)]
        self.sti = {"shift": din("st_shift", [2, 16, 896]), "wkv": din("st_wkv", [2, 16, 4, 64, 64]),
                   "gla": din("st_gla", [2, 16, 4, 32, 64]), "conv": din("st_conv", [2, 16, 3, 768]),
                   "ssm": din("st_ssm", [2, 16, 8, 64, 64])}
        self.o_p = {"shift": dout("o_p_shift", [2, 896]), "wkv": dout("o_p_wkv", [2, 4, 64, 64]),
                    "gla": dout("o_p_gla", [2, 4, 32, 64]), "conv": dout("o_p_conv", [2, 3, 768]),
                    "ssm": dout("o_p_ssm", [2, 8, 64, 64])}
        self.o_s = {"shift": dout("o_s_shift", [2, 16, 896]), "wkv": dout("o_s_wkv", [2, 16, 4, 64, 64]),
                    "gla": dout("o_s_gla", [2, 16, 4, 32, 64]), "conv": dout("o_s_conv", [2, 16, 3, 768]),
                    "ssm": dout("o_s_ssm", [2, 16, 8, 64, 64])}
        self.s_win = dscr("s_win", [D, 2968])
        self.s_wout = dscr("s_wout", [D, D])
        dsc32 = lambda n, s: nc.dram_tensor(n, list(s), F32, kind="Internal").ap()
        self.sc = {n: dsc32("sc_" + n, [128, 256]) for n in ("x", "B", "C", "y")}
        for nme in ("ra", "rw", "rb", "rk", "rr"):
            self.sc[nme] = dsc32("sc_" + nme, [128, 256])
        self.sc["rv"] = dsc32("sc_rv", [128, 128])
        self.sc["ry"] = dsc32("sc_ry", [128, 128])
        for nme, wdt_ in (("gq", 128), ("gk", 128), ("ge", 128), ("gv", 256), ("go", 256)):
            self.sc[nme] = dsc32("sc_" + nme, [64, wdt_])
        self.sc["dt"] = dsc32("sc_dt", [128, 4])
        self.sc["dA"] = dsc32("sc_dA", [128, 4])
        self.aoff = 0
        self.mwbuf = [A([128, 8, 128], BF16) for i in range(3)]
        self.yT = A([128, 8, 512], BF16)
        self.sm = A([128, 64], F32)
        self.stage = A([128, 512], F32)
        self.smp_tm = A([128, 768], F32)
        self.wrhs = A([128, 8, 512], BF16)
        self.wdt = A([128, 8, 8], BF16)
        self.ST = A([128, 4, 64], F32)
        self.STb = A([128, 4, 64], BF16)
        self.hist_ssd = A([128, 6, 3], F32)
        self.cw = A([128, 6, 4], F32)
        self.cb = A([128, 6], F32)
        self.ncb = A([128, 6], F32)
        self.dtb_bc = A([128, 8], F32)
        self.A_bc = A([128, 8], F32)
        self.D_bc = A([128, 8], F32)
        self.mnorm_bc = A([128, 512], F32)
        self.wrhs_g = A([128, 8, 512], BF16)
        self.gw2p = A([128, 128], F32)
        self.gb_bc = A([128, 128], F32)
        self.gnorm_bc = A([128, 256], F32)
        self.Sg = A([128, 64], F32)
        self.Sgb = A([128, 64], BF16)
        self.cm32 = A([128, 4, 128], F32)
        self.rw_mark = self.aoff
        self.rwkv_persist()
        self.scratch_mark = self.aoff
        self.zgs = A([128, 512], F32)
        self.ezg = A([128, 512], F32)
        self.ysb = A([128, 512], F32)
        self.ytmp = A([128, 512], F32)
        self.xtm = A([128, 512], BF16)
        self.xdt = A([128, 512], BF16)
        self.xD = A([128, 512], BF16)
        self.Btm = A([128, 128], F32)
        self.BCb = A([128, 2, 512], BF16)
        self.BCm = A([128, 4, 512], BF16)
        self.xbh = A([128, 4, 64], F32)
        self.Bbh = A([128, 4, 64], F32)
        self.Cbh = A([128, 4, 64], F32)
        self.ybh = A([128, 4, 64], F32)
        self.dtbh = A([128, 4], F32)
        self.dAbh = A([128, 4], F32)
        self.ustart = self.aoff
        self.raw = A([128, 7, 515], F32)
        self.proc = A([128, 7, 512], F32)
        self.E = A([128, 8, 128], F32)
        self.Bwp = A([128, 8, 128], BF16)
        self.scT = A([128, 8, 128], BF16)
        uend = self.aoff
        self.aoff = self.ustart
        self.raw_s = A([128, 7, 112], F32)
        self.proc_s = A([128, 7, 64], F32)
        self.Ssm = A([128, 64, 64], F32)
        self.tmpS = A([128, 64, 64], F32)
        ssd_end = max(self.aoff, uend)
        self.aoff = self.scratch_mark
        self.gla_carve()
        gla_end = self.aoff
        self.aoff = self.scratch_mark
        self.rwkv_carve()
        self.aoff = max(self.aoff, gla_end, ssd_end)
        self.uid = 0

    def u(self):
        self.uid += 1
        return self.uid

    def carve(self, shape, dt):
        n = 1
        for d in shape[1:]:
            n *= d
        words = (n * (4 if dt == F32 else 2) + 3) // 4
        off = self.aoff
        assert off + words <= self.AW, ("arena overflow", off, words, self.AW)
        self.aoff = off + words
        ap = self.arena[0:shape[0], off:off + words]
        if dt != F32:
            ap = ap.bitcast(dt)
        ap = ap[:, 0:n]
        if len(shape) == 3:
            ap = ap.rearrange("p (a b) -> p a b", a=shape[1])
        elif len(shape) == 4:
            ap = ap.rearrange("p (a b c) -> p a b c", a=shape[1], b=shape[2])
        return ap

    def setup(self):
        P = self.P
        P.dma("sp", self.ident[:], self.cst["ident"], writes=["ident"])
        P.op("act", lambda e: e.copy(out=self.identb[:], in_=self.ident[:]), reads=["ident"], writes=["identb"])
        P.op("pool", lambda e: e.memset(self.onesb[:], 1.0), writes=["onesb"])
        P.op("pool", lambda e: e.memset(self.ones32[:], 1.0), writes=["ones32"])
        for n in self.cT:
            P.dma("sp", self.cT[n][:], self.cst[n], writes=["k_" + n])
        P.dma("sp", self.hm32[:], self.cst["hm32"], writes=["hm32"])
        P.op("pool", lambda e: e.memset(self.eps[:], EPS), writes=["eps"])
        names = []
        for l in range(2):
            names += [("ffn1_norm", l), ("mix_norm", l), ("ffn2_norm", l), ("ple_norm", l)]
        names.append(("final_norm", None))
        self.gidx = {}
        for i, (n, l) in enumerate(names):
            src = self.w[n][l] if l is not None else self.w[n]
            self.gidx[n, l] = i
            P.dma("sp", self.gam[:, i, :], src.rearrange("(c p) -> p c", p=128), writes=[("gam", i)],
                  allow_slow_non_contiguous=True)

    def convert_ffn(self, l, f):
        P = self.P
        pre = "ffn%d_" % f
        wg = self.w[pre + "w_gate"][l]
        wu = self.w[pre + "w_up"][l]
        wdn = self.w[pre + "w_down"][l]
        for j in range(NJ):
            for gi, wsrc in enumerate((wg, wu)):
                P.dma("pool", self.s_gu[l, f][j, :, gi, :, :],
                      wsrc[:, j * 128:(j + 1) * 128].rearrange("(c p) m -> p c m", p=128),
                      writes=[("s_gu", f, j, gi)])
        for j0 in range(0, NJ, 2):
            P.dma("pool", self.s_d[l, f][j0:j0 + 2], wdn[j0 * 128:(j0 + 2) * 128, :].rearrange("(j p) n -> j p n", p=128),
                  writes=[("s_d", f, j0)])

    def convert_ple(self, l):
        P = self.P
        for c0 in range(0, 8, 2):
            P.dma("pool", self.s_pg[l][c0 * 128:(c0 + 2) * 128, :].rearrange("(j p) n -> j p n", p=128),
                  self.w["ple_w_gate"][l][c0 * 128:(c0 + 2) * 128, :].rearrange("(j p) n -> j p n", p=128),
                  writes=[("s_pg", c0)])
        P.dma("pool", self.s_pp[l].rearrange("(j p) n -> j p n", p=128),
              self.w["ple_w_proj"][l].rearrange("(j p) n -> j p n", p=128), writes=[("s_pp",)])

    def load_x(self):
        P = self.P
        nt = (T + 127) // 128
        for i in range(nt):
            t0 = i * 128
            n = min(128, T - t0)
            tin = self.tin[i % 2]
            tk = ("tin", i % 2)
            P.dma("sp", tin[0:n, :], self.xin[t0:t0 + n, :], writes=[tk])
            for half in range(2):
                bk = 6 + half
                bank = self.bank[bk]

                def tr(e, half=half, bank=bank, tin=tin, n=n):
                    r = None
                    for c4 in range(4):
                        c = half * 4 + c4
                        r = e.transpose(out=bank[:, c4 * 128:c4 * 128 + n], in_=tin[0:n, c * 128:(c + 1) * 128],
                                        identity=self.ident[0:n, 0:n])
                    return r
                P.op("pe", tr, reads=[tk, "ident"], writes=[("bank", bk)])
                eng = "dve" if half == 0 else "act"
                if eng == "dve":
                    fn = lambda e, half=half, bank=bank, t0=t0, n=n: e.tensor_copy(
                        out=self.xT[:, half * 4:half * 4 + 4, t0:t0 + n],
                        in_=bank[:].rearrange("p (c t) -> p c t", c=4)[:, :, 0:n])
                else:
                    fn = lambda e, half=half, bank=bank, t0=t0, n=n: e.copy(
                        out=self.xT[:, half * 4:half * 4 + 4, t0:t0 + n],
                        in_=bank[:].rearrange("p (c t) -> p c t", c=4)[:, :, 0:n])
                P.op(eng, fn, reads=[("bank", bk)], writes=[("xT", c, i) for c in range(half * 4, half * 4 + 4)])

    def xkeys(self, c, t0, n):
        return [("xT", c, i) for i in range(t0 // 128, (t0 + n + 127) // 128)]

    def norm(self, ti, gi, hbuf, out_f32=None):
        P = self.P
        t0, n = self.tiles[ti]
        hT = self.hT[hbuf]
        SS = 5
        ss = self.bank[SS]
        for c in range(8):
            sq = self.sq[c % 2]
            sk = ("sq", c % 2)
            if c % 4 != 3:
                P.op("act", lambda e, sq=sq, c=c: e.activation(out=sq[:, 0:n], in_=self.xT[:, c, t0:t0 + n], func=AF.Square),
                     reads=self.xkeys(c, t0, n), writes=[sk])
            else:
                P.op("dve", lambda e, sq=sq, c=c: e.tensor_tensor(out=sq[:, 0:n], in0=self.xT[:, c, t0:t0 + n],
                                                                    in1=self.xT[:, c, t0:t0 + n], op=ALU.mult),
                     reads=self.xkeys(c, t0, n), writes=[sk])
            P.op("pe", lambda e, sq=sq, c=c: e.matmul(ss[:, 0:n], lhsT=self.onesb[:], rhs=sq[:, 0:n], start=(c == 0), stop=(c == 7)),
                 reads=[sk, "onesb"], writes=[("bank", SS)])
        P.op("act", lambda e: e.activation(out=self.rstd[:, 0:n], in_=ss[:, 0:n], func=AF.Ln, bias=self.eps[:], scale=1.0 / D),
             reads=[("bank", SS), "eps"], writes=["rstd"])
        P.op("act", lambda e: e.activation(out=self.rstd[:, 0:n], in_=self.rstd[:, 0:n], func=AF.Exp, scale=-0.5),
             reads=["rstd"], writes=["rstd"])
        for c in range(8):
            eng = "dve"
            if out_f32 is None:
                dst = hT[:, c, 0:n]
                wk = [("hT", hbuf, c)]
            else:
                dst = out_f32[:, c, 0:n]
                wk = [("of32", c)]
            P.op(eng, lambda e, c=c, dst=dst: e.scalar_tensor_tensor(
                out=dst, in0=self.xT[:, c, t0:t0 + n], scalar=self.gam[:, gi, c:c + 1], in1=self.rstd[:, 0:n],
                op0=ALU.mult, op1=ALU.mult),
                reads=self.xkeys(c, t0, n) + ["rstd", ("gam", gi)], writes=wk)

    def ffn_phase(self, l, f):
        P = self.P
        self.tiles = FT
        gi = self.gidx["ffn%d_norm" % f, l]
        for j0 in range(0, NJ, 2):
            P.dma("sp", self.wd[:, j0:j0 + 2, :], self.s_d[l, f][j0:j0 + 2].rearrange("j p n -> p j n"),
                  reads=[("s_d", f, j0)], writes=[("wd", j0), ("wd", j0 + 1)])
        self.norm(0, gi, 0)
        for ti in range(len(FT)):
            if ti + 1 < len(FT):
                self.norm(ti + 1, gi, (ti + 1) % 2)
            self.ffn_tile(l, f, ti, ti % 2)

    def ffn_tile(self, l, f, ti, hbuf):
        P = self.P
        t0, n = self.tiles[ti]
        hT = self.hT[hbuf]
        hk = [("hT", hbuf, c) for c in range(8)]
        for j in range(NJ):
            wb = self.wgu[self.wcount % 3]
            wk = ("wgu", self.wcount % 3)
            self.wcount += 1
            P.dma("sp", wb[:], self.s_gu[l, f][j], reads=[("s_gu", f, j, 0), ("s_gu", f, j, 1)], writes=[wk])
            gb = j % 2
            G = self.bank[gb]
            U = self.bank[2 + gb]

            def mm(e, wb=wb, gi_=0, dst=G):
                r = None
                for c in range(8):
                    r = e.matmul(dst[:, 0:n], lhsT=wb[:, gi_, c, :], rhs=hT[:, c, 0:n], start=(c == 0), stop=(c == 7))
                return r
            dmm = 0.3 + 8 * n / 2400.0
            P.op("pe", lambda e, wb=wb, G=G: mm(e, wb, 0, G), reads=[wk] + hk, writes=[("bank", gb)], dur=dmm)
            P.op("pe", lambda e, wb=wb, U=U: mm(e, wb, 1, U), reads=[wk] + hk, writes=[("bank", 2 + gb)], dur=dmm)
            sg = self.sg[gb]
            P.op("act", lambda e, sg=sg, G=G: e.activation(out=sg[:, 0:n], in_=G[:, 0:n], func=AF.Silu),
                 reads=[("bank", gb)], writes=[("sg", gb)])
            P.op("dve", lambda e, sg=sg, U=U, j=j: e.tensor_tensor(out=self.act[:, j, 0:n], in0=sg[:, 0:n], in1=U[:, 0:n], op=ALU.mult),
                 reads=[("sg", gb), ("bank", 2 + gb)], writes=[("act", j)])
        for c in range(8):
            yb = 6 + c % 2
            Y = self.bank[yb]

            def dn(e, c=c, Y=Y):
                r = None
                for j in range(NJ):
                    r = e.matmul(Y[:, 0:n], lhsT=self.wd[:, j, c * 128:(c + 1) * 128], rhs=self.act[:, j, 0:n],
                                 start=(j == 0), stop=(j == NJ - 1))
                return r
            P.op("pe", dn, reads=[("wd", j) for j in range(NJ)] + [("act", j) for j in range(NJ)], writes=[("bank", yb)], dur=0.3 + NJ * n / 2400.0)
            xk = self.xkeys(c, t0, n)
            P.op("dve", lambda e, c=c, Y=Y: e.scalar_tensor_tensor(
                out=self.xT[:, c, t0:t0 + n], in0=Y[:, 0:n], scalar=0.5, in1=self.xT[:, c, t0:t0 + n],
                op0=ALU.mult, op1=ALU.add), reads=[("bank", yb)] + xk, writes=xk)

    def ple_phase(self, l):
        P = self.P
        self.tiles = FT
        gi = self.gidx["ple_norm", l]
        wpg = self.wd[:, 0:8, :]
        wpp = self.wd[:, 8:10, :]
        for c0 in range(0, 8, 2):
            P.dma("sp", wpg[:, c0:c0 + 2, :], self.s_pg[l][c0 * 128:(c0 + 2) * 128, :].rearrange("(c p) n -> p c n", p=128),
                  reads=[("s_pg", c0)], writes=[("wd", c0), ("wd", c0 + 1)])
        P.dma("sp", wpp, self.s_pp[l].rearrange("(c p) n -> p c n", p=128), reads=[("s_pp",)], writes=[("wd", 8), ("wd", 9)])
        self.norm(0, gi, 0)
        for ti in range(len(FT)):
            if ti + 1 < len(FT):
                self.norm(ti + 1, gi, (ti + 1) % 2)
            self.ple_tile(l, ti, wpg, wpp)

    def ple_tile(self, l, ti, wpg, wpp):
        P = self.P
        if True:
            t0, n = self.tiles[ti]
            hbuf = ti % 2
            hT = self.hT[hbuf]
            hk = [("hT", hbuf, c) for c in range(8)]
            for s in range((n + 127) // 128):
                m = min(128, n - s * 128)
                ptm = self.ptm[s % 2]
                P.dma("sp", ptm[0:m, :], self.pin[l, t0 + s * 128:t0 + s * 128 + m, :], writes=[("ptm", s % 2)])
                TB = 4
                tb = self.bank[TB]

                def tr(e, ptm=ptm, m=m, tb=tb):
                    r = None
                    for k in range(2):
                        r = e.transpose(out=tb[:, k * 128:k * 128 + m], in_=ptm[0:m, k * 128:(k + 1) * 128], identity=self.ident[0:m, 0:m])
                    return r
                P.op("pe", tr, reads=[("ptm", s % 2), "ident"], writes=[("bank", TB)])
                P.op("act", lambda e, s=s, m=m, tb=tb: e.copy(out=self.pT[:, :, s * 128:s * 128 + m],
                                                             in_=tb[:, 0:256].rearrange("p (k t) -> p k t", k=2)[:, :, 0:m]),
                     reads=[("bank", TB)], writes=[("pT", s)])
            pk = [("pT", s) for s in range((n + 127) // 128)]
            for dc in range(8):
                gb = dc % 2
                G = self.bank[gb]
                Pp = self.bank[2 + gb]

                def gm(e, dc=dc, G=G):
                    r = None
                    for c in range(8):
                        r = e.matmul(G[:, 0:n], lhsT=wpg[:, c, dc * 128:(dc + 1) * 128], rhs=hT[:, c, 0:n], start=(c == 0), stop=(c == 7))
                    return r
                P.op("pe", gm, reads=hk + [("wd", c) for c in range(8)], writes=[("bank", gb)])

                def pm(e, dc=dc, Pp=Pp):
                    r = None
                    for k in range(2):
                        r = e.matmul(Pp[:, 0:n], lhsT=wpp[:, k, dc * 128:(dc + 1) * 128], rhs=self.pT[:, k, 0:n], start=(k == 0), stop=(k == 1))
                    return r
                P.op("pe", pm, reads=pk + [("wd", 8), ("wd", 9)], writes=[("bank", 2 + gb)])
                sg = self.sg[gb]
                P.op("act", lambda e, sg=sg, G=G: e.activation(out=sg[:, 0:n], in_=G[:, 0:n], func=AF.Sigmoid),
                     reads=[("bank", gb)], writes=[("sg", gb)], dur=0.8)
                P.op("dve", lambda e, sg=sg, Pp=Pp: e.tensor_tensor(out=sg[:, 0:n], in0=sg[:, 0:n], in1=Pp[:, 0:n], op=ALU.mult),
                     reads=[("sg", gb), ("bank", 2 + gb)], writes=[("sg", gb)])
                xk = self.xkeys(dc, t0, n)
                import os
                kple = ""
                if kple == "proj":
                    P.op("dve", lambda e, Pp=Pp, dc=dc: e.tensor_copy(out=self.xT[:, dc, t0:t0 + n], in_=Pp[:, 0:n]),
                         reads=[("bank", 2 + gb), ("sg", gb)] + xk, writes=xk)
                    continue
                P.op("dve", lambda e, sg=sg, dc=dc: e.tensor_tensor(out=self.xT[:, dc, t0:t0 + n], in0=self.xT[:, dc, t0:t0 + n],
                                                                      in1=sg[:, 0:n], op=ALU.add),
                     reads=[("sg", gb)] + xk, writes=xk)

    def final(self):
        P = self.P
        self.tiles = FT
        gi = self.gidx["final_norm", None]
        of = self.act[:].rearrange("p j n -> p (j n)")[:, 0:8192].bitcast(F32).rearrange("p (c n) -> p c n", c=8)
        for ti in range(len(FT)):
            t0, n = FT[ti]
            self.norm(ti, gi, 0, out_f32=of)
            for s in range((n + 127) // 128):
                m = min(128, n - s * 128)
                ob = self.tin[s % 2]
                ok = ("tin", s % 2)
                for half in range(2):
                    bk = 6 + half
                    bank = self.bank[bk]

                    def tr(e, half=half, bank=bank, s=s, m=m):
                        r = None
                        for c4 in range(4):
                            c = half * 4 + c4
                            r = e.transpose(out=bank[0:m, c4 * 128:(c4 + 1) * 128], in_=of[:, c, s * 128:s * 128 + m],
                                            identity=self.ident[:])
                        return r
                    P.op("pe", tr, reads=[("of32", c) for c in range(half * 4, half * 4 + 4)] + ["ident"], writes=[("bank", bk)])
                    if half == 0:
                        P.op("dve", lambda e, ob=ob, bank=bank, m=m: e.tensor_copy(out=ob[0:m, 0:512], in_=bank[0:m, :]),
                             reads=[("bank", bk)], writes=[ok])
                    else:
                        P.op("act", lambda e, ob=ob, bank=bank, m=m: e.copy(out=ob[0:m, 512:1024], in_=bank[0:m, :]),
                             reads=[("bank", bk)], writes=[ok])
                P.dma("sp", self.yout[t0 + s * 128:t0 + s * 128 + m, :], ob[0:m, :], reads=[ok], writes=[("yout", ti, s)])

    def rwkv_persist(self):
        A = self.carve
        r = {}
        r["mu7"] = A([128, 7], F32)
        for nme in ("w0n", "a0n", "kk_", "ka", "oma", "rk", "hm64"):
            r[nme] = A([128, 2], F32)
        r["lsc"] = A([128, 1], F32)
        r["w2p"] = A([128, 256], F32)
        r["a2p"] = A([128, 256], F32)
        r["g2p"] = A([128, 256], F32)
        r["lnw_bc"] = A([128, 256], F32)
        r["lnb_bc"] = A([128, 256], F32)
        r["Sf"] = A([128, 2, 128], F32)
        r["Sb"] = A([128, 2, 128], BF16)
        r["hist"] = A([128, 7, 1], F32)
        r["tiny"] = A([128, 1], F32)
        r["gneps"] = A([128, 1], F32)
        r["lstrict"] = A([128, 128], F32)
        self.r = r

    def rwkv_carve(self):
        A = self.carve
        r = self.r
        u0 = self.aoff
        r["raw"] = A([128, 7, 513], F32)
        r["zs"] = A([128, 7, 512], F32)
        u1 = self.aoff
        self.aoff = u0
        r["raw_s"] = A([128, 7, 80], F32)
        r["zs_s"] = A([128, 7, 64], F32)
        r["Ss"] = A([128, 32, 64], F32)
        r["tS"] = A([128, 32, 64], F32)
        for nme in ("va", "vw", "vb", "vk", "vr"):
            r[nme] = A([128, 4, 64], F32)
        r["vv"] = A([128, 4, 32], F32)
        r["vy"] = A([128, 4, 32], F32)
        r["sa"] = A([128, 32], F32)
        assert self.aoff <= u1, (self.aoff, u1)
        self.aoff = u1
        r["lact"] = A([128, 512], F32)
        for nme in ("sgw", "al", "kk", "kp", "bq", "cs", "Ep", "Em", "Epv", "At", "Rt", "Bh", "Kh", "rkr", "T1"):
            r[nme] = A([128, 2, 128], F32)
        for nme in ("Atb", "Rtb", "Bhb"):
            r[nme] = A([128, 2, 128], BF16)
        r["Atb_1"] = A([128, 2, 128], BF16)
        r["Rtb_1"] = A([128, 2, 128], BF16)
        r["Ep_1"] = A([128, 2, 128], F32)
        for nme in ("Atm", "Bhm", "Khm"):
            r[nme] = A([128, 4, 128], BF16)
        for nme in ("Vtb", "Bcb", "Kcb", "RHSb", "Ub"):
            r[nme] = A([128, 256], BF16)
        for nme in ("vtf", "Ysb", "ysq", "ytmp", "gsb"):
            r[nme] = A([128, 256], F32)
        for nme in ("A", "M", "MakT", "NrbT", "NrkT", "Z"):
            r[nme] = A([128, 4, 128], BF16)
        r["stg"] = A([128, 4, 128], F32)
        r["bon"] = A([128, 4], F32)

    def gla_carve(self):
        A = self.carve
        g = {}
        g["qT"] = A([128, 512], F32)
        g["kT"] = A([128, 512], F32)
        g["glT"] = A([128, 512], F32)
        g["la"] = A([128, 128], F32)
        g["eb"] = A([128, 128], F32)
        g["enb"] = A([128, 128], F32)
        g["ktf"] = A([128, 128], F32)
        g["qtb"] = A([128, 128], BF16)
        g["km"] = A([128, 4, 128], BF16)
        g["qm"] = A([128, 4, 128], BF16)
        g["ktm"] = A([128, 128], BF16)
        g["ktmm"] = A([128, 4, 128], BF16)
        g["attf"] = A([128, 4, 128], F32)
        g["attm"] = A([128, 4, 128], BF16)
        g["vtm"] = A([128, 256], BF16)
        g["vtf"] = A([128, 256], F32)
        g["gs"] = A([128, 256], F32)
        g["gtmp"] = A([128, 256], F32)
        g["osb"] = A([128, 256], F32)
        g["osq"] = A([128, 256], F32)
        g["t1"] = A([128, 64], F32)
        g["qtm"] = A([128, 128], F32)
        g["ktm32"] = A([128, 128], F32)
        g["ela"] = A([128, 128], F32)
        g["Ss"] = A([128, 32, 64], F32)
        g["tS"] = A([128, 32, 64], F32)
        g["qbh"] = A([128, 4, 32], F32)
        g["kbh"] = A([128, 4, 32], F32)
        g["ebh"] = A([128, 4, 32], F32)
        g["vbh"] = A([128, 4, 64], F32)
        g["obh"] = A([128, 4, 64], F32)
        self.g = g

    DBK = ("Atb", "Rtb", "Ep")

    def _mk(self, k):
        par = getattr(self, "kpar", None)
        if par is None:
            return k
        if isinstance(k, str) and k in self.DBK:
            return k + "#%d" % par
        if isinstance(k, tuple) and isinstance(k[0], str) and k[0] in self.DBK:
            return (k[0] + "#%d" % par,) + tuple(k[1:])
        return k

    def OP(self, eng, fn, r=(), w=()):
        return self.P.op(eng, fn, reads=[self._mk(k) for k in r], writes=[self._mk(k) for k in w])

    def convert_mix(self, l):
        P = self.P
        for c in range(8):
            P.dma("pool", self.s_win[c * 128:(c + 1) * 128, :], self.w["w_in"][l][c * 128:(c + 1) * 128, :], writes=[("s_win", c)])
        for c0 in range(0, 8, 2):
            P.dma("pool", self.s_wout[c0 * 128:(c0 + 2) * 128, :].rearrange("(j p) n -> j p n", p=128),
                  self.w["w_out"][l][c0 * 128:(c0 + 2) * 128, :].rearrange("(j p) n -> j p n", p=128), writes=[("s_wout", c0)])

    def bcast_load(self, dst, src1d, key):
        self.P.dma("sp", dst, src1d.partition_broadcast(128), writes=[key])

    def load_wchunk(self, col0, width):
        slot = self.wcount % 3
        self.wcount += 1
        wb = self.mwbuf[slot]
        wk = ("mwbuf", slot)
        self.P.dma("sp", wb[:, :, 0:width], self.s_win.rearrange("(c p) n -> p c n", p=128)[:, :, col0:col0 + width],
                   reads=[("s_win", c) for c in range(8)], writes=[wk])
        return wb, wk

    def inproj_fm(self, ti, col0, width, dst, dkeys, eng="act"):
        t0, n = TILES[ti]
        hb = ti % 2
        hT = self.hT[hb]
        hk = [("hT", hb, c) for c in range(8)]
        wb, wk = self.load_wchunk(col0, width)
        bi = self.bcount % 2
        self.bcount += 1
        bank = self.bank[bi]

        def mm(e):
            r = None
            for c in range(8):
                r = e.matmul(bank[0:width, 0:n], lhsT=wb[:, c, 0:width], rhs=hT[:, c, 0:n], start=(c == 0), stop=(c == 7))
            return r
        self.OP("pe", mm, [wk] + hk, [("bank", bi)])
        if eng == "act":
            self.OP("act", lambda e: e.copy(out=dst, in_=bank[0:width, 0:n]), [("bank", bi)], dkeys)
        else:
            self.OP("dve", lambda e: e.tensor_copy(out=dst, in_=bank[0:width, 0:n]), [("bank", bi)], dkeys)

    def load_wrhs(self, segs):
        for (col0, w, off) in segs:
            self.P.dma("sp", self.wrhs[:, :, off:off + w], self.s_win.rearrange("(c p) n -> p c n", p=128)[:, :, col0:col0 + w],
                       reads=[("s_win", c) for c in range(8)], writes=[("wrhs", off)])

    def mixer_phase(self, l):
        P = self.P
        mix = self.mix
        self.tiles = TILES
        P.barrier()
        import os
        P.sched = True
        self.OP("dve", lambda e: e.memset(self.yT[:], 0.0), [], [("yT", c) for c in range(8)])
        if "ssd" in mix:
            self.ssd_setup(l)
        if "gla" in mix:
            self.gla_setup(l)
        if "rwkv" in mix:
            self.rwkv_setup(l)
        gi = self.gidx["mix_norm", l]
        self.norm(0, gi, 0)
        for ti in range(len(TILES)):
            if ti + 1 < len(TILES):
                self.norm(ti + 1, gi, (ti + 1) % 2)
            if ti == len(TILES) - 1:
                P.barrier()
            import os
            kssd = ""
            if "ssd" in mix and kssd != "setup":
                if ti < 4:
                    self.ssd_tile(l, ti)
                elif kssd == "":
                    self.ssd_sample(l)
                P.barrier()
            if "gla" in mix:
                if ti < 4:
                    self.gla_tile(l, ti)
                else:
                    self.gla_sample(l)
                P.barrier()
            if "rwkv" in mix:
                if ti < 4:
                    self.rwkv_tile(l, ti)
                else:
                    self.rwkv_sample(l)
                P.barrier()
            self.outproj(ti)
        P.barrier()
        P.sched = self.gsched

    def outproj(self, ti):
        t0, n = TILES[ti]
        yk = [("yT", c) for c in range(8)]
        for dc in range(8):
            slot = self.wcount % 3
            self.wcount += 1
            wb = self.mwbuf[slot]
            wk = ("mwbuf", slot)
            self.P.dma("sp", wb[:], self.s_wout.rearrange("(c p) n -> p c n", p=128)[:, :, dc * 128:(dc + 1) * 128],
                       reads=[("s_wout", c0) for c0 in range(0, 8, 2)], writes=[wk])
            bi = 6 + dc % 2
            bank = self.bank[bi]

            def mm(e, wb=wb, bank=bank):
                r = None
                for c in range(8):
                    r = e.matmul(bank[:, 0:n], lhsT=wb[:, c, :], rhs=self.yT[:, c, 0:n], start=(c == 0), stop=(c == 7))
                return r
            self.OP("pe", mm, yk + [wk], [("bank", bi)])
            xk = self.xkeys(dc, t0, n)
            self.OP("dve", lambda e, dc=dc, bank=bank: e.tensor_tensor(out=self.xT[:, dc, t0:t0 + n], in0=self.xT[:, dc, t0:t0 + n],
                                                                     in1=bank[:, 0:n], op=ALU.add), [("bank", bi)] + xk, xk)

    M2 = 1680
    def ssd_setup(self, l):
        P = self.P
        w = self.w
        for j in range(4):
            P.dma("sp", self.cw[:, :, j], w["mamba_conv_w"][l, j].rearrange("(q p) -> p q", p=128), writes=["cw"], allow_slow_non_contiguous=True)
        P.dma("sp", self.cb[:], w["mamba_conv_b"][l].rearrange("(q p) -> p q", p=128), writes=["cb"], allow_slow_non_contiguous=True)
        self.OP("dve", lambda e: e.tensor_scalar_mul(out=self.ncb[:], in0=self.cb[:], scalar1=-1.0), ["cb"], ["ncb"])
        self.bcast_load(self.dtb_bc[:], w["mamba_dt_bias"][l], "dtb_bc")
        self.bcast_load(self.A_bc[:], w["mamba_A_log"][l], "A_bc")
        self.OP("act", lambda e: e.activation(out=self.A_bc[:], in_=self.A_bc[:], func=AF.Exp), ["A_bc"], ["A_bc"])
        self.OP("dve", lambda e: e.tensor_scalar_mul(out=self.A_bc[:], in0=self.A_bc[:], scalar1=-1.0), ["A_bc"], ["A_bc"])
        self.bcast_load(self.D_bc[:], w["mamba_D"][l], "D_bc")
        self.bcast_load(self.mnorm_bc[:], w["mamba_norm"][l], "mnorm_bc")
        self.OP("dve", lambda e: e.memset(self.hist_ssd[:], 0.0), [], ["hist_ssd"])
        self.OP("dve", lambda e: e.memset(self.ST[:], 0.0), [], ["ST"])
        self.OP("dve", lambda e: e.memset(self.STb[:], 0.0), [], ["STb"])
        self.OP("dve", lambda e: e.memset(self.Bwp[:], 0.0), [], ["Bwp"])
        self.load_wrhs([(self.M2, 512, 0)])
        P.dma("sp", self.wdt[:], self.s_win.rearrange("(c p) n -> p c n", p=128)[:, :, self.M2 + 1280:self.M2 + 1288],
              reads=[("s_win", c) for c in range(8)], writes=["wdt"])

    def silu_into(self, dst, src, tmp, rk, wk, tk, np_=128):
        self.OP("act", lambda e: e.activation(out=tmp, in_=src, func=AF.Exp, scale=-1.0), rk, [tk])
        self.OP("act", lambda e: e.activation(out=tmp, in_=tmp, func=AF.Ln, bias=1.0), [tk], [tk])
        self.OP("act", lambda e: e.activation(out=tmp, in_=tmp, func=AF.Exp, scale=-1.0), [tk], [tk])
        self.OP("dve", lambda e: e.tensor_tensor(out=dst, in0=src, in1=tmp, op=ALU.mult), list(rk) + [tk], [wk])

    def ssd_conv(self, raw, xc, n, step):
        for q in range(6):
            rk = [("raw", q)]
            ok = ("xc", q)
            dst = xc[:, q, 0:n]
            self.OP("dve", lambda e, q=q, dst=dst: e.tensor_scalar_mul(out=dst, in0=raw[:, q, 0:n], scalar1=self.cw[:, q, 0:1]), rk + ["cw"], [ok])
            for j in range(1, 4):
                self.OP("dve", lambda e, q=q, j=j, dst=dst: e.scalar_tensor_tensor(
                    out=dst, in0=raw[:, q, j * step:j * step + n], scalar=self.cw[:, q, j:j + 1], in1=dst, op0=ALU.mult, op1=ALU.add),
                    rk + ["cw", ok], [ok])
            tmp = self.ezg[:, 0:n]
            self.OP("act", lambda e, q=q, dst=dst, tmp=tmp: e.activation(out=tmp, in_=dst, func=AF.Exp, scale=-1.0, bias=self.ncb[:, q:q + 1]),
                    [ok, "ncb"], ["ezg"])
            self.OP("act", lambda e, tmp=tmp: e.activation(out=tmp, in_=tmp, func=AF.Ln, bias=1.0), ["ezg"], ["ezg"])
            self.OP("act", lambda e, tmp=tmp: e.activation(out=tmp, in_=tmp, func=AF.Exp, scale=-1.0), ["ezg"], ["ezg"])
            self.OP("dve", lambda e, q=q, dst=dst, tmp=tmp: e.scalar_tensor_tensor(
                out=dst, in0=dst, scalar=self.cb[:, q:q + 1], in1=tmp, op0=ALU.add, op1=ALU.mult), [ok, "ezg", "cb"], [ok])

    def ssd_tile(self, l, ti):
        P = self.P
        t0, n = TILES[ti]
        raw, xc = self.raw, self.proc
        M2 = self.M2
        self.OP("dve", lambda e: e.memset(self.Bwp[:], 0.0), [], ["Bwp"])
        self.OP("dve", lambda e: e.tensor_copy(out=raw[:, 0:6, 0:3], in_=self.hist_ssd[:]), ["hist_ssd"], [("raw", q) for q in range(6)])
        for q in range(6):
            self.inproj_fm(ti, M2 + 512 + q * 128, 128, raw[:, q, 3:3 + n], [("raw", q)], eng="act")
        self.OP("dve", lambda e: e.tensor_copy(out=self.hist_ssd[:], in_=raw[:, 0:6, n:n + 3]), [("raw", q) for q in range(6)], ["hist_ssd"])
        if ti == 3:
            for j in range(3):
                P.dma("sp", self.o_p["conv"][l, j].rearrange("(q p) -> p q", p=128), raw[:, 0:6, n + j], reads=[("raw", q) for q in range(6)],
                      writes=[("o_p_conv", l, j)], allow_slow_non_contiguous=True)
        import os
        kssd = ""
        if kssd == "inproj":
            return
        self.ssd_conv(raw, xc, n, 1)
        self.OP("act", lambda e: e.copy(out=self.BCb[:, :, 0:n], in_=xc[:, 4:6, 0:n]), [("xc", 4), ("xc", 5)], ["BCb"])
        for bc in range(2):
            for g in range(2):
                self.OP("act", lambda e, bc=bc, g=g: e.activation(out=self.BCm[:, bc * 2 + g, 0:n], in_=xc[:, 4 + bc, 0:n], func=AF.Copy,
                                                                  scale=self.cT["blk64"][:, g * 64:g * 64 + 1]),
                        [("xc", 4 + bc), "k_blk64"], ["BCm"])
        if kssd == "conv":
            return
        for k in range(n // 128):
            self.ssd_chunk(l, ti, k)
        if ti == 3 and kssd != "nostate":
            self.ssd_state_out(l)

    def ssd_tokmajor(self, hT, hk, c0, m):
        Z = self.bank[2]
        DT = self.bank[3]
        sm = self.sm

        def mmz(e):
            r = None
            for c in range(8):
                r = e.matmul(Z[0:m, :], lhsT=hT[:, c, c0:c0 + m], rhs=self.wrhs[:, c, :], start=(c == 0), stop=(c == 7))
            return r
        self.OP("pe", mmz, hk + [("wrhs", 0)], [("bank", 2)])

        def mmd(e):
            r = None
            for c in range(8):
                r = e.matmul(DT[0:m, 0:8], lhsT=hT[:, c, c0:c0 + m], rhs=self.wdt[:, c, :], start=(c == 0), stop=(c == 7))
            return r
        self.OP("pe", mmd, hk + ["wdt"], [("bank", 3)])
        self.silu_into(self.zgs[0:m, :], Z[0:m, :], self.ezg[0:m, :], [("bank", 2)], "zgs", "ezg")
        self.OP("dve", lambda e: e.tensor_tensor(out=sm[0:m, 0:8], in0=DT[0:m, 0:8], in1=self.dtb_bc[0:m, :], op=ALU.add), [("bank", 3), "dtb_bc"], ["sm"])
        self.OP("act", lambda e: e.activation(out=sm[0:m, 0:8], in_=sm[0:m, 0:8], func=AF.Exp), ["sm"], ["sm"])
        self.OP("act", lambda e: e.activation(out=sm[0:m, 0:8], in_=sm[0:m, 0:8], func=AF.Ln, bias=1.0), ["sm"], ["sm"])
        self.OP("dve", lambda e: e.tensor_tensor(out=sm[0:m, 8:16], in0=sm[0:m, 0:8], in1=self.A_bc[0:m, :], op=ALU.mult), ["sm", "A_bc"], ["sm"])

    def ssd_post(self, m, tokcols):
        sm = self.sm
        self.OP("dve", lambda e: e.tensor_tensor(out=self.ysb[0:m, :], in0=self.ysb[0:m, :], in1=self.zgs[0:m, :], op=ALU.mult), ["ysb", "zgs"], ["ysb"])
        self.OP("dve", lambda e: e.memset(sm[0:m, 56:57], 0.0), [], ["sm"])
        self.OP("act", lambda e: e.activation(out=self.ytmp[0:m, :], in_=self.ysb[0:m, :], func=AF.Square, accum_out=sm[0:m, 56:57]), ["ysb", "sm"], ["ytmp", "sm"])
        self.OP("act", lambda e: e.activation(out=sm[0:m, 57:58], in_=sm[0:m, 56:57], func=AF.Ln, bias=self.eps[0:m, :], scale=1.0 / 512), ["sm", "eps"], ["sm"])
        self.OP("act", lambda e: e.activation(out=sm[0:m, 57:58], in_=sm[0:m, 57:58], func=AF.Exp, scale=-0.5), ["sm"], ["sm"])
        self.OP("dve", lambda e: e.scalar_tensor_tensor(out=self.ytmp[0:m, :], in0=self.ysb[0:m, :], scalar=sm[0:m, 57:58], in1=self.mnorm_bc[0:m, :],
                                                        op0=ALU.mult, op1=ALU.mult), ["ysb", "sm", "mnorm_bc"], ["ytmp"])
        TB = self.bank[5]

        def tr(e):
            r = None
            for q in range(4):
                r = e.transpose(out=TB[:, q * 128:q * 128 + m], in_=self.ytmp[0:m, q * 128:(q + 1) * 128], identity=self.ident[0:m, 0:m])
            return r
        self.OP("pe", tr, ["ytmp", "ident"], [("bank", 5)])
        self.OP("act", lambda e: e.copy(out=self.yT[:, 4:8, tokcols], in_=TB[:].rearrange("p (q t) -> p q t", q=4)[:, :, 0:m]),
                [("bank", 5)], [("yT", c) for c in range(4, 8)])

    def ssd_chunk(self, l, ti, k):
        hb = ti % 2
        hT = self.hT[hb]
        hk = [("hT", hb, c) for c in range(8)]
        c0 = k * 128
        tok = slice(c0, c0 + 128)
        sm = self.sm
        xc = self.proc
        self.ssd_tokmajor(hT, hk, c0, 128)
        if self.kch == 1:
            return
        CB = self.bank[3]
        self.OP("pe", lambda e: e.matmul(CB[:, 16:24], lhsT=self.cT["triu"][:], rhs=sm[:, 8:16], start=True, stop=True), ["sm", "k_triu"], [("bank", 3)])
        self.OP("pe", lambda e: e.matmul(CB[:, 24:32], lhsT=self.ones32[:], rhs=sm[:, 8:16], start=True, stop=True), ["sm", "ones32"], [("bank", 3)])
        self.OP("dve", lambda e: e.tensor_copy(out=sm[:, 16:24], in_=CB[:, 16:24]), [("bank", 3)], ["sm"])
        self.OP("dve", lambda e: e.tensor_scalar_mul(out=sm[:, 24:32], in0=CB[:, 16:24], scalar1=-1.0), [("bank", 3)], ["sm"])
        self.OP("act", lambda e: e.activation(out=sm[:, 32:40], in_=CB[:, 16:24], func=AF.Exp), [("bank", 3)], ["sm"])
        self.OP("dve", lambda e: e.tensor_tensor(out=sm[:, 40:48], in0=CB[:, 24:32], in1=sm[:, 16:24], op=ALU.subtract), [("bank", 3), "sm"], ["sm"])
        self.OP("act", lambda e: e.activation(out=sm[:, 40:48], in_=sm[:, 40:48], func=AF.Exp), ["sm"], ["sm"])
        self.OP("dve", lambda e: e.tensor_tensor(out=sm[:, 40:48], in0=sm[:, 40:48], in1=sm[:, 0:8], op=ALU.mult), ["sm"], ["sm"])
        self.OP("act", lambda e: e.activation(out=sm[:, 48:56], in_=CB[:, 24:32], func=AF.Exp), [("bank", 3)], ["sm"])
        if self.kch == 2:
            return
        XB = self.bank[4]

        def trx(e):
            r = None
            for q in range(4):
                r = e.transpose(out=XB[:, q * 128:(q + 1) * 128], in_=xc[:, q, tok], identity=self.ident[:])
            return r
        self.OP("pe", trx, [("xc", q) for q in range(4)] + ["ident"], [("bank", 4)])
        self.OP("act", lambda e: e.copy(out=self.xtm[:], in_=XB[:]), [("bank", 4)], ["xtm"])
        if self.kch == 21:
            return
        xb3 = self.xtm[:].rearrange("p (h d) -> p h d", h=8)
        self.OP("dve", lambda e: e.tensor_tensor(out=self.xdt[:].rearrange("p (h d) -> p h d", h=8), in0=xb3,
                                                 in1=sm[:, 0:8].unsqueeze(2).to_broadcast([128, 8, 64]), op=ALU.mult), ["xtm", "sm"], ["xdt"])
        self.OP("dve", lambda e: e.tensor_tensor(out=self.xD[:].rearrange("p (h d) -> p h d", h=8), in0=xb3,
                                                 in1=self.D_bc[:].unsqueeze(2).to_broadcast([128, 8, 64]), op=ALU.mult), ["xtm", "D_bc"], ["xD"])
        if self.kch == 22:
            return
        BT = self.bank[5]
        self.OP("pe", lambda e: e.transpose(out=BT[:, 0:128], in_=xc[:, 4, tok], identity=self.ident[:]), [("xc", 4), "ident"], [("bank", 5)])
        self.OP("act", lambda e: e.copy(out=self.Btm[:], in_=BT[:, 0:128]), [("bank", 5)], ["Btm"])
        if self.kch == 3:
            return
        CBT = self.bank[5]
        for g in range(2):
            self.OP("pe", lambda e, g=g: e.matmul(CBT[:, 128 + g * 128:256 + g * 128], lhsT=self.BCm[:, g, tok],
                                                   rhs=self.BCb[:, 1, tok], start=True, stop=True), ["BCb", "BCm"], [("bank", 5)])
        if self.kch == 4:
            return
        for h in range(8):
            bi = 6 + h // 4
            bank = self.bank[bi]
            col = (h % 4) * 128

            def mmc(e, h=h, bank=bank, col=col):
                e.matmul(bank[:, col:col + 128], lhsT=sm[:, 8 + h:9 + h].to_broadcast([128, 128]), rhs=self.cT["triu"][:], start=True, stop=False)
                return e.matmul(bank[:, col:col + 128], lhsT=self.ident[:], rhs=self.cT["negmask"][:], start=False, stop=True)
            self.OP("pe", mmc, ["sm", "k_triu", "k_negmask", "ident"], [("bank", bi)])
            self.OP("act", lambda e, h=h, bank=bank, col=col: e.activation(out=self.E[:, h, :], in_=bank[:, col:col + 128], func=AF.Exp,
                                                                        bias=sm[:, 24 + h:25 + h]), [("bank", bi), "sm"], [("E", h)])
        self.OP("act", lambda e: e.copy(out=self.stage[:, 0:256], in_=CBT[:, 128:384]), [("bank", 5)], ["stage"])
        for g in range(2):
            self.OP("dve", lambda e, g=g: e.tensor_tensor(out=self.scT[:, g * 4:(g + 1) * 4, :], in0=self.E[:, g * 4:(g + 1) * 4, :],
                                                          in1=self.stage[:, g * 128:(g + 1) * 128].unsqueeze(1).to_broadcast([128, 4, 128]), op=ALU.mult),
                    [("E", h) for h in range(g * 4, g * 4 + 4)] + ["stage"], [("scT", g)])
        if self.kch == 5:
            return
        Y = self.bank[2]
        YI = self.bank[4]

        def mmy(e):
            e.matmul(Y[:, :], lhsT=self.identb[:], rhs=self.xD[:], start=True, stop=False)
            r = None
            for h in range(8):
                r = e.matmul(Y[:, h * 64:(h + 1) * 64], lhsT=self.scT[:, h, :], rhs=self.xdt[:, h * 64:(h + 1) * 64], start=False, stop=(h == 7))
            return r
        self.OP("pe", mmy, ["identb", "xD", "xdt", ("scT", 0), ("scT", 1)], [("bank", 2)])

        def mmi(e):
            r = None
            for h in range(8):
                g, i = h // 4, h % 4
                r = e.matmul(YI[:, h * 64:(h + 1) * 64], lhsT=self.BCm[:, 2 + g, tok], rhs=self.STb[:, i, :], start=True, stop=True)
            return r
        self.OP("pe", mmi, ["BCm", "STb"], [("bank", 4)])
        self.OP("act", lambda e: e.copy(out=self.ytmp[:], in_=YI[:]), [("bank", 4)], ["ytmp"])
        self.OP("dve", lambda e: e.tensor_tensor(out=self.ytmp[:].rearrange("p (h d) -> p h d", h=8), in0=self.ytmp[:].rearrange("p (h d) -> p h d", h=8),
                                                 in1=sm[:, 32:40].unsqueeze(2).to_broadcast([128, 8, 64]), op=ALU.mult), ["ytmp", "sm"], ["ytmp"])
        self.OP("dve", lambda e: e.tensor_tensor(out=self.ysb[:], in0=self.ytmp[:], in1=Y[:], op=ALU.add), ["ytmp", ("bank", 2)], ["ysb"])
        if self.kch == 6:
            return
        self.ssd_post(128, tok)
        if self.kch == 7:
            return
        for g in range(2):
            self.OP("dve", lambda e, g=g: e.tensor_tensor(out=self.Bwp[:, g * 4:(g + 1) * 4, g * 64:(g + 1) * 64],
                                                          in0=self.Btm[:, g * 64:(g + 1) * 64].unsqueeze(1).to_broadcast([128, 4, 64]),
                                                          in1=sm[:, 40 + g * 4:44 + g * 4].unsqueeze(2).to_broadcast([128, 4, 64]), op=ALU.mult),
                    ["Btm", "sm"], ["Bwp"])
        SN = self.bank[3]

        def mms(e):
            r = None
            for i in range(4):
                e.matmul(SN[:, 256 + i * 64:256 + (i + 1) * 64], lhsT=self.Bwp[:, i, :], rhs=self.xtm[:, i * 64:(i + 1) * 64], start=True, stop=False)
                r = e.matmul(SN[:, 256 + i * 64:256 + (i + 1) * 64], lhsT=self.Bwp[:, 4 + i, :], rhs=self.xtm[:, (4 + i) * 64:(5 + i) * 64], start=False, stop=True)
            return r
        self.OP("pe", mms, ["Bwp", "xtm"], [("bank", 3)])
        self.OP("dve", lambda e: e.tensor_copy(out=sm[0:64, 58:62], in_=sm[0:64, 48:52]), ["sm"], ["sm"])
        self.OP("dve", lambda e: e.tensor_copy(out=sm[64:128, 58:62], in_=sm[64:128, 52:56]), ["sm"], ["sm"])
        self.OP("dve", lambda e: e.tensor_tensor(out=self.ST[:], in0=self.ST[:], in1=sm[:, 58:62].unsqueeze(2).to_broadcast([128, 4, 64]), op=ALU.mult),
                ["ST", "sm"], ["ST"])
        self.OP("dve", lambda e: e.tensor_tensor(out=self.ST[:], in0=self.ST[:], in1=SN[:, 256:512].rearrange("p (i d) -> p i d", i=4), op=ALU.add),
                ["ST", ("bank", 3)], ["ST"])
        self.OP("act", lambda e: e.copy(out=self.STb[:], in_=self.ST[:]), ["ST"], ["STb"])

    def ssd_state_out(self, l):
        TB = self.bank[5]

        def tr(e):
            r = None
            for i in range(4):
                r = e.transpose(out=TB[0:64, i * 128:(i + 1) * 128], in_=self.ST[:, i, :], identity=self.ident[:])
            return r
        self.OP("pe", tr, ["ST", "ident"], [("bank", 5)])
        self.OP("act", lambda e: e.copy(out=self.stage[0:64, :], in_=TB[0:64, :]), [("bank", 5)], ["stage"])
        for i in range(4):
            self.P.dma("sp", self.o_p["ssm"][l].rearrange("(g i) p n -> i p g n", g=2)[i],
                       self.stage[0:64, i * 128:(i + 1) * 128].rearrange("p (g n) -> p g n", g=2), reads=["stage"], writes=[("o_p_ssm", l, i)])

    def ssd_sample(self, l):
        P = self.P
        ti = 4
        t0, n = TILES[ti]
        raw, xc = self.raw_s, self.proc_s
        M2 = self.M2
        hb = ti % 2
        hT = self.hT[hb]
        hk = [("hT", hb, c) for c in range(8)]
        sm = self.sm
        for j in range(3):
            P.dma("sp", self.smp_tm[j * 16:(j + 1) * 16, :], self.sti["conv"][l, :, j, :], writes=["smp_tm"])
        for q in range(6):
            TB = self.bank[5]
            self.OP("pe", lambda e, q=q: e.transpose(out=TB[:, 0:48], in_=self.smp_tm[0:48, q * 128:(q + 1) * 128], identity=self.ident[0:48, 0:48]),
                    ["smp_tm", "ident"], [("bank", 5)])
            self.OP("act", lambda e, q=q: e.copy(out=raw[:, q, 0:48], in_=TB[:, 0:48]), [("bank", 5)], [("raw", q)])
        for q in range(6):
            self.inproj_fm(ti, M2 + 512 + q * 128, 128, raw[:, q, 48:48 + n], [("raw", q)], eng="act")
        for q in range(6):
            TB = self.bank[5]
            self.OP("pe", lambda e, q=q: e.transpose(out=TB[0:48, 0:128], in_=raw[:, q, 64:112], identity=self.ident[:]), [("raw", q), "ident"], [("bank", 5)])
            self.OP("act", lambda e, q=q: e.copy(out=self.ytmp[0:48, 0:128], in_=TB[0:48, 0:128]), [("bank", 5)], ["ytmp"])
            for j in range(3):
                P.dma("sp", self.o_s["conv"][l, :, j, q * 128:(q + 1) * 128], self.ytmp[j * 16:(j + 1) * 16, 0:128], reads=["ytmp"],
                      writes=[("o_s_conv", l, j, q)])
        self.ssd_conv(raw, xc, n, 16)
        for q in range(6):
            TB = self.bank[5]
            self.OP("pe", lambda e, q=q: e.transpose(out=TB[0:64, 0:128], in_=xc[:, q, 0:64], identity=self.ident[:]), [("xc", q), "ident"], [("bank", 5)])
            self.OP("act", lambda e, q=q: e.copy(out=self.smp_tm[0:64, q * 128:(q + 1) * 128], in_=TB[0:64, 0:128]), [("bank", 5)], ["smp_tm"])
        self.ssd_tokmajor(hT, hk, 0, 64)
        self.OP("act", lambda e: e.activation(out=sm[0:64, 16:24], in_=sm[0:64, 8:16], func=AF.Exp), ["sm"], ["sm"])
        sc = self.sc
        for t in range(4):
            rows = slice(t * 16, (t + 1) * 16)
            P.dma("sp", sc["x"].rearrange("(b h) (t p) -> b h t p", h=8, t=4)[:, :, t, :],
                  self.smp_tm[rows, 0:512].rearrange("b (h p) -> b h p", h=8), reads=["smp_tm"], writes=["sc_x"])
            for i in range(4):
                P.dma("sp", sc["B"].rearrange("(b g i) (t p) -> b g i t p", g=2, i=4, t=4)[:, :, i, t, :],
                      self.smp_tm[rows, 512:640].rearrange("b (g p) -> b g p", g=2), reads=["smp_tm"], writes=["sc_B"])
                P.dma("sp", sc["C"].rearrange("(b g i) (t p) -> b g i t p", g=2, i=4, t=4)[:, :, i, t, :],
                      self.smp_tm[rows, 640:768].rearrange("b (g p) -> b g p", g=2), reads=["smp_tm"], writes=["sc_C"])
            P.dma("sp", sc["dt"].rearrange("(b h) t -> b h t", h=8)[:, :, t], sm[rows, 0:8], reads=["sm"], writes=["sc_dt"],
                  allow_slow_non_contiguous=True)
            P.dma("sp", sc["dA"].rearrange("(b h) t -> b h t", h=8)[:, :, t], sm[rows, 16:24], reads=["sm"], writes=["sc_dA"],
                  allow_slow_non_contiguous=True)
        P.dma("sp", self.xbh[:].rearrange("p t d -> p (t d)"), sc["x"], reads=["sc_x"], writes=["xbh"])
        P.dma("sp", self.Bbh[:].rearrange("p t d -> p (t d)"), sc["B"], reads=["sc_B"], writes=["Bbh"])
        P.dma("sp", self.Cbh[:].rearrange("p t d -> p (t d)"), sc["C"], reads=["sc_C"], writes=["Cbh"])
        P.dma("sp", self.dtbh[:], sc["dt"], reads=["sc_dt"], writes=["dtbh"])
        P.dma("sp", self.dAbh[:], sc["dA"], reads=["sc_dA"], writes=["dAbh"])
        P.dma("sp", self.Ssm[:].rearrange("p a b -> p (a b)"), self.sti["ssm"][l].rearrange("b h p n -> (b h) (p n)"), writes=["Ssm"])
        self.OP("dve", lambda e: e.tensor_tensor(out=self.xbh[:], in0=self.xbh[:], in1=self.dtbh[:].unsqueeze(2).to_broadcast([128, 4, 64]), op=ALU.mult),
                ["xbh", "dtbh"], ["xbh"])
        for t in range(4):
            self.OP("dve", lambda e, t=t: e.tensor_tensor(out=self.tmpS[:], in0=self.xbh[:, t, :].unsqueeze(2).to_broadcast([128, 64, 64]),
                                                          in1=self.Bbh[:, t, :].unsqueeze(1).to_broadcast([128, 64, 64]), op=ALU.mult),
                    ["xbh", "Bbh"], ["tmpS"])
            self.OP("dve", lambda e, t=t: e.scalar_tensor_tensor(out=self.Ssm[:], in0=self.Ssm[:], scalar=self.dAbh[:, t:t + 1], in1=self.tmpS[:],
                                                                 op0=ALU.mult, op1=ALU.add), ["Ssm", "tmpS", "dAbh"], ["Ssm"])
            self.OP("dve", lambda e, t=t: e.tensor_tensor(out=self.tmpS[:], in0=self.Ssm[:], in1=self.Cbh[:, t, :].unsqueeze(1).to_broadcast([128, 64, 64]),
                                                          op=ALU.mult), ["Ssm", "Cbh"], ["tmpS"])
            self.OP("dve", lambda e, t=t: e.tensor_reduce(out=self.ybh[:, t, :], in_=self.tmpS[:], axis=AX.X, op=ALU.add), ["tmpS"], ["ybh"])
        P.dma("sp", self.o_s["ssm"][l].rearrange("b h p n -> (b h) (p n)"), self.Ssm[:].rearrange("p a b -> p (a b)"), reads=["Ssm"], writes=[("o_s_ssm", l)])
        P.dma("sp", sc["y"], self.ybh[:].rearrange("p t d -> p (t d)"), reads=["ybh"], writes=["sc_y"])
        for t in range(4):
            rows = slice(t * 16, (t + 1) * 16)
            P.dma("sp", self.ysb[rows, :].rearrange("b (h p) -> b h p", h=8), sc["y"].rearrange("(b h) (t p) -> b h t p", h=8, t=4)[:, :, t, :],
                  reads=["sc_y"], writes=["ysb"])
        self.OP("dve", lambda e: e.tensor_tensor(out=self.ytmp[0:64, :].rearrange("p (h d) -> p h d", h=8),
                                                 in0=self.smp_tm[0:64, 0:512].rearrange("p (h d) -> p h d", h=8),
                                                 in1=self.D_bc[0:64, :].unsqueeze(2).to_broadcast([64, 8, 64]), op=ALU.mult), ["smp_tm", "D_bc"], ["ytmp"])
        self.OP("dve", lambda e: e.tensor_tensor(out=self.ysb[0:64, :], in0=self.ysb[0:64, :], in1=self.ytmp[0:64, :], op=ALU.add), ["ysb", "ytmp"], ["ysb"])
        self.ssd_post(64, slice(0, 64))

    GB = 896
    def gla_setup(self, l):
        P = self.P
        w = self.w
        win3 = self.s_win.rearrange("(c p) n -> p c n", p=128)
        rk = [("s_win", c) for c in range(8)]
        P.dma("sp", self.wrhs_g[:, :, 0:256], win3[:, :, self.GB + 256:self.GB + 512], reads=rk, writes=["wrhs_g"])
        P.dma("sp", self.wrhs_g[:, :, 256:512], win3[:, :, self.GB + 528:self.GB + 784], reads=rk, writes=["wrhs_g"])
        self.OP("dve", lambda e: e.memset(self.gw2p[:], 0.0), [], ["gw2p"])
        P.dma("sp", self.gw2p[0:16, :], w["gla_gate_w2"][l], reads=["gw2p"], writes=["gw2p"])
        self.bcast_load(self.gb_bc[:], w["gla_gate_b"][l], "gb_bc")
        for i in range(4):
            self.bcast_load(self.gnorm_bc[:, i * 64:(i + 1) * 64], w["gla_norm"][l], "gnorm_bc")
        P.dma("sp", self.cm32[:].rearrange("p h j -> p (h j)"), self.cst["cm32"], writes=["cm32"])
        self.OP("dve", lambda e: e.memset(self.Sg[:], 0.0), [], ["Sg"])
        self.OP("dve", lambda e: e.memset(self.Sgb[:], 0.0), [], ["Sgb"])

    def gla_inproj(self, ti):
        t0, n = TILES[ti]
        g = self.g
        self.OP("dve", lambda e: e.memset(g["glT"][:, 0:n], 0.0), [], ["glT"])
        self.inproj_fm(ti, self.GB, 128, g["qT"][:, 0:n], ["qT"], eng="act")
        self.inproj_fm(ti, self.GB + 128, 128, g["kT"][:, 0:n], ["kT"], eng="dve")
        self.inproj_fm(ti, self.GB + 512, 16, g["glT"][0:16, 0:n], ["glT"], eng="act")

    def gla_tok(self, ti, c0, m):
        g = self.g
        hb = ti % 2
        hT = self.hT[hb]
        hk = [("hT", hb, c) for c in range(8)]
        VG = self.bank[2]

        def mmv(e):
            r = None
            for c in range(8):
                r = e.matmul(VG[0:m, :], lhsT=hT[:, c, c0:c0 + m], rhs=self.wrhs_g[:, c, :], start=(c == 0), stop=(c == 7))
            return r
        self.OP("pe", mmv, hk + ["wrhs_g"], [("bank", 2)])
        self.OP("act", lambda e: e.copy(out=g["vtm"][0:m, :], in_=VG[0:m, 0:256]), [("bank", 2)], ["vtm"])
        self.OP("act", lambda e: e.copy(out=g["vtf"][0:m, :], in_=VG[0:m, 0:256]), [("bank", 2)], ["vtf"])
        self.silu_into(g["gs"][0:m, :], VG[0:m, 256:512], g["gtmp"][0:m, :], [("bank", 2)], "gs", "gtmp")
        GP = self.bank[3]
        self.OP("pe", lambda e: e.matmul(GP[0:m, 0:128], lhsT=g["glT"][:, c0:c0 + m], rhs=self.gw2p[:], start=True, stop=True), ["glT", "gw2p"], [("bank", 3)])
        la = g["la"]
        self.OP("dve", lambda e: e.tensor_tensor(out=la[0:m, :], in0=GP[0:m, 0:128], in1=self.gb_bc[0:m, :], op=ALU.add), [("bank", 3), "gb_bc"], ["la"])
        self.OP("act", lambda e: e.activation(out=la[0:m, :], in_=la[0:m, :], func=AF.Exp, scale=-1.0), ["la"], ["la"])
        self.OP("act", lambda e: e.activation(out=la[0:m, :], in_=la[0:m, :], func=AF.Ln, bias=1.0), ["la"], ["la"])
        self.OP("dve", lambda e: e.tensor_scalar_mul(out=la[0:m, :], in0=la[0:m, :], scalar1=-1.0 / 16.0), ["la"], ["la"])

    def gla_post(self, m, tokcols, src_psum=None):
        g = self.g
        sm = self.sm
        osb, osq = g["osb"], g["osq"]
        self.OP("act", lambda e: e.activation(out=osq[0:m, :], in_=osb[0:m, :], func=AF.Square), ["osb"], ["osq"])
        self.OP("dve", lambda e: e.tensor_reduce(out=sm[0:m, 0:4], in_=osq[0:m, :].rearrange("p (h d) -> p h d", h=4), axis=AX.X, op=ALU.add), ["osq"], ["sm"])
        self.OP("act", lambda e: e.activation(out=sm[0:m, 0:4], in_=sm[0:m, 0:4], func=AF.Ln, bias=self.eps[0:m, :], scale=1.0 / 64), ["sm", "eps"], ["sm"])
        self.OP("act", lambda e: e.activation(out=sm[0:m, 0:4], in_=sm[0:m, 0:4], func=AF.Exp, scale=-0.5), ["sm"], ["sm"])
        self.OP("dve", lambda e: e.tensor_tensor(out=osb[0:m, :].rearrange("p (h d) -> p h d", h=4), in0=osb[0:m, :].rearrange("p (h d) -> p h d", h=4),
                                                 in1=sm[0:m, 0:4].unsqueeze(2).to_broadcast([m, 4, 64]), op=ALU.mult), ["osb", "sm"], ["osb"])
        self.OP("dve", lambda e: e.tensor_tensor(out=osb[0:m, :], in0=osb[0:m, :], in1=self.gnorm_bc[0:m, :], op=ALU.mult), ["osb", "gnorm_bc"], ["osb"])
        self.OP("dve", lambda e: e.tensor_tensor(out=osb[0:m, :], in0=osb[0:m, :], in1=g["gs"][0:m, :], op=ALU.mult), ["osb", "gs"], ["osb"])
        TB = self.bank[5]

        def tr(e):
            r = None
            for q in range(2):
                r = e.transpose(out=TB[:, q * 128:q * 128 + m], in_=osb[0:m, q * 128:(q + 1) * 128], identity=self.ident[0:m, 0:m])
            return r
        self.OP("pe", tr, ["osb", "ident"], [("bank", 5)])
        self.OP("act", lambda e: e.copy(out=self.yT[:, 2:4, tokcols], in_=TB[:, 0:256].rearrange("p (q t) -> p q t", q=2)[:, :, 0:m]),
                [("bank", 5)], [("yT", 2), ("yT", 3)])

    def gla_tile(self, l, ti):
        t0, n = TILES[ti]
        self.gla_inproj(ti)
        for k in range(n // 128):
            self.gla_chunk(l, ti, k)
        if ti == 3:
            self.P.dma("sp", self.o_p["gla"][l].rearrange("h k v -> (h k) v"), self.Sg[:], reads=["Sg"], writes=[("o_p_gla", l)])

    def gla_chunk(self, l, ti, k):
        g = self.g
        c0 = k * 128
        tok = slice(c0, c0 + 128)
        self.gla_tok(ti, c0, 128)
        la = g["la"]
        BP = self.bank[3]
        self.OP("pe", lambda e: e.matmul(BP[:, 128:256], lhsT=la[:], rhs=self.cT["triu"][:], start=True, stop=True), ["la", "k_triu"], [("bank", 3)])
        self.OP("act", lambda e: e.activation(out=g["eb"][:], in_=BP[:, 128:256], func=AF.Exp), [("bank", 3)], ["eb"])
        self.OP("act", lambda e: e.activation(out=g["enb"][:], in_=BP[:, 128:256], func=AF.Exp, scale=-1.0), [("bank", 3)], ["enb"])
        self.OP("dve", lambda e: e.scalar_tensor_tensor(out=g["qtb"][:], in0=g["qT"][:, tok], scalar=32.0 ** -0.5, in1=g["eb"][:], op0=ALU.mult, op1=ALU.mult),
                ["qT", "eb"], ["qtb"])
        self.OP("dve", lambda e: e.tensor_tensor(out=g["ktf"][:], in0=g["kT"][:, tok], in1=g["enb"][:], op=ALU.mult), ["kT", "enb"], ["ktf"])
        self.OP("dve", lambda e: e.tensor_tensor(out=g["km"][:], in0=g["ktf"][:].unsqueeze(1).to_broadcast([128, 4, 128]),
                                                 in1=self.hm32[:].unsqueeze(2).to_broadcast([128, 4, 128]), op=ALU.mult), ["ktf", "hm32"], ["km"])
        self.OP("dve", lambda e: e.tensor_tensor(out=g["qm"][:], in0=g["qtb"][:].unsqueeze(1).to_broadcast([128, 4, 128]),
                                                 in1=self.hm32[:].unsqueeze(2).to_broadcast([128, 4, 128]), op=ALU.mult), ["qtb", "hm32"], ["qm"])
        KT = self.bank[5]
        self.OP("pe", lambda e: e.transpose(out=KT[:, 0:128], in_=g["ktf"][:], identity=self.ident[:]), ["ktf", "ident"], [("bank", 5)])
        self.OP("act", lambda e: e.copy(out=g["ktm32"][:], in_=KT[:, 0:128]), [("bank", 5)], ["ktm32"])
        self.OP("dve", lambda e: e.tensor_tensor(out=g["ktmm"][:], in0=g["ktm32"][:].unsqueeze(1).to_broadcast([128, 4, 128]),
                                                 in1=self.cm32[:], op=ALU.mult), ["ktm32", "cm32"], ["ktmm"])
        ATT = self.bank[4]

        def mma(e):
            r = None
            for h in range(4):
                r = e.matmul(ATT[:, h * 128:(h + 1) * 128], lhsT=g["km"][:, h, :], rhs=g["qtb"][:], start=True, stop=True)
            return r
        self.OP("pe", mma, ["km", "qtb"], [("bank", 4)])
        self.OP("act", lambda e: e.copy(out=g["attf"][:].rearrange("p h t -> p (h t)"), in_=ATT[:]), [("bank", 4)], ["attf"])
        self.OP("dve", lambda e: e.tensor_tensor(out=g["attm"][:], in0=g["attf"][:], in1=self.cT["triu"][:].unsqueeze(1).to_broadcast([128, 4, 128]), op=ALU.mult),
                ["attf", "k_triu"], ["attm"])
        O = self.bank[2]

        def mmo(e):
            r = None
            for h in range(4):
                e.matmul(O[:, h * 64:(h + 1) * 64], lhsT=g["attm"][:, h, :], rhs=g["vtm"][:, h * 64:(h + 1) * 64], start=True, stop=False)
                r = e.matmul(O[:, h * 64:(h + 1) * 64], lhsT=g["qm"][:, h, :], rhs=self.Sgb[:], start=False, stop=True)
            return r
        self.OP("pe", mmo, ["attm", "vtm", "qm", "Sgb"], [("bank", 2)])
        self.OP("act", lambda e: e.copy(out=g["osb"][:], in_=O[:, 0:256]), [("bank", 2)], ["osb"])
        self.gla_post(128, tok)
        SN = self.bank[3]

        def mms(e):
            r = None
            for h in range(4):
                r = e.matmul(SN[:, 256:320], lhsT=g["ktmm"][:, h, :], rhs=g["vtm"][:, h * 64:(h + 1) * 64], start=(h == 0), stop=(h == 3))
            return r
        self.OP("pe", mms, ["ktmm", "vtm"], [("bank", 3)])
        self.OP("act", lambda e: e.activation(out=g["t1"][:], in_=SN[:, 256:320], func=AF.Copy, scale=g["eb"][:, 127:128]), [("bank", 3), "eb"], ["t1"])
        self.OP("dve", lambda e: e.scalar_tensor_tensor(out=self.Sg[:], in0=self.Sg[:], scalar=g["eb"][:, 127:128], in1=g["t1"][:], op0=ALU.mult, op1=ALU.add),
                ["Sg", "eb", "t1"], ["Sg"])
        self.OP("act", lambda e: e.copy(out=self.Sgb[:], in_=self.Sg[:]), ["Sg"], ["Sgb"])

    def gla_sample(self, l):
        P = self.P
        g = self.g
        ti = 4
        t0, n = TILES[ti]
        self.gla_inproj(ti)
        self.gla_tok(ti, 0, 64)
        la = g["la"]
        self.OP("act", lambda e: e.activation(out=g["ela"][0:64, :], in_=la[0:64, :], func=AF.Exp), ["la"], ["ela"])
        TB = self.bank[5]
        self.OP("pe", lambda e: e.transpose(out=TB[0:64, 0:128], in_=g["qT"][:, 0:64], identity=self.ident[:]), ["qT", "ident"], [("bank", 5)])
        self.OP("act", lambda e: e.activation(out=g["qtm"][0:64, :], in_=TB[0:64, 0:128], func=AF.Copy, scale=32.0 ** -0.5), [("bank", 5)], ["qtm"])
        self.OP("pe", lambda e: e.transpose(out=TB[0:64, 128:256], in_=g["kT"][:, 0:64], identity=self.ident[:]), ["kT", "ident"], [("bank", 5)])
        self.OP("act", lambda e: e.copy(out=g["ktm32"][0:64, :], in_=TB[0:64, 128:256]), [("bank", 5)], ["ktm32"])
        sc = self.sc
        for t in range(4):
            rows = slice(t * 16, (t + 1) * 16)
            for nme, src, wd_, key in (("gq", g["qtm"], 32, "qtm"), ("gk", g["ktm32"], 32, "ktm32"), ("ge", g["ela"], 32, "ela"), ("gv", g["vtf"], 64, "vtf")):
                P.dma("sp", sc[nme].rearrange("(b h) (t w) -> b h t w", h=4, t=4)[:, :, t, :],
                      src[rows, 0:4 * wd_].rearrange("b (h w) -> b h w", h=4), reads=[key], writes=["sc_" + nme])
        for nme, dst, key in (("gq", g["qbh"], "qbh"), ("gk", g["kbh"], "kbh"), ("ge", g["ebh"], "ebh"), ("gv", g["vbh"], "vbh")):
            P.dma("sp", dst[0:64].rearrange("p t w -> p (t w)"), sc[nme], reads=["sc_" + nme], writes=[key])
        Ss, tS = g["Ss"], g["tS"]
        P.dma("sp", Ss[0:64].rearrange("p k v -> p (k v)"), self.sti["gla"][l].rearrange("b h k v -> (b h) (k v)"), writes=["Ss"])
        for t in range(4):
            self.OP("dve", lambda e, t=t: e.tensor_tensor(out=tS[0:64], in0=g["kbh"][0:64, t, :].unsqueeze(2).to_broadcast([64, 32, 64]),
                                                          in1=g["vbh"][0:64, t, :].unsqueeze(1).to_broadcast([64, 32, 64]), op=ALU.mult), ["kbh", "vbh"], ["tS"])
            self.OP("dve", lambda e, t=t: e.tensor_tensor(out=Ss[0:64], in0=Ss[0:64], in1=g["ebh"][0:64, t, :].unsqueeze(2).to_broadcast([64, 32, 64]), op=ALU.mult),
                    ["Ss", "ebh"], ["Ss"])
            self.OP("dve", lambda e, t=t: e.tensor_tensor(out=Ss[0:64], in0=Ss[0:64], in1=tS[0:64], op=ALU.add), ["Ss", "tS"], ["Ss"])
            self.OP("dve", lambda e, t=t: e.tensor_tensor(out=tS[0:64], in0=Ss[0:64], in1=g["qbh"][0:64, t, :].unsqueeze(2).to_broadcast([64, 32, 64]), op=ALU.mult),
                    ["Ss", "qbh"], ["tS"])
            self.OP("dve", lambda e, t=t: e.tensor_reduce(out=g["obh"][0:64, t, :], in_=tS[0:64].rearrange("p k v -> p v k"), axis=AX.X, op=ALU.add), ["tS"], ["obh"])
        P.dma("sp", self.o_s["gla"][l].rearrange("b h k v -> (b h) (k v)"), Ss[0:64].rearrange("p k v -> p (k v)"), reads=["Ss"], writes=[("o_s_gla", l)])
        P.dma("sp", sc["go"], g["obh"][0:64].rearrange("p t w -> p (t w)"), reads=["obh"], writes=["sc_go"])
        for t in range(4):
            rows = slice(t * 16, (t + 1) * 16)
            P.dma("sp", g["osb"][rows, :].rearrange("b (h w) -> b h w", h=4), sc["go"].rearrange("(b h) (t w) -> b h t w", h=4, t=4)[:, :, t, :],
                  reads=["sc_go"], writes=["osb"])
        self.gla_post(64, slice(0, 64))

    C0 = 0.6065306597126334
    def rwkv_setup(self, l):
        P = self.P
        r = self.r
        w = self.w

        def pp(dst, src1d, key):
            P.dma("sp", dst, src1d.rearrange("(j p) -> p j", p=128), writes=[key], allow_slow_non_contiguous=True)
        P.dma("sp", r["mu7"][:], w["rwkv_mu"][l].rearrange("(q p) -> p q", p=128), writes=["mu7"], allow_slow_non_contiguous=True)
        pp(r["w0n"][:], w["rwkv_w0"][l], "w0n")
        self.OP("dve", lambda e: e.tensor_scalar_mul(out=r["w0n"][:], in0=r["w0n"][:], scalar1=-1.0), ["w0n"], ["w0n"])
        pp(r["a0n"][:], w["rwkv_a0"][l], "a0n")
        self.OP("dve", lambda e: e.tensor_scalar_mul(out=r["a0n"][:], in0=r["a0n"][:], scalar1=-1.0), ["a0n"], ["a0n"])
        pp(r["kk_"][:], w["rwkv_k_k"][l], "kk_")
        pp(r["ka"][:], w["rwkv_k_a"][l], "ka")
        self.OP("dve", lambda e: e.tensor_scalar(out=r["oma"][:], in0=r["ka"][:], scalar1=-1.0, scalar2=1.0, op0=ALU.mult, op1=ALU.add), ["ka"], ["oma"])
        pp(r["rk"][:], w["rwkv_r_k"][l].rearrange("h k -> (h k)"), "rk")
        self.OP("dve", lambda e: e.tensor_copy(out=r["hm64"][:, 0:1], in_=self.cT["blk64"][:, 0:1]), ["k_blk64"], ["hm64"])
        self.OP("dve", lambda e: e.tensor_copy(out=r["hm64"][:, 1:2], in_=self.cT["blk64"][:, 64:65]), ["k_blk64", "hm64"], ["hm64"])
        self.OP("dve", lambda e: e.memset(r["lsc"][0:32, :], -2.0), [], ["lsc"])
        self.OP("dve", lambda e: e.memset(r["lsc"][32:64, :], 0.0), ["lsc"], ["lsc"])
        self.OP("dve", lambda e: e.memset(r["lsc"][64:128, :], -1.0), ["lsc"], ["lsc"])
        for nme, src, r0, r1 in (("w2p", "rwkv_w2", 0, 32), ("a2p", "rwkv_a2", 32, 64), ("g2p", "rwkv_g2", 64, 128)):
            self.OP("dve", lambda e, nme=nme: e.memset(r[nme][:], 0.0), [], [nme])
            P.dma("sp", r[nme][r0:r1, :], w[src][l], reads=[nme], writes=[nme])
        self.bcast_load(r["lnw_bc"][:], w["rwkv_ln_w"][l], "lnw_bc")
        self.bcast_load(r["lnb_bc"][:], w["rwkv_ln_b"][l], "lnb_bc")
        self.OP("dve", lambda e: e.memset(r["Sf"][:], 0.0), [], ["Sf"])
        self.OP("dve", lambda e: e.memset(r["Sb"][:], 0.0), [], ["Sb"])
        self.OP("dve", lambda e: e.memset(r["hist"][:], 0.0), [], ["rhist"])
        self.OP("dve", lambda e: e.memset(r["tiny"][:], 1e-24), [], ["tiny"])
        self.OP("dve", lambda e: e.memset(r["gneps"][:], 64e-5), [], ["gneps"])
        self.OP("dve", lambda e: e.tensor_scalar(out=r["lstrict"][:], in0=self.cT["triu"][:], scalar1=-1.0, scalar2=1.0, op0=ALU.mult, op1=ALU.add),
                ["k_triu"], ["lstrict"])

    def rwkv_front(self, ti, raw, zs, n, step):
        r = self.r
        rk = [("rraw", q) for q in range(7)]
        for q in range(7):
            self.inproj_fm(ti, q * 128, 128, raw[:, q, step:step + n], [("rraw", q)], eng="act")
        self.OP("dve", lambda e: e.tensor_tensor(out=zs[:, :, 0:n], in0=raw[:, :, 0:n], in1=raw[:, :, step:step + n], op=ALU.subtract), rk, ["zs"])
        self.OP("dve", lambda e: e.tensor_tensor(out=zs[:, :, 0:n], in0=zs[:, :, 0:n], in1=r["mu7"][:].unsqueeze(2).to_broadcast([128, 7, n]), op=ALU.mult),
                ["zs", "mu7"], ["zs"])
        self.OP("dve", lambda e: e.tensor_tensor(out=zs[:, :, 0:n], in0=zs[:, :, 0:n], in1=raw[:, :, step:step + n], op=ALU.add), ["zs"] + rk, ["zs"])
        la = r["lact"]
        self.OP("act", lambda e: e.activation(out=la[:, 0:n], in_=zs[:, 6, 0:n], func=AF.Exp, scale=r["lsc"][:]), ["zs", "lsc"], ["lact"])
        self.OP("act", lambda e: e.activation(out=la[:, 0:n], in_=la[:, 0:n], func=AF.Ln, bias=1.0), ["lact"], ["lact"])
        self.OP("act", lambda e: e.activation(out=la[:, 0:n], in_=la[:, 0:n], func=AF.Exp, scale=-1.0), ["lact"], ["lact"])
        self.OP("dve", lambda e: e.tensor_scalar(out=la[0:32, 0:n], in0=la[0:32, 0:n], scalar1=2.0, scalar2=-1.0, op0=ALU.mult, op1=ALU.add), ["lact"], ["lact"])
        self.OP("dve", lambda e: e.tensor_copy(out=la[32:64, 0:n], in_=zs[32:64, 6, 0:n]), ["lact", "zs"], ["lact"])

    def sigm_from(self, dst, src_psum, nbias, rk, key):
        self.OP("act", lambda e: e.activation(out=dst, in_=src_psum, func=AF.Exp, scale=-1.0, bias=nbias), rk, [key])
        self.OP("act", lambda e: e.activation(out=dst, in_=dst, func=AF.Ln, bias=1.0), [key], [key])
        self.OP("act", lambda e: e.activation(out=dst, in_=dst, func=AF.Exp, scale=-1.0), [key], [key])

    def rwkv_prelim(self, j, zs, c0, n, chunked):
        r = self.r
        tok = slice(c0, c0 + n)
        rr, kr = zs[:, j, tok], zs[:, 2 + j, tok]
        la = r["lact"]
        PS = self.bank[2]
        jc = slice(j * 128, (j + 1) * 128)
        V = lambda nme: r[nme][:, j, 0:n]
        K = lambda nme: (nme, j)
        self.OP("pe", lambda e: e.matmul(PS[:, 0:n], lhsT=r["w2p"][:, jc], rhs=la[:, tok], start=True, stop=True), ["w2p", "lact"], [("bank", 2)])
        self.OP("pe", lambda e: e.matmul(PS[:, 128:128 + n], lhsT=r["a2p"][:, jc], rhs=la[:, tok], start=True, stop=True), ["a2p", "lact"], [("bank", 2)])
        self.sigm_from(V("sgw"), PS[:, 0:n], r["w0n"][:, j:j + 1], [("bank", 2), "w0n"], K("sgw"))
        self.sigm_from(V("al"), PS[:, 128:128 + n], r["a0n"][:, j:j + 1], [("bank", 2), "a0n"], K("al"))
        self.OP("act", lambda e: e.activation(out=V("kk"), in_=kr, func=AF.Copy, scale=r["kk_"][:, j:j + 1]), ["zs", "kk_"], [K("kk")])
        self.OP("act", lambda e: e.activation(out=V("T1"), in_=V("kk"), func=AF.Square), [K("kk")], [K("T1")])
        self.OP("pe", lambda e: e.matmul(PS[:, 256:256 + n], lhsT=self.cT["blk64"][:], rhs=V("T1"), start=True, stop=True), [K("T1"), "k_blk64"], [("bank", 2)])
        self.OP("act", lambda e: e.activation(out=V("T1"), in_=PS[:, 256:256 + n], func=AF.Ln, bias=r["tiny"][:]), [("bank", 2), "tiny"], [K("T1")])
        self.OP("act", lambda e: e.activation(out=V("T1"), in_=V("T1"), func=AF.Exp, scale=-0.5), [K("T1")], [K("T1")])
        self.OP("dve", lambda e: e.tensor_tensor(out=V("kk"), in0=V("kk"), in1=V("T1"), op=ALU.mult), [K("kk"), K("T1")], [K("kk")])
        self.OP("dve", lambda e: e.tensor_scalar(out=V("T1"), in0=V("al"), scalar1=r["ka"][:, j:j + 1], scalar2=r["oma"][:, j:j + 1], op0=ALU.mult, op1=ALU.add),
                [K("al"), "ka", "oma", K("T1")], [K("T1")])
        self.OP("dve", lambda e: e.tensor_tensor(out=V("kp"), in0=kr, in1=V("T1"), op=ALU.mult), ["zs", K("T1")], [K("kp")])
        self.OP("dve", lambda e: e.tensor_tensor(out=V("bq"), in0=V("kk"), in1=V("al"), op=ALU.mult), [K("kk"), K("al")], [K("bq")])
        self.OP("dve", lambda e: e.scalar_tensor_tensor(out=V("rkr"), in0=rr, scalar=r["rk"][:, j:j + 1], in1=V("kp"), op0=ALU.mult, op1=ALU.mult),
                ["zs", "rk", K("kp")], [K("rkr")])
        if not chunked:
            self.OP("act", lambda e: e.activation(out=V("Ep"), in_=V("sgw"), func=AF.Exp, scale=-self.C0), [K("sgw")], [K("Ep")])
            return
        self.OP("dve", lambda e: e.tensor_tensor_scan(out=V("cs"), data0=self.ones32[:, 0:n], data1=V("sgw"), initial=0.0, op0=ALU.mult, op1=ALU.add),
                [K("sgw"), "ones32"], [K("cs")])
        self.OP("act", lambda e: e.activation(out=V("Ep"), in_=V("cs"), func=AF.Exp, scale=-self.C0), [K("cs")], [K("Ep")])
        self.OP("act", lambda e: e.activation(out=V("Em"), in_=V("cs"), func=AF.Exp, scale=self.C0), [K("cs")], [K("Em")])
        self.OP("dve", lambda e: e.tensor_tensor(out=V("sgw"), in0=V("cs"), in1=V("sgw"), op=ALU.subtract), [K("cs"), K("sgw")], [K("sgw")])
        self.OP("act", lambda e: e.activation(out=V("Epv"), in_=V("sgw"), func=AF.Exp, scale=-self.C0), [K("sgw")], [K("Epv")])
        self.OP("dve", lambda e: e.scalar_tensor_tensor(out=V("At"), in0=V("kk"), scalar=-1.0, in1=V("Epv"), op0=ALU.mult, op1=ALU.mult), [K("kk"), K("Epv")], [K("At")])
        self.OP("dve", lambda e: e.tensor_tensor(out=V("Rt"), in0=rr, in1=V("Ep"), op=ALU.mult), ["zs", K("Ep")], [K("Rt")])
        self.OP("dve", lambda e: e.tensor_tensor(out=V("Bh"), in0=V("bq"), in1=V("Em"), op=ALU.mult), [K("bq"), K("Em")], [K("Bh")])
        self.OP("dve", lambda e: e.tensor_tensor(out=V("Kh"), in0=V("kp"), in1=V("Em"), op=ALU.mult), [K("kp"), K("Em")], [K("Kh")])
        for src, dstb in (("At", "Atb"), ("Rt", "Rtb"), ("Bh", "Bhb")):
            self.OP("act", lambda e, src=src, dstb=dstb: e.copy(out=r[dstb][:, j, :], in_=r[src][:, j, :]), [K(src)], [K(dstb)])
        hm = r["hm64"][:].unsqueeze(2).to_broadcast([128, 2, 128])
        for src, dstm in (("At", "Atm"), ("Bh", "Bhm"), ("Kh", "Khm")):
            self.OP("dve", lambda e, src=src, dstm=dstm: e.tensor_tensor(out=r[dstm][:, 2 * j:2 * j + 2, :],
                                                                         in0=r[src][:, j, :].unsqueeze(1).to_broadcast([128, 2, 128]), in1=hm, op=ALU.mult),
                    [K(src), "hm64"], [K(dstm)])
        for nme in ("Bh", "Kh"):
            self.OP("act", lambda e, nme=nme: e.activation(out=r[nme][:, j, :], in_=r[nme][:, j, :], func=AF.Copy, scale=r["Ep"][:, j, 127:128]),
                    [K(nme), K("Ep")], [K(nme)])

    def rwkv_tokmajor(self, zs, c0, m, chunked):
        r = self.r
        tok = slice(c0, c0 + m)
        TV = self.bank[4]

        def trv(e):
            rr_ = None
            for j in range(2):
                rr_ = e.transpose(out=TV[0:m, j * 128:(j + 1) * 128], in_=zs[:, 4 + j, tok], identity=self.ident[:])
            return rr_
        self.OP("pe", trv, ["zs", "ident"], [("bank", 4)])
        self.OP("act", lambda e: e.copy(out=r["vtf"][0:m, :], in_=TV[0:m, 0:256]), [("bank", 4)], ["vtf"])
        self.OP("act", lambda e: e.copy(out=r["Vtb"][0:m, :], in_=TV[0:m, 0:256]), [("bank", 4)], ["Vtb"])
        if chunked:
            TK = self.bank[5]

            def trb(e):
                rr_ = None
                for j in range(2):
                    e.transpose(out=TV[:, 256 + j * 128:256 + (j + 1) * 128], in_=r["Bh"][:, j, :], identity=self.ident[:])
                    rr_ = e.transpose(out=TK[:, j * 128:(j + 1) * 128], in_=r["Kh"][:, j, :], identity=self.ident[:])
                return rr_
            self.OP("pe", trb, [("Bh", 0), ("Bh", 1), ("Kh", 0), ("Kh", 1), "ident"], [("bank", 4), ("bank", 5)])
            self.OP("act", lambda e: e.copy(out=r["Bcb"][:], in_=TV[:, 256:512]), [("bank", 4)], ["Bcb"])
            self.OP("act", lambda e: e.copy(out=r["Kcb"][:], in_=TK[:, 0:256]), [("bank", 5)], ["Kcb"])
        G = self.bank[3]
        self.OP("pe", lambda e: e.matmul(G[0:m, 0:256], lhsT=r["lact"][:, tok], rhs=r["g2p"][:], start=True, stop=True), ["lact", "g2p"], [("bank", 3)])
        self.OP("act", lambda e: e.copy(out=r["gsb"][0:m, :], in_=G[0:m, 0:256]), [("bank", 3)], ["gsb"])

        def mmb(e):
            rr_ = None
            for j in range(2):
                rr_ = e.matmul(G[0:m, 256 + 2 * j:258 + 2 * j], lhsT=r["rkr"][:, j, 0:m], rhs=r["hm64"][:], start=True, stop=True)
            return rr_
        self.OP("pe", mmb, [("rkr", 0), ("rkr", 1), "hm64"], [("bank", 3)])
        self.OP("act", lambda e: e.copy(out=r["bon"][0:m, :], in_=G[0:m, 256:260]), [("bank", 3)], ["bon"])

    def rwkv_post(self, m, tokcols):
        r = self.r
        sm = self.sm
        Y, ysq, yt = r["Ysb"], r["ysq"], r["ytmp"]
        v3 = lambda ap: ap[0:m, :].rearrange("p (h d) -> p h d", h=4)
        self.OP("act", lambda e: e.activation(out=ysq[0:m, :], in_=Y[0:m, :], func=AF.Square), ["Ysb"], ["ysq"])
        self.OP("dve", lambda e: e.tensor_reduce(out=sm[0:m, 0:4], in_=v3(Y), axis=AX.X, op=ALU.add), ["Ysb"], ["sm"])
        self.OP("dve", lambda e: e.tensor_reduce(out=sm[0:m, 4:8], in_=v3(ysq), axis=AX.X, op=ALU.add), ["ysq", "sm"], ["sm"])
        self.OP("dve", lambda e: e.tensor_scalar_mul(out=sm[0:m, 0:8], in0=sm[0:m, 0:8], scalar1=1.0 / 64), ["sm"], ["sm"])
        self.OP("dve", lambda e: e.tensor_tensor(out=sm[0:m, 8:12], in0=sm[0:m, 0:4], in1=sm[0:m, 0:4], op=ALU.mult), ["sm"], ["sm"])
        self.OP("dve", lambda e: e.tensor_tensor(out=sm[0:m, 8:12], in0=sm[0:m, 4:8], in1=sm[0:m, 8:12], op=ALU.subtract), ["sm"], ["sm"])
        self.OP("act", lambda e: e.activation(out=sm[0:m, 8:12], in_=sm[0:m, 8:12], func=AF.Ln, bias=r["gneps"][0:m, :]), ["sm", "gneps"], ["sm"])
        self.OP("act", lambda e: e.activation(out=sm[0:m, 8:12], in_=sm[0:m, 8:12], func=AF.Exp, scale=-0.5), ["sm"], ["sm"])
        self.OP("dve", lambda e: e.tensor_tensor(out=v3(Y), in0=v3(Y), in1=sm[0:m, 0:4].unsqueeze(2).to_broadcast([m, 4, 64]), op=ALU.subtract), ["Ysb", "sm"], ["Ysb"])
        self.OP("dve", lambda e: e.tensor_tensor(out=v3(Y), in0=v3(Y), in1=sm[0:m, 8:12].unsqueeze(2).to_broadcast([m, 4, 64]), op=ALU.mult), ["Ysb", "sm"], ["Ysb"])
        self.OP("dve", lambda e: e.tensor_tensor(out=Y[0:m, :], in0=Y[0:m, :], in1=r["lnw_bc"][0:m, :], op=ALU.mult), ["Ysb", "lnw_bc"], ["Ysb"])
        self.OP("dve", lambda e: e.tensor_tensor(out=Y[0:m, :], in0=Y[0:m, :], in1=r["lnb_bc"][0:m, :], op=ALU.add), ["Ysb", "lnb_bc"], ["Ysb"])
        self.OP("dve", lambda e: e.tensor_tensor(out=v3(yt), in0=v3(r["vtf"]), in1=r["bon"][0:m, :].unsqueeze(2).to_broadcast([m, 4, 64]), op=ALU.mult),
                ["vtf", "bon"], ["ytmp_r"])
        self.OP("dve", lambda e: e.tensor_tensor(out=Y[0:m, :], in0=Y[0:m, :], in1=yt[0:m, :], op=ALU.add), ["Ysb", "ytmp_r"], ["Ysb"])
        self.OP("dve", lambda e: e.tensor_tensor(out=Y[0:m, :], in0=Y[0:m, :], in1=r["gsb"][0:m, :], op=ALU.mult), ["Ysb", "gsb"], ["Ysb"])
        TB = self.bank[5]

        def tr(e):
            rr_ = None
            for q in range(2):
                rr_ = e.transpose(out=TB[:, q * 128:q * 128 + m], in_=Y[0:m, q * 128:(q + 1) * 128], identity=self.ident[0:m, 0:m])
            return rr_
        self.OP("pe", tr, ["Ysb", "ident"], [("bank", 5)])
        self.OP("act", lambda e: e.copy(out=self.yT[:, 0:2, tokcols], in_=TB[:, 0:256].rearrange("p (q t) -> p q t", q=2)[:, :, 0:m]),
                [("bank", 5)], [("yT", 0), ("yT", 1)])

    def rwkv_tile(self, l, ti):
        P = self.P
        r = self.r
        t0, n = TILES[ti]
        raw, zs = r["raw"], r["zs"]
        self.OP("dve", lambda e: e.tensor_copy(out=raw[:, :, 0:1], in_=r["hist"][:]), ["rhist"], [("rraw", q) for q in range(7)])
        self.rwkv_front(ti, raw, zs, n, 1)
        self.OP("dve", lambda e: e.tensor_copy(out=r["hist"][:], in_=raw[:, :, n:n + 1]), [("rraw", q) for q in range(7)], ["rhist"])
        if ti == 3:
            P.dma("sp", self.o_p["shift"][l].rearrange("(q p) -> p q", p=128), raw[:, :, n], reads=[("rraw", q) for q in range(7)],
                  writes=[("o_p_shift", l)], allow_slow_non_contiguous=True)
        for k in range(n // 128):
            self.rwkv_chunk(l, ti, k)
        if ti == 3:
            self.rwkv_state_out(l)

    def rwkv_scores(self, name, lhs, rhs, mask, bi):
        r = self.r
        bank = self.bank[bi]

        def mm(e):
            rr_ = None
            for h in range(4):
                rr_ = e.matmul(bank[:, h * 128:(h + 1) * 128], lhsT=r[lhs][:, h, :], rhs=r[rhs][:, h // 2, :], start=True, stop=True)
            return rr_
        self.OP("pe", mm, [(lhs, 0), (lhs, 1), (rhs, 0), (rhs, 1)], [("bank", bi)])
        self.OP("act", lambda e: e.copy(out=r["stg"][:].rearrange("p h t -> p (h t)"), in_=bank[:]), [("bank", bi)], ["stg"])
        self.OP("dve", lambda e: e.tensor_tensor(out=r[name][:], in0=r["stg"][:], in1=mask.unsqueeze(1).to_broadcast([128, 4, 128]), op=ALU.mult),
                ["stg", "k_triu", "k_mstrict", "lstrict"], [name])

    def rwkv_chunk(self, l, ti, k):
        base = self.r
        par = (ti * 4 + k) % 2
        if par == 1:
            rr = dict(base)
            for nme in self.DBK:
                rr[nme] = base[nme + "_1"]
            self.r = rr
        self.kpar = par
        try:
            self._rwkv_chunk(l, ti, k)
        finally:
            self.r = base
            self.kpar = None

    def _rwkv_chunk(self, l, ti, k):
        r = self.r
        c0 = k * 128
        tok = slice(c0, c0 + 128)
        zs = r["zs"]
        for j in range(2):
            self.rwkv_prelim(j, zs, c0, 128, True)
        self.rwkv_tokmajor(zs, c0, 128, True)
        triu, mstrict, lstrict = self.cT["triu"][:], self.cT["mstrict"][:], r["lstrict"][:]
        self.rwkv_scores("A", "Bhm", "Atb", mstrict, 6)
        self.rwkv_scores("M", "Atm", "Bhb", lstrict, 7)
        self.rwkv_scores("MakT", "Khm", "Atb", mstrict, 6)
        self.rwkv_scores("NrbT", "Bhm", "Rtb", triu, 7)
        self.rwkv_scores("NrkT", "Khm", "Rtb", triu, 6)
        self.OP("dve", lambda e: e.tensor_tensor(out=r["Z"][:], in0=r["A"][:], in1=self.ident[:].unsqueeze(1).to_broadcast([128, 4, 128]), op=ALU.add),
                ["A", "ident"], ["Z"])
        BA, BM, BZ = self.bank[0], self.bank[1], self.bank[7]
        for lvl in range(6):
            last = (lvl == 5)
            if not last:
                def mmA(e):
                    rr_ = None
                    for h in range(4):
                        rr_ = e.matmul(BA[:, h * 128:(h + 1) * 128], lhsT=r["M"][:, h, :], rhs=r["A"][:, h, :], start=True, stop=True)
                    return rr_
                self.OP("pe", mmA, ["M", "A"], [("bank", 0)])

            def mmM(e):
                rr_ = None
                for h in range(4):
                    rr_ = e.matmul(BM[:, h * 128:(h + 1) * 128], lhsT=r["A"][:, h, :], rhs=r["M"][:, h, :], start=True, stop=True)
                return rr_
            self.OP("pe", mmM, ["M", "A"], [("bank", 1)])
            if not last:
                self.OP("act", lambda e: e.copy(out=r["A"][:].rearrange("p h t -> p (h t)"), in_=BA[:]), [("bank", 0)], ["A"])
            self.OP("act", lambda e: e.copy(out=r["M"][:].rearrange("p h t -> p (h t)"), in_=BM[:]), [("bank", 1)], ["M"])

            def mmZ(e):
                rr_ = None
                for h in range(4):
                    rr_ = e.matmul(BZ[:, h * 128:(h + 1) * 128], lhsT=r["M"][:, h, :], rhs=r["Z"][:, h, :], start=True, stop=True)
                return rr_
            self.OP("pe", mmZ, ["M", "Z"], [("bank", 7)])
            self.OP("dve", lambda e: e.tensor_tensor(out=r["Z"][:].rearrange("p h t -> p (h t)"), in0=r["Z"][:].rearrange("p h t -> p (h t)"), in1=BZ[:], op=ALU.add),
                    [("bank", 7), "Z"], ["Z"])
        R1, R2 = self.bank[0], self.bank[1]

        def mmR(e):
            rr_ = None
            for j in range(2):
                e.matmul(R1[:, j * 128:(j + 1) * 128], lhsT=r["Atb"][:, j, :], rhs=r["Sb"][:, j, :], start=True, stop=False)
                for h2 in range(2):
                    h = 2 * j + h2
                    rr_ = e.matmul(R1[:, h * 64:(h + 1) * 64], lhsT=r["MakT"][:, h, :], rhs=r["Vtb"][:, h * 64:(h + 1) * 64], start=False, stop=(h2 == 1))
            return rr_
        self.OP("pe", mmR, [("Atb", 0), ("Atb", 1), "Sb", "MakT", "Vtb"], [("bank", 0)])
        self.OP("act", lambda e: e.copy(out=r["RHSb"][:], in_=R1[:, 0:256]), [("bank", 0)], ["RHSb"])

        def mmU(e):
            rr_ = None
            for h in range(4):
                rr_ = e.matmul(R2[:, h * 64:(h + 1) * 64], lhsT=r["Z"][:, h, :], rhs=r["RHSb"][:, h * 64:(h + 1) * 64], start=True, stop=True)
            return rr_
        self.OP("pe", mmU, ["Z", "RHSb"], [("bank", 1)])
        self.OP("act", lambda e: e.copy(out=r["Ub"][:], in_=R2[:, 0:256]), [("bank", 1)], ["Ub"])

        def mmY(e):
            rr_ = None
            for j in range(2):
                e.matmul(R1[:, 256 + j * 128:256 + (j + 1) * 128], lhsT=r["Rtb"][:, j, :], rhs=r["Sb"][:, j, :], start=True, stop=False)
                for h2 in range(2):
                    h = 2 * j + h2
                    cs_ = slice(256 + h * 64, 256 + (h + 1) * 64)
                    e.matmul(R1[:, cs_], lhsT=r["NrbT"][:, h, :], rhs=r["Ub"][:, h * 64:(h + 1) * 64], start=False, stop=False)
                    rr_ = e.matmul(R1[:, cs_], lhsT=r["NrkT"][:, h, :], rhs=r["Vtb"][:, h * 64:(h + 1) * 64], start=False, stop=(h2 == 1))
            return rr_
        self.OP("pe", mmY, [("Rtb", 0), ("Rtb", 1), "Sb", "NrbT", "NrkT", "Ub", "Vtb"], [("bank", 0)])
        self.OP("act", lambda e: e.copy(out=r["Ysb"][:], in_=R1[:, 256:512]), [("bank", 0)], ["Ysb"])

        def mmS(e):
            rr_ = None
            for j in range(2):
                jc = slice(j * 128, (j + 1) * 128)
                e.matmul(R2[:, 256 + j * 128:256 + (j + 1) * 128], lhsT=r["Bcb"][:, jc], rhs=r["Ub"][:, jc], start=True, stop=False)
                rr_ = e.matmul(R2[:, 256 + j * 128:256 + (j + 1) * 128], lhsT=r["Kcb"][:, jc], rhs=r["Vtb"][:, jc], start=False, stop=True)
            return rr_
        self.OP("pe", mmS, ["Bcb", "Kcb", "Ub", "Vtb"], [("bank", 1)])
        for j in range(2):
            self.OP("dve", lambda e, j=j: e.tensor_tensor(out=r["T1"][:, j, :], in0=R2[:, 256 + j * 128:256 + (j + 1) * 128], in1=self.cT["blk64"][:], op=ALU.mult),
                    [("bank", 1), "k_blk64"], [("T1", j)])
            self.OP("dve", lambda e, j=j: e.scalar_tensor_tensor(out=r["Sf"][:, j, :], in0=r["Sf"][:, j, :], scalar=r["Ep"][:, j, 127:128], in1=r["T1"][:, j, :],
                                                                 op0=ALU.mult, op1=ALU.add), ["Sf", ("Ep", j), ("T1", j)], ["Sf"])
        self.OP("act", lambda e: e.copy(out=r["Sb"][:], in_=r["Sf"][:]), ["Sf"], ["Sb"])
        self.rwkv_post(128, tok)

    def rwkv_state_out(self, l):
        r = self.r
        TB = self.bank[5]

        def tr(e):
            rr_ = None
            for j in range(2):
                rr_ = e.transpose(out=TB[:, j * 128:(j + 1) * 128], in_=r["Sf"][:, j, :], identity=self.ident[:])
            return rr_
        self.OP("pe", tr, ["Sf", "ident"], [("bank", 5)])
        self.OP("act", lambda e: e.copy(out=r["ytmp"][:], in_=TB[:, 0:256]), [("bank", 5)], ["ytmp_r"])
        for j in range(2):
            for h2 in range(2):
                self.P.dma("sp", self.o_p["wkv"][l, 2 * j + h2], r["ytmp"][h2 * 64:(h2 + 1) * 64, j * 128 + h2 * 64:j * 128 + (h2 + 1) * 64],
                           reads=["ytmp_r"], writes=[("o_p_wkv", l, j, h2)])

    def rwkv_sample(self, l):
        P = self.P
        r = self.r
        ti = 4
        t0, n = TILES[ti]
        raw, zs = r["raw_s"], r["zs_s"]
        P.dma("sp", self.smp_tm[0:16, 0:768], self.sti["shift"][l][:, 0:768], writes=["smp_tm"])
        P.dma("sp", self.stage[0:16, 0:128], self.sti["shift"][l][:, 768:896], writes=["stage"])
        for q in range(7):
            TB = self.bank[5]
            src = self.smp_tm[0:16, q * 128:(q + 1) * 128] if q < 6 else self.stage[0:16, 0:128]
            self.OP("pe", lambda e, src=src: e.transpose(out=TB[:, 0:16], in_=src, identity=self.ident[0:16, 0:16]), ["smp_tm", "stage", "ident"], [("bank", 5)])
            self.OP("act", lambda e, q=q: e.copy(out=raw[:, q, 0:16], in_=TB[:, 0:16]), [("bank", 5)], [("rraw", q)])
        self.rwkv_front(ti, raw, zs, n, 16)
        for q in range(7):
            TB = self.bank[5]
            self.OP("pe", lambda e, q=q: e.transpose(out=TB[0:16, 0:128], in_=raw[:, q, 64:80], identity=self.ident[:]), [("rraw", q), "ident"], [("bank", 5)])
            self.OP("act", lambda e: e.copy(out=self.stage[0:16, 128:256], in_=TB[0:16, 0:128]), [("bank", 5)], ["stage"])
            P.dma("sp", self.o_s["shift"][l][:, q * 128:(q + 1) * 128], self.stage[0:16, 128:256], reads=["stage"], writes=[("o_s_shift", l, q)])
        for j in range(2):
            self.rwkv_prelim(j, zs, 0, 64, False)
        self.rwkv_tokmajor(zs, 0, 64, False)
        sc = self.sc
        for nme, src, scn, neg in (("vr", None, "rr", False), ("vw", "Ep", "rw", False), ("vk", "kp", "rk", False), ("va", "kk", "ra", True), ("vb", "bq", "rb", False)):
            TB = self.bank[5]

            def trq(e, src=src):
                rr_ = None
                for j in range(2):
                    in_ = zs[:, j, 0:64] if src is None else r[src][:, j, 0:64]
                    rr_ = e.transpose(out=TB[0:64, j * 128:(j + 1) * 128], in_=in_, identity=self.ident[:])
                return rr_
            rkeys = ["zs"] if src is None else [(src, 0), (src, 1)]
            self.OP("pe", trq, rkeys + ["ident"], [("bank", 5)])
            self.OP("act", lambda e, neg=neg: e.activation(out=r["ysq"][0:64, :], in_=TB[0:64, 0:256], func=AF.Copy, scale=(-1.0 if neg else 1.0)),
                    [("bank", 5)], ["ysq"])
            for t in range(4):
                rows = slice(t * 16, (t + 1) * 16)
                for vh in range(2):
                    P.dma("sp", sc[scn].rearrange("(vh b h) (t w) -> vh b h t w", vh=2, h=4, t=4)[vh, :, :, t, :],
                          r["ysq"][rows, :].rearrange("b (h w) -> b h w", h=4), reads=["ysq"], writes=["sc_" + scn])
            P.dma("sp", r[nme][:].rearrange("p t w -> p (t w)"), sc[scn], reads=["sc_" + scn], writes=[nme])
        for t in range(4):
            rows = slice(t * 16, (t + 1) * 16)
            for vh in range(2):
                P.dma("sp", sc["rv"].rearrange("(vh b h) (t w) -> vh b h t w", vh=2, h=4, t=4)[vh, :, :, t, :],
                      r["vtf"][rows, :].rearrange("b (h w) -> b h w", h=4)[:, :, vh * 32:(vh + 1) * 32], reads=["vtf"], writes=["sc_rv"])
        P.dma("sp", r["vv"][:].rearrange("p t w -> p (t w)"), sc["rv"], reads=["sc_rv"], writes=["vv"])
        Ss, tS = r["Ss"], r["tS"]
        for vh in range(2):
            P.dma("sp", Ss[vh * 64:(vh + 1) * 64].rearrange("p v k -> p (v k)"),
                  self.sti["wkv"][l].rearrange("b h v k -> (b h) (v k)")[:, vh * 2048:(vh + 1) * 2048], writes=["Ss_r"])
        bk = lambda ap, t: ap[:, t, :].unsqueeze(1).to_broadcast([128, 32, 64])
        for t in range(4):
            self.OP("dve", lambda e, t=t: e.tensor_tensor(out=tS[:], in0=Ss[:], in1=bk(r["va"], t), op=ALU.mult), ["Ss_r", "va"], ["tS_r"])
            self.OP("dve", lambda e, t=t: e.tensor_reduce(out=r["sa"][:], in_=tS[:], axis=AX.X, op=ALU.add), ["tS_r"], ["sa"])
            self.OP("dve", lambda e, t=t: e.tensor_tensor(out=Ss[:], in0=Ss[:], in1=bk(r["vw"], t), op=ALU.mult), ["Ss_r", "vw"], ["Ss_r"])
            self.OP("dve", lambda e, t=t: e.tensor_tensor(out=tS[:], in0=r["sa"][:].unsqueeze(2).to_broadcast([128, 32, 64]), in1=bk(r["vb"], t), op=ALU.mult),
                    ["sa", "vb", "tS_r"], ["tS_r"])
            self.OP("dve", lambda e, t=t: e.tensor_tensor(out=Ss[:], in0=Ss[:], in1=tS[:], op=ALU.add), ["Ss_r", "tS_r"], ["Ss_r"])
            self.OP("dve", lambda e, t=t: e.tensor_tensor(out=tS[:], in0=r["vv"][:, t, :].unsqueeze(2).to_broadcast([128, 32, 64]), in1=bk(r["vk"], t), op=ALU.mult),
                    ["vv", "vk", "tS_r"], ["tS_r"])
            self.OP("dve", lambda e, t=t: e.tensor_tensor(out=Ss[:], in0=Ss[:], in1=tS[:], op=ALU.add), ["Ss_r", "tS_r"], ["Ss_r"])
            self.OP("dve", lambda e, t=t: e.tensor_tensor(out=tS[:], in0=Ss[:], in1=bk(r["vr"], t), op=ALU.mult), ["Ss_r", "vr", "tS_r"], ["tS_r"])
            self.OP("dve", lambda e, t=t: e.tensor_reduce(out=r["vy"][:, t, :], in_=tS[:], axis=AX.X, op=ALU.add), ["tS_r"], ["vy"])
        for vh in range(2):
            P.dma("sp", self.o_s["wkv"][l].rearrange("b h v k -> (b h) (v k)")[:, vh * 2048:(vh + 1) * 2048],
                  Ss[vh * 64:(vh + 1) * 64].rearrange("p v k -> p (v k)"), reads=["Ss_r"], writes=[("o_s_wkv", l, vh)])
        P.dma("sp", sc["ry"], r["vy"][:].rearrange("p t w -> p (t w)"), reads=["vy"], writes=["sc_ry"])
        for t in range(4):
            rows = slice(t * 16, (t + 1) * 16)
            for vh in range(2):
                P.dma("sp", r["Ysb"][rows, :].rearrange("b (h w) -> b h w", h=4)[:, :, vh * 32:(vh + 1) * 32],
                      sc["ry"].rearrange("(vh b h) (t w) -> vh b h t w", vh=2, h=4, t=4)[vh, :, :, t, :], reads=["sc_ry"], writes=["Ysb"])
        self.rwkv_post(64, slice(0, 64))

    def build(self):
        P = self.P
        self.wcount = 0
        self.bcount = 0
        import os
        self.kch = 0
        self.mix = ["ssd", "gla", "rwkv"]
        self.setup()
        self.gsched = True
        P.sched = self.gsched
        self.convert_ffn(0, 1)
        self.load_x()
        self.convert_mix(0)
        self.convert_ffn(0, 2)
        self.convert_ple(0)
        for l in range(2):
            self.ffn_phase(l, 1)
            if l == 0:
                self.convert_ffn(1, 1)
            if self.mix:
                self.mixer_phase(l)
            if l == 0:
                self.convert_mix(1)
            self.ffn_phase(l, 2)
            if l == 0:
                self.convert_ffn(1, 2)
            self.ple_phase(l)
            if l == 0:
                self.convert_ple(1)
        P.barrier()
        self.final()
        P.barrier(final=True)
        P.run(self.st)


_CACHE = {}


def build_program(shapes):
    nc = bass.Bass("TRN2", target_bir_lowering=False)
    with ExitStack() as st:
        k = K(nc, st, shapes)
        k.build()
    return nc


def kernel(**inputs):
    inp = {k: np.ascontiguousarray(np.asarray(v, dtype=np.float32)) for k, v in inputs.items()}
    shapes = {n: inp[n].shape for n in WNAMES}
    key = tuple(sorted((n, tuple(s)) for n, s in shapes.items()))
    if key not in _CACHE:
        _CACHE[key] = build_program(shapes)
    nc = _CACHE[key]
    consts = host_consts()
    in_maps = []
    for c in range(NCORES):
        m = {}
        m["xin"] = np.ascontiguousarray(np.concatenate(
            [inp["x_prompt"][c], inp["x_sample"][16 * c:16 * c + 16].transpose(1, 0, 2).reshape(TS, D)], axis=0))
        m["pin"] = np.ascontiguousarray(np.concatenate(
            [inp["p_prompt"][:, c], inp["p_sample"][:, 16 * c:16 * c + 16].transpose(0, 2, 1, 3).reshape(2, TS, PLE)], axis=1))
        m["st_shift"] = np.ascontiguousarray(inp["state_rwkv_shift"][:, 16 * c:16 * c + 16])
        m["st_wkv"] = np.ascontiguousarray(inp["state_rwkv_wkv"][:, 16 * c:16 * c + 16])
        m["st_gla"] = np.ascontiguousarray(inp["state_gla"][:, 16 * c:16 * c + 16])
        m["st_conv"] = np.ascontiguousarray(inp["state_mamba_conv"][:, 16 * c:16 * c + 16])
        m["st_ssm"] = np.ascontiguousarray(inp["state_mamba_ssm"][:, 16 * c:16 * c + 16])
        for n in WNAMES:
            m[n] = inp[n]
        for n, a in consts.items():
            m["c_" + n] = a
        in_maps.append(m)
    res = run_bass_kernel_spmd(nc, in_maps, core_ids=list(range(NCORES)))
    R = list(res.results)
    y = np.stack([r["yout"] for r in R], axis=0)
    y_prompt = np.ascontiguousarray(y[:, :TP, :])
    y_sample = np.ascontiguousarray(y[:, TP:, :].reshape(NCORES, 4, 16, D).transpose(0, 2, 1, 3).reshape(NCORES * 16, 4, D))
    outs = [y_prompt, y_sample]
    for nm in ("shift", "wkv", "gla", "conv", "ssm"):
        outs.append(np.ascontiguousarray(np.stack([r["o_p_" + nm] for r in R], axis=1)))
    for nm in ("shift", "wkv", "gla", "conv", "ssm"):
        outs.append(np.ascontiguousarray(np.concatenate([r["o_s_" + nm] for r in R], axis=1)))
    return tuple(outs)
```
